# Optimizing a Trainium2 kernel written in Bass

```python
import jax, jax.numpy as jnp
from jax import lax
import numpy as np

D_MODEL = 2048
BATCH = 16
SEQ = 256
DEPTH = 2
DEC_BATCH = 2
DEC_SEQ = 4096
PAST_LEN = 512

GRID_W = 64
CHUNK = 64
HEAD_DIM = 128
H_A = 8
H_B = 8
H_C = 8
W_MIX = 8 * HEAD_DIM
CONV_K = 3
D_FF = 4096
N_MOD = 9
ROPE_BASE = 10000.0
EPS = 1e-6

IN_SIZES = [W_MIX, W_MIX, W_MIX, W_MIX, 2 * H_A, 2 * H_A,
            3 * W_MIX, W_MIX, 2 * H_B, 2 * H_B,
            W_MIX, W_MIX, W_MIX, W_MIX,
            3 * D_MODEL]
N_IN = int(sum(IN_SIZES))
IN_OFFSETS = [int(o) for o in np.cumsum(IN_SIZES)[:-1]]

kernel_name = "hybrid_mlstm_deltanet_retention_flow_step"


def rmsnorm(x, g):
    xf = x.astype(jnp.float32)
    y = xf * lax.rsqrt(jnp.mean(xf * xf, axis=-1, keepdims=True) + EPS)
    return (y * g.astype(jnp.float32)).astype(x.dtype)


def head_rms(x, g):
    y = x * lax.rsqrt(jnp.mean(x * x, axis=-1, keepdims=True) + EPS)
    return y * g.astype(jnp.float32)


def l2norm(x):
    return x * lax.rsqrt(jnp.sum(x * x, axis=-1, keepdims=True) + EPS)


def heads(x, h):
    return x.reshape(*x.shape[:-1], h, -1)


def flip(x):
    return jnp.flip(x, axis=1)


def chunk(x):
    b, t, h, d = x.shape
    return x.reshape(b, t // CHUNK, CHUNK, h, d).transpose(1, 0, 3, 2, 4)


def unchunk(y):
    nc, b, h, l, d = y.shape
    return y.transpose(1, 0, 3, 2, 4).reshape(b, nc * l, h, d)


def masks():
    idx = jnp.arange(CHUNK)
    return idx[:, None] >= idx[None, :], idx[:, None] > idx[None, :]


def mlstm_scan(q, k, v, ig, lf, C0, n0, m0):
    qc, kc, vc = chunk(q), chunk(k), chunk(v)
    igc = chunk(ig[..., None])[..., 0]
    b = jnp.cumsum(chunk(lf[..., None])[..., 0], axis=-1)
    incl, _ = masks()
    logD = jnp.where(incl, b[..., :, None] - b[..., None, :] + igc[..., None, :], -jnp.inf)
    m_in = jnp.max(logD, axis=-1)
    bL = b[..., -1]
    log_end = bL[..., None] - b + igc
    m_end = jnp.max(log_end, axis=-1)
    qk = jnp.einsum('nbhld,nbhmd->nbhlm', qc, kc)

    def step(carry, inp):
        C, n, m = carry
        qn, kn, vn, bn, logDn, m_inn, qkn, le_n, me_n, bL_n = inp
        m_i = jnp.maximum(bn + m[..., None], m_inn)
        w_prev = jnp.exp(bn + m[..., None] - m_i)
        S = qkn * jnp.exp(logDn - m_i[..., None])
        num = w_prev[..., None] * jnp.einsum('bhld,bhde->bhle', qn, C) + jnp.einsum('bhlm,bhme->bhle', S, vn)
        den = w_prev * jnp.einsum('bhld,bhd->bhl', qn, n) + jnp.sum(S, axis=-1)
        h = num / jnp.maximum(jnp.abs(den), jnp.exp(-m_i))[..., None]
        m_new = jnp.maximum(bL_n + m, me_n)
        w_s = jnp.exp(bL_n + m - m_new)
        w_t = jnp.exp(le_n - m_new[..., None])
        C = w_s[..., None, None] * C + jnp.einsum('bhld,bhle->bhde', kn * w_t[..., None], vn)
        n = w_s[..., None] * n + jnp.einsum('bhld,bhl->bhd', kn, w_t)
        return (C, n, m_new), h

    (C, n, m), h = lax.scan(step, (C0, n0, m0), (qc, kc, vc, b, logD, m_in, qk, log_end, m_end, bL))
    return unchunk(h), C, n, m


def delta_scan(q, k, v, g, beta, S0):
    qc, kc, vc = chunk(q), chunk(k), chunk(v)
    G = jnp.cumsum(chunk(g[..., None])[..., 0], axis=-1)
    bc = chunk(beta[..., None])[..., 0]
    incl, strict = masks()
    diff = G[..., :, None] - G[..., None, :]
    decay = jnp.where(incl, jnp.exp(jnp.where(incl, diff, 0.0)), 0.0)
    A = jnp.where(strict, jnp.einsum('nbhld,nbhmd->nbhlm', kc, kc) * decay, 0.0) * bc[..., :, None]
    IA = A + jnp.eye(CHUNK, dtype=A.dtype)
    rhs = jnp.concatenate([bc[..., None] * vc, (bc * jnp.exp(G))[..., None] * kc], axis=-1)
    sol = lax.linalg.triangular_solve(IA, rhs, left_side=True, lower=True, unit_diagonal=True)
    U0, Wk = sol[..., :HEAD_DIM], sol[..., HEAD_DIM:]
    qk = jnp.einsum('nbhld,nbhmd->nbhlm', qc, kc) * decay
    q_dec = qc * jnp.exp(G)[..., None]
    k_end = kc * jnp.exp(G[..., -1:] - G)[..., None]
    c_dec = jnp.exp(G[..., -1])

    def step(S, inp):
        qk_n, qd_n, ke_n, U0_n, W_n, cd_n = inp
        U = U0_n - jnp.einsum('bhld,bhde->bhle', W_n, S)
        o = jnp.einsum('bhld,bhde->bhle', qd_n, S) + jnp.einsum('bhlm,bhme->bhle', qk_n, U)
        S = cd_n[..., None, None] * S + jnp.einsum('bhld,bhle->bhde', ke_n, U)
        return S, o

    S_fin, o = lax.scan(step, S0, (qk, q_dec, k_end, U0, Wk, c_dec))
    return unchunk(o), S_fin


def retention_scan(q, k, v, log_gamma, S0):
    qc, kc, vc = chunk(q), chunk(k), chunk(v)
    idx = jnp.arange(CHUNK, dtype=jnp.float32)
    incl, _ = masks()
    lg = log_gamma.astype(jnp.float32)[:, None]
    decay = jnp.where(incl, jnp.exp(lg[..., None] * jnp.maximum(idx[:, None] - idx[None, :], 0.0)), 0.0)
    inner = jnp.einsum('nbhlm,nbhme->nbhle', jnp.einsum('nbhld,nbhmd->nbhlm', qc, kc) * decay, vc)
    q_dec = qc * jnp.exp(lg * (idx + 1.0))[..., None]
    k_end = kc * jnp.exp(lg * (CHUNK - 1.0 - idx))[..., None]
    c_dec = jnp.exp(lg * CHUNK)[..., None]

    def step(S, inp):
        qd, ke, vn = inp
        o = jnp.einsum('bhld,bhde->bhle', qd, S)
        S = c_dec * S + jnp.einsum('bhld,bhle->bhde', ke, vn)
        return S, o

    S_fin, cross = lax.scan(step, S0, (q_dec, k_end, vc))
    return unchunk(inner + cross), S_fin


def short_conv(x, w):
    ch = x.shape[-1]
    return lax.conv_general_dilated(x, w.astype(x.dtype)[:, None, :], window_strides=(1,),
                                    padding=[(CONV_K // 2, CONV_K // 2)],
                                    dimension_numbers=('NWC', 'WIO', 'NWC'), feature_group_count=ch)


def rope_tables(T):
    n_rows = T // GRID_W
    pos_r = jnp.repeat(jnp.arange(n_rows), GRID_W).astype(jnp.float32)
    pos_c = jnp.tile(jnp.arange(GRID_W), n_rows).astype(jnp.float32)
    nf = HEAD_DIM // 4
    freqs = ROPE_BASE ** (-jnp.arange(nf, dtype=jnp.float32) / nf)
    ang = jnp.concatenate([pos_r[:, None] * freqs, pos_c[:, None] * freqs], axis=-1)
    ang = jnp.concatenate([ang, ang], axis=-1)[:, None, :]
    return jnp.cos(ang), jnp.sin(ang)


def apply_rope(x, cs):
    cos, sin = cs
    x1, x2 = x[..., :HEAD_DIM // 2], x[..., HEAD_DIM // 2:]
    return x * cos + jnp.concatenate([-x2, x1], axis=-1) * sin


def swiglu(h, w13, w2):
    a, b = jnp.split(h @ w13, 2, axis=-1)
    return (jax.nn.silu(a) * b) @ w2


def token_mixing(h, states, rope, w_in, b_in, f_bias, conv_w, A_log, dt_bias, log_gamma, hn_g, w_br, w_out):
    f32 = jnp.float32
    Bsz, T, _ = h.shape
    P = (h @ w_in + b_in).astype(f32)
    qA, kA, vA, oA, iA, fA, qkvB, zB, betaB, aB, qC, kC, vC, gC, gm = jnp.split(P, IN_OFFSETS, axis=-1)
    C0, n0, m0, SD0, SR0 = [s.astype(f32) for s in states]
    qA = heads(qA, H_A)
    kA = heads(kA, H_A) * HEAD_DIM ** -0.5
    vA = heads(vA, H_A)
    iA = iA.reshape(Bsz, T, 2, H_A)
    lfA = jax.nn.log_sigmoid(fA.reshape(Bsz, T, 2, H_A) + f_bias.astype(f32))
    hf, Cf, nf, mf = mlstm_scan(qA, kA, vA, iA[:, :, 0], lfA[:, :, 0], C0[:, 0], n0[:, 0], m0[:, 0])
    hb, Cb, nb, mb = mlstm_scan(flip(qA), flip(kA), flip(vA), flip(iA[:, :, 1]), flip(lfA[:, :, 1]),
                                C0[:, 1], n0[:, 1], m0[:, 1])
    yA = head_rms(hf + flip(hb), hn_g[0]) * jax.nn.sigmoid(heads(oA, H_A))
    qkvB = jax.nn.silu(short_conv(qkvB, conv_w))
    qB, kB, vB = jnp.split(qkvB, 3, axis=-1)
    qB = l2norm(heads(qB, H_B)) * HEAD_DIM ** -0.5
    kB = l2norm(heads(kB, H_B))
    vB = heads(vB, H_B)
    beta = jax.nn.sigmoid(betaB.reshape(Bsz, T, 2, H_B))
    gB = -jnp.exp(A_log.astype(f32)) * jax.nn.softplus(aB.reshape(Bsz, T, 2, H_B) + dt_bias.astype(f32))
    of, SDf = delta_scan(qB, kB, vB, gB[:, :, 0], beta[:, :, 0], SD0[:, 0])
    ob, SDb = delta_scan(flip(qB), flip(kB), flip(vB), flip(gB[:, :, 1]), flip(beta[:, :, 1]), SD0[:, 1])
    yB = head_rms(of + flip(ob), hn_g[1]) * jax.nn.silu(heads(zB, H_B))
    qC = heads(qC, H_C) * HEAD_DIM ** -0.5
    kC = heads(kC, H_C)
    vC = heads(vC, H_C)
    if rope is not None:
        qC = apply_rope(qC, rope)
        kC = apply_rope(kC, rope)
    rf, SRf = retention_scan(qC, kC, vC, log_gamma[0], SR0[:, 0])
    rb, SRb = retention_scan(flip(qC), flip(kC), flip(vC), log_gamma[1], SR0[:, 1])
    yC = head_rms(rf + flip(rb), hn_g[2]) * jax.nn.silu(heads(gC, H_C))
    ys = jnp.stack([yA.reshape(Bsz, T, W_MIX), yB.reshape(Bsz, T, W_MIX), yC.reshape(Bsz, T, W_MIX)], axis=2).astype(h.dtype)
    br = jnp.einsum('btiw,iwd->btid', ys, w_br)
    gates = jax.nn.sigmoid(gm.reshape(Bsz, T, 3, -1)).astype(h.dtype)
    out = jnp.sum(gates * br, axis=2) @ w_out
    new_states = (jnp.stack([Cf, Cb], axis=1), jnp.stack([nf, nb], axis=1), jnp.stack([mf, mb], axis=1),
                  jnp.stack([SDf, SDb], axis=1), jnp.stack([SRf, SRb], axis=1))
    return out, new_states


def adaln(cv, w, b):
    return (jax.nn.silu(cv) @ w + b).reshape(cv.shape[0], N_MOD, -1)


def trunk_layer(x, ada, states, rope, norm_g, w_in, b_in, f_bias, conv_w, A_log, dt_bias, log_gamma,
                hn_g, w_br, w_out, w13, w2):
    mod = [ada[:, i][:, None, :] for i in range(N_MOD)]
    h = rmsnorm(x, norm_g[0]) * (1.0 + mod[1]) + mod[0]
    x = x + 0.5 * mod[2] * swiglu(h, w13[0], w2[0])
    h = rmsnorm(x, norm_g[1]) * (1.0 + mod[4]) + mod[3]
    out, new_states = token_mixing(h, states, rope, w_in, b_in, f_bias, conv_w, A_log, dt_bias, log_gamma,
                                   hn_g, w_br, w_out)
    x = x + mod[5] * out
    h = rmsnorm(x, norm_g[2]) * (1.0 + mod[7]) + mod[6]
    x = x + 0.5 * mod[8] * swiglu(h, w13[1], w2[1])
    return x, new_states


def setup_inputs(seed: int = 0) -> dict:
    key = jax.random.key(seed)
    ks = jax.random.split(key, 32)
    nrm = jax.random.normal
    D = D_MODEL
    sd = D ** -0.5
    f_bias = jnp.linspace(3.0, 6.0, H_A)[None, None, :] + 0.1 * nrm(ks[13], (DEPTH, 2, H_A))
    dt = jnp.exp(jax.random.uniform(ks[16], (DEPTH, 2, H_B), minval=np.log(0.001), maxval=np.log(0.1)))
    base_lg = jnp.log(1.0 - 2.0 ** (-5.0 - jnp.arange(H_C, dtype=jnp.float32)))
    return {
        'x_prompt': nrm(ks[0], (BATCH, SEQ, D)),
        'x_sample': nrm(ks[1], (DEC_BATCH, DEC_SEQ, D)),
        'state_mlstm_C': 0.1 * nrm(ks[2], (DEC_BATCH, DEPTH, 2, H_A, HEAD_DIM, HEAD_DIM)),
        'state_mlstm_n': 0.1 * nrm(ks[3], (DEC_BATCH, DEPTH, 2, H_A, HEAD_DIM)),
        'state_mlstm_m': nrm(ks[4], (DEC_BATCH, DEPTH, 2, H_A)),
        'state_delta_S': 0.1 * nrm(ks[5], (DEC_BATCH, DEPTH, 2, H_B, HEAD_DIM, HEAD_DIM)),
        'state_ret_S': 0.5 * nrm(ks[6], (DEC_BATCH, DEPTH, 2, H_C, HEAD_DIM, HEAD_DIM)),
        'c': nrm(ks[7], (DEC_BATCH, D)),
        'c_ctx': nrm(ks[8], (D,)),
        'norm_g': 1.0 + 0.02 * nrm(ks[9], (DEPTH, 3, D)),
        'final_norm_g': 1.0 + 0.02 * nrm(ks[10], (D,)),
        'w_ada': 0.5 * sd * nrm(ks[11], (DEPTH, D, N_MOD * D)),
        'b_ada': 0.02 * nrm(ks[12], (DEPTH, N_MOD * D)),
        'w_in': sd * nrm(ks[14], (DEPTH, D, N_IN)),
        'b_in': 0.02 * nrm(ks[15], (DEPTH, N_IN)),
        'mlstm_f_bias': f_bias,
        'conv_w': CONV_K ** -0.5 * nrm(ks[17], (DEPTH, CONV_K, 3 * W_MIX)),
        'delta_A_log': jnp.log(jax.random.uniform(ks[18], (DEPTH, 2, H_B), minval=1.0, maxval=16.0)),
        'delta_dt_bias': dt + jnp.log(-jnp.expm1(-dt)),
        'ret_log_gamma': base_lg[None, None, :] * (1.0 + 0.05 * nrm(ks[19], (DEPTH, 2, H_C))),
        'head_norm_g': 1.0 + 0.02 * nrm(ks[20], (DEPTH, 3, HEAD_DIM)),
        'w_br': W_MIX ** -0.5 * nrm(ks[21], (DEPTH, 3, W_MIX, D)),
        'w_out': sd * nrm(ks[22], (DEPTH, D, D)),
        'ffn_w13': sd * nrm(ks[23], (DEPTH, 2, D, 2 * D_FF)),
        'ffn_w2': D_FF ** -0.5 * nrm(ks[24], (DEPTH, 2, D_FF, D)),
    }


def reference(x_prompt, x_sample, state_mlstm_C, state_mlstm_n, state_mlstm_m, state_delta_S, state_ret_S,
              c, c_ctx, norm_g, final_norm_g, w_ada, b_ada, w_in, b_in, mlstm_f_bias, conv_w, delta_A_log,
              delta_dt_bias, ret_log_gamma, head_norm_g, w_br, w_out, ffn_w13, ffn_w2):
    f32 = jnp.float32
    Bp = x_prompt.shape[0]
    zero_states = (jnp.zeros((Bp, 2, H_A, HEAD_DIM, HEAD_DIM), f32), jnp.zeros((Bp, 2, H_A, HEAD_DIM), f32),
                   jnp.zeros((Bp, 2, H_A), f32), jnp.zeros((Bp, 2, H_B, HEAD_DIM, HEAD_DIM), f32),
                   jnp.zeros((Bp, 2, H_C, HEAD_DIM, HEAD_DIM), f32))
    rope = rope_tables(x_sample.shape[1])
    xp, xs = x_prompt, x_sample
    ctx_states = []
    for l in range(DEPTH):
        lw = (norm_g[l], w_in[l], b_in[l], mlstm_f_bias[l], conv_w[l], delta_A_log[l], delta_dt_bias[l],
              ret_log_gamma[l], head_norm_g[l], w_br[l], w_out[l], ffn_w13[l], ffn_w2[l])
        xp, st = trunk_layer(xp, adaln(c_ctx[None, :], w_ada[l], b_ada[l]), zero_states, None, *lw)
        ctx_states.append(st)
        cache_l = (state_mlstm_C[:, l], state_mlstm_n[:, l], state_mlstm_m[:, l], state_delta_S[:, l], state_ret_S[:, l])
        xs, _ = trunk_layer(xs, adaln(c, w_ada[l], b_ada[l]), cache_l, rope, *lw)
    y_prompt = rmsnorm(xp, final_norm_g)
    y_sample = rmsnorm(xs, final_norm_g)
    dt_out = x_prompt.dtype
    new_mlstm_C = jnp.stack([s[0] for s in ctx_states], axis=1).astype(dt_out)
    new_mlstm_n = jnp.stack([s[1] for s in ctx_states], axis=1).astype(dt_out)
    new_mlstm_m = jnp.stack([s[2] for s in ctx_states], axis=1).astype(dt_out)
    new_delta_S = jnp.stack([s[3] for s in ctx_states], axis=1).astype(dt_out)
    new_ret_S = jnp.stack([s[4] for s in ctx_states], axis=1).astype(dt_out)
    return (y_prompt, y_sample, new_mlstm_C, new_mlstm_n, new_mlstm_m, new_delta_S, new_ret_S)
```

```python
import numpy as np
from contextlib import ExitStack
import concourse.bass as bass
import concourse.mybir as mybir
from concourse.bass_utils import run_bass_kernel_spmd

F32 = mybir.dt.float32
BF16 = mybir.dt.bfloat16
AF = mybir.ActivationFunctionType
ALU = mybir.AluOpType
AX = mybir.AxisListType

D = 2048
KC = 16
DEPTH = 2
HD = 128
NH = 8
WM = 1024
DFF = 4096
NMOD = 9
EPS = 1e-6
L = 64
N_IN = 18496
OFF = dict(qA=0, kA=1024, vA=2048, oA=3072, iA=4096, fA=4112, qkvB=4128, zB=7200, betaB=8224, aB=8240,
           qC=8256, kC=9280, vC=10304, gC=11328, gm=12352)
SEGS = [(0, 256), (256, 256), (512, 4096)]
TC = 4608
TT = 512
NTT = TC // TT
TG = 1536
NTG = TC // TG


def _build_cst():
    r = np.arange(64)
    LE = (r[:, None] <= r[None, :]).astype(np.float32)
    M = {"LE": LE, "GE": LE.T.copy(), "LT": (r[:, None] < r[None, :]).astype(np.float32),
         "GT": (r[:, None] > r[None, :]).astype(np.float32)}
    for k in range(6):
        M["BM%d" % k] = ((r[:, None] >> (k + 1) == r[None, :] >> (k + 1)) & (r[:, None] >> k != r[None, :] >> k)).astype(np.float32)
    cols = {}
    arr = []
    off = 0
    for n, m in M.items():
        a = np.zeros((128, 64), np.float32); a[:64] = m
        arr.append(a); cols[n] = (off, 64); off += 64
    for n, v in (("p1f", r + 1.0), ("p1b", 64.0 - r), ("p2f", 63.0 - r), ("p2b", r * 1.0)):
        a = np.zeros((128, 1), np.float32); a[:64, 0] = v
        arr.append(a); cols[n] = (off, 1); off += 1
    Rm = np.zeros((128, 128), np.float32)
    for dp in range(64):
        Rm[dp + 64, dp] = -1.0
        Rm[dp, dp + 64] = 1.0
    arr.append(Rm); cols["Rm"] = (off, 128); off += 128
    return np.concatenate(arr, axis=1), cols


_CST_NP, CST = _build_cst()
NCST = _CST_NP.shape[1]


def _rope_np():
    T = 4096
    pos_r = np.repeat(np.arange(T // 64), 64).astype(np.float32)
    pos_c = np.tile(np.arange(64), T // 64).astype(np.float32)
    nf = HD // 4
    freqs = (10000.0 ** (-np.arange(nf, dtype=np.float32) / nf)).astype(np.float32)
    ang = np.concatenate([pos_r[:, None] * freqs, pos_c[:, None] * freqs], axis=-1)
    ang = np.concatenate([ang, ang], axis=-1)
    return np.ascontiguousarray(np.cos(ang).T.astype(np.float32)), np.ascontiguousarray(np.sin(ang).T.astype(np.float32))


class Res:
    __slots__ = ("name", "w", "rs", "parent", "kids")

    def __init__(self, name):
        self.name = name
        self.w = None
        self.rs = []
        self.parent = None
        self.kids = []


class Sched:
    ENGS = ("pe", "act", "dve", "pool", "sp")

    def __init__(self, nc, es):
        self.nc = nc
        self.es = es
        self.ops = []
        self.per_eng = {e: [] for e in self.ENGS}

    def _deps(self, reads, writes, idx):
        deps = set()
        for r in reads:
            if r.w is not None:
                deps.add(r.w)
            if r.parent is not None and r.parent.w is not None:
                deps.add(r.parent.w)
            for k in r.kids:
                if k.w is not None:
                    deps.add(k.w)
        for w in writes:
            rel = [w] + w.kids + ([w.parent] if w.parent is not None else [])
            for x in rel:
                if x.w is not None:
                    deps.add(x.w)
                deps.update(x.rs)
        for r in reads:
            r.rs.append(idx)
        for w in writes:
            w.w = idx
            w.rs = []
        deps.discard(idx)
        return deps

    def op(self, eng, fn, reads=(), writes=()):
        idx = len(self.ops)
        deps = self._deps(reads, writes, idx)
        self.ops.append((eng, fn, deps, False))
        return idx

    def dma(self, eng, fn, reads=(), writes=()):
        idx = len(self.ops)
        deps = self._deps(reads, writes, idx)
        self.ops.append((eng, fn, deps, True))
        return idx

    def emit(self, final_waits=True):
        nc, es = self.nc, self.es
        NDS = {"sp": 24, "pool": 12, "act": 4}
        sem = {e: es.enter_context(nc.semaphore("s_" + e)) for e in ("pe", "act", "dve", "pool")}
        dsem = {q: [es.enter_context(nc.semaphore("d_%s%d" % (q, i))) for i in range(n)] for q, n in NDS.items()}
        cnt = {e: 0 for e in sem}
        dcnt = {q: [0] * n for q, n in NDS.items()}
        dnext = {q: 0 for q in NDS}
        handle = {}
        known = {e: {} for e in self.ENGS}
        prog = {e: [] for e in self.ENGS}
        last_dma = {}

        def need(eng, s, v):
            k = known[eng]
            if k.get(id(s), 0) < v:
                k[id(s)] = v
                prog[eng].append(("wait", s, v))

        for idx, (eng, fn, deps, is_dma) in enumerate(self.ops):
            for d in sorted(deps):
                deng = self.ops[d][0]
                if (not self.ops[d][3]) and deng == eng and eng == "pe":
                    continue
                s, v = handle[d]
                need(eng, s, v)
            if is_dma:
                q = eng
                slot = dnext[q]
                dnext[q] = (slot + 1) % len(dsem[q])
                s = dsem[q][slot]
                if dcnt[q][slot] > 0:
                    need(eng, s, 16 * dcnt[q][slot])
                dcnt[q][slot] += 1
                v = 16 * dcnt[q][slot]
                handle[idx] = (s, v)
                prog[eng].append(("dma", fn, s))
                last_dma[(q, slot)] = (s, v)
            else:
                cnt[eng] += 1
                handle[idx] = (sem[eng], cnt[eng])
                prog[eng].append(("op", fn, sem[eng]))
        for (q, slot), (s, v) in last_dma.items():
            need("sp", s, v)
        for e in ("pe", "act", "dve", "pool"):
            if cnt[e]:
                need("sp", sem[e], cnt[e])

        block = es.enter_context(nc.Block())

        def run(engobj, items):
            for it in items:
                if it[0] == "wait":
                    engobj.wait_ge(it[1], it[2])
                elif it[0] == "dma":
                    it[1](engobj).then_inc(it[2], 16)
                else:
                    it[1](engobj).then_inc(it[2], 1)

        @block.sync
        def _(e):
            run(e, prog["sp"])

        @block.gpsimd
        def _(e):
            run(e, prog["pool"])

        @block.scalar
        def _(e):
            run(e, prog["act"])

        @block.vector
        def _(e):
            run(e, prog["dve"])

        @block.tensor
        def _(e):
            run(e, prog["pe"])


class Builder:
    def __init__(self, cfg):
        self.cfg = cfg
        self.nc = bass.Bass("TRN2", target_bir_lowering=False)
        self.es = ExitStack()
        self.S = Sched(self.nc, self.es)
        self.res = {}
        self.tiles = {}

    def R(self, key):
        r = self.res.get(key)
        if r is None:
            r = self.res[key] = Res(str(key))
        return r

    def alias(self, child, parent):
        c, p = self.R(child), self.R(parent)
        c.parent = p
        p.kids.append(c)

    def sb(self, name, shape, dt=F32):
        t = self.es.enter_context(self.nc.sbuf_tensor(name, list(shape), dt))
        self.tiles[name] = t
        return t

    def ps(self, name, shape, dt=F32):
        t = self.es.enter_context(self.nc.psum_tensor(name, list(shape), dt))
        self.tiles[name] = t
        return t

    def dram(self, name, shape, dt=F32, kind="Internal"):
        return self.nc.dram_tensor(name, list(shape), dt, kind=kind).ap()

    def op(self, eng, fn, reads=(), writes=()):
        return self.S.op(eng, fn, [self.R(r) for r in reads], [self.R(w) for w in writes])

    def dma(self, q, out, in_, reads=(), writes=(), **kw):
        return self.S.dma(q, lambda e: e.dma_start(out=out, in_=in_, **kw),
                          [self.R(r) for r in reads], [self.R(w) for w in writes])


def _act(func, out, in_, **kw):
    return lambda e: e.activation(out=out, in_=in_, func=func, **kw)


class Kern(Builder):
    def build(self):
        nc = self.nc
        cfg = self.cfg
        di = lambda n, s: nc.dram_tensor(n, list(s), F32, kind="ExternalInput").ap()
        do = lambda n, s: nc.dram_tensor(n, list(s), F32, kind="ExternalOutput").ap()
        self.x_tok = di("x_tok", [TC, D])
        self.cvec = di("cvec", [2, D])
        self.norm_g = di("norm_g", [DEPTH, 3, D])
        self.final_g = di("final_norm_g", [D])
        self.w_ada = di("w_ada", [DEPTH, D, NMOD * D])
        self.b_ada = di("b_ada", [DEPTH, NMOD * D])
        self.w_in = di("w_in", [DEPTH, D, N_IN])
        self.b_in = di("b_in", [DEPTH, N_IN])
        self.w_br = di("w_br", [DEPTH, 3, WM, D])
        self.w_out = di("w_out", [DEPTH, D, D])
        self.w13 = di("ffn_w13", [DEPTH, 2, D, 2 * DFF])
        self.w2 = di("ffn_w2", [DEPTH, 2, DFF, D])
        self.ident_in = di("ident", [128, 128])
        self.y_tok = do("y_tok", [TC, D])
        self.xT = self.dram("xT", [KC, 128, TC])

        self.ident = self.sb("ident_sb", [128, 128], F32)
        self.dma("sp", self.ident[:], self.ident_in, writes=["ident"])
        self.ones_bf = self.sb("ones_bf", [128, 128], BF16)
        self.op("dve", lambda e: e.memset(self.ones_bf[:], 1.0), writes=["ones_bf"])
        self.eps_t = self.sb("eps_t", [128, 1], F32)
        self.op("dve", lambda e: e.memset(self.eps_t[:], EPS), writes=["eps_t"])

        self.NWR = 4
        self.wr = [self.sb("wr%d" % i, [128, KC, 128], BF16) for i in range(self.NWR)]
        self.wr_i = 0
        self.NPS = 4
        self.psb = [self.ps("psb%d" % i, [128, 512], F32) for i in range(8)]
        self.ps_i = 0
        hraw = self.sb("hT", [128, KC * TG // 2], F32)
        self.hraw = hraw
        self.hT = hraw[:].bitcast(BF16).rearrange("p (k t) -> p k t", k=KC)

        graw = self.sb("gT", [128, KC * TG // 2], F32)
        self.graw = graw
        gf = graw[:]
        self.gT = graw[:].bitcast(BF16).rearrange("p (k t) -> p k t", k=KC)
        self._xl = [gf[:, i * 2048:(i + 1) * 2048] for i in range(2)]
        self._xo = [gf[:, 4096 + i * 2048:4096 + (i + 1) * 2048].rearrange("p (k t) -> p k t", k=KC) for i in range(2)]
        self._yn = gf[:, 0:8192].rearrange("p (k t) -> p k t", k=KC)
        self._yo = [gf[:, 8192 + i * 2048:8192 + (i + 1) * 2048] for i in range(2)]
        for nm in ("xl0", "xl1", "xo0", "xo1", "yn", "yo0", "yo1"):
            self.alias(nm, "gT")
        self.xq = [self.sb("xq%d" % i, [128, 4, TT], F32) for i in range(3)]
        self.xq_i = 0
        self.sq = [self.sb("sq%d" % i, [128, 4, TT], BF16) for i in range(2)]
        self.rstd = self.sb("rstd", [128, TT], F32)
        self.tmpf = [self.sb("tmpf%d" % i, [128, TT], F32) for i in range(3)]
        self.tmp_i = 0
        self.xr = [self.sb("xr%d" % i, [128, TT], F32) for i in range(4)]
        self.xr_i = 0

        self.phase_load_x()
        self.phase_adaln()
        for l in range(DEPTH):
            for tg in range(NTG):
                self.norm_to_hT(tg, l, 0)
                self.ffn(tg, l, 0)
            if cfg.get("mixer", True):
                self.mixer(l)
            for tg in range(NTG):
                self.norm_to_hT(tg, l, 2)
                self.ffn(tg, l, 1)
        self.phase_final()
        self.S.emit()
        return nc

    def next_ps(self):
        i = self.ps_i
        self.ps_i = (i + 1) % self.NPS
        return i

    def next_tmp(self):
        i = self.tmp_i
        self.tmp_i = (i + 1) % len(self.tmpf)
        return i

    def phase_load_x(self):
        xl, xo = self._xl, self._xo
        for t in range(TC // 128):
            b = t % 2
            self.dma("sp", xl[b], self.x_tok[t * 128:(t + 1) * 128, :], writes=["xl%d" % b])
            for q in range(4):
                pi = self.next_ps()
                ps = self.psb[pi]

                def tr(e, b=b, q=q, ps=ps):
                    for j in range(4):
                        k = q * 4 + j
                        r = e.transpose(out=ps[:, j * 128:(j + 1) * 128], in_=xl[b][:, k * 128:(k + 1) * 128],
                                        identity=self.ident[:])
                    return r
                self.op("pe", tr, reads=["xl%d" % b, "ident"], writes=["psb%d" % pi])
                self.op("dve", lambda e, b=b, q=q, ps=ps: e.tensor_copy(
                    out=xo[b][:, q * 4:(q + 1) * 4, :], in_=ps[:].rearrange("p (a b) -> p a b", a=4)),
                    reads=["psb%d" % pi], writes=["xo%d" % b])
            self.dma("sp", self.xT[:, :, t * 128:(t + 1) * 128].rearrange("k p t -> p k t"), xo[b],
                     reads=["xo%d" % b], writes=[("xT", t // 4)])

    def load_vecT(self, name, src_1d, nblk):
        t = self.sb(name, [128, nblk], F32)
        self.dma("sp", t[:], src_1d.rearrange("(j p) -> p j", p=128), writes=[name],
                 allow_slow_non_contiguous=True)
        return t

    def phase_adaln(self):
        nc = self.nc
        cT = self.sb("cT", [128, KC, 2], F32)
        for b in range(2):
            self.dma("sp", cT[:, :, b], self.cvec[b].rearrange("(k p) -> p k", p=128), writes=["cT"],
                     allow_slow_non_contiguous=True)
        cS = self.sb("cS", [128, KC, 2], BF16)
        self.op("act", _act(AF.Silu, cS[:], cT[:]), reads=["cT"], writes=["cS"])
        self.mod = []
        self.nsc, self.nbi, self.gsc = {}, {}, {}
        for l in range(DEPTH):
            bT = self.load_vecT("badaT%d" % l, self.b_ada[l], NMOD * KC)
            gT_ = [self.load_vecT("ng%d_%d" % (l, i), self.norm_g[l, i], KC) for i in range(3)]
            mod = self.sb("mod%d" % l, [128, NMOD * KC, 2], F32)
            self.mod.append(mod)

            def evac(blk, tt, ps, pi, mod=mod, bT=bT, l=l):
                self.op("dve", lambda e: e.tensor_scalar(out=mod[:, blk, :], in0=ps[:, 0:2], scalar1=bT[:, blk:blk + 1],
                                                         scalar2=None, op0=ALU.add),
                        reads=["psb%d" % pi, "badaT%d" % l], writes=["mod%d" % l])
            blocks = [(self.w_ada[l][:, j * 128:(j + 1) * 128], KC, 128) for j in range(NMOD * KC)]
            self.linear(blocks, lambda k, tt: cS[:, k, :], ["cS"], [2], evac)
            for i in range(3):
                sc = self.sb("nsc%d_%d" % (l, i), [128, KC, 2], F32)
                bi = self.sb("nbi%d_%d" % (l, i), [128, KC, 2], F32)
                gs = self.sb("gsc%d_%d" % (l, i), [128, KC, 2], F32)
                m0 = mod[:, (3 * i) * KC:(3 * i + 1) * KC, :]
                m1 = mod[:, (3 * i + 1) * KC:(3 * i + 2) * KC, :]
                m2 = mod[:, (3 * i + 2) * KC:(3 * i + 3) * KC, :]
                for b in range(2):
                    self.op("dve", lambda e, sc=sc, m1=m1, b=b, g=gT_[i]: e.scalar_tensor_tensor(
                        out=sc[:, :, b], in0=m1[:, :, b], scalar=1.0, in1=g[:], op0=ALU.add, op1=ALU.mult),
                        reads=["mod%d" % l, "ng%d_%d" % (l, i)], writes=["nsc%d_%d" % (l, i)])
                self.op("dve", lambda e, bi=bi, m0=m0: e.tensor_copy(out=bi[:], in_=m0), reads=["mod%d" % l],
                        writes=["nbi%d_%d" % (l, i)])
                fac = 1.0 if i == 1 else 0.5
                self.op("dve", lambda e, gs=gs, m2=m2, fac=fac: e.tensor_scalar(
                    out=gs[:], in0=m2, scalar1=fac, scalar2=None, op0=ALU.mult), reads=["mod%d" % l],
                    writes=["gsc%d_%d" % (l, i)])
                self.nsc[(l, i)], self.nbi[(l, i)], self.gsc[(l, i)] = sc, bi, gs

    def linear(self, blocks, in_fn, in_res, ntoks, evac, pf=3):
        nb = len(blocks)
        slots = {}

        def issue(j):
            ap, kc, ncols = blocks[j]
            s = self.wr_i
            self.wr_i = (s + 1) % self.NWR
            slots[j] = s
            self.dma("pool", self.wr[s][:, 0:kc, 0:ncols], ap.rearrange("(k p) n -> p k n", p=128),
                     writes=["wr%d" % s])
        for j in range(min(pf, nb)):
            issue(j)
        for j in range(nb):
            if j + pf < nb:
                issue(j + pf)
            ap, kc, ncols = blocks[j]
            s = slots[j]
            for tt, nt in enumerate(ntoks):
                pi = self.next_ps()
                ps = self.psb[pi]

                def mm(e, s=s, kc=kc, ncols=ncols, tt=tt, nt=nt, ps=ps):
                    for k in range(kc):
                        r = e.matmul(ps[0:ncols, 0:nt], self.wr[s][:, k, 0:ncols], in_fn(k, tt),
                                     start=(k == 0), stop=(k == kc - 1))
                    return r
                self.op("pe", mm, reads=["wr%d" % s] + list(in_res), writes=["psb%d" % pi])
                evac(j, tt, ps, pi)

    def cvi(self, tg, tt):
        return 0 if (tg == 0 and tt == 0) else 1

    def load_xq(self, t0, q):
        xi = self.xq_i
        self.xq_i = (xi + 1) % len(self.xq)
        self.dma("sp", self.xq[xi][:], self.xT[q * 4:(q + 1) * 4, :, t0:t0 + TT].rearrange("k p t -> p k t"),
                 reads=[("xT", t0 // TT)], writes=["xq%d" % xi])
        return xi

    def rstd_tile(self, t0):
        pi = self.next_ps()
        ps = self.psb[pi]
        for q in range(4):
            xi = self.load_xq(t0, q)
            sqt = self.sq[q % 2]
            self.op("act", _act(AF.Square, sqt[:], self.xq[xi][:]), reads=["xq%d" % xi], writes=["sq%d" % (q % 2)])

            def mm(e, ps=ps, q=q, sqt=sqt):
                for k in range(4):
                    r = e.matmul(ps[:, :], self.ones_bf[:], sqt[:, k, :], start=(q == 0 and k == 0),
                                 stop=(q == 3 and k == 3))
                return r
            self.op("pe", mm, reads=["sq%d" % (q % 2), "ones_bf"], writes=["psb%d" % pi])
        self.op("act", _act(AF.Sqrt, self.rstd[:], ps[:, :], scale=1.0 / D, bias=self.eps_t[:]),
                reads=["psb%d" % pi, "eps_t"], writes=["rstd"])
        self.op("dve", lambda e: e.reciprocal(out=self.rstd[:], in_=self.rstd[:]), reads=["rstd"], writes=["rstd"])

    def norm_to_hT(self, tg, l, i):
        sc, bi = self.nsc[(l, i)], self.nbi[(l, i)]
        for tt in range(TG // TT):
            t0 = tg * TG + tt * TT
            self.rstd_tile(t0)
            cv = self.cvi(tg, tt)
            for q in range(4):
                xi = self.load_xq(t0, q)
                for kk in range(4):
                    k = q * 4 + kk
                    ti = self.next_tmp()
                    tm = self.tmpf[ti]
                    self.op("dve", lambda e, kk=kk, tm=tm, xi=xi: e.tensor_tensor(
                        out=tm[:], in0=self.xq[xi][:, kk, :], in1=self.rstd[:], op=ALU.mult),
                        reads=["xq%d" % xi, "rstd"], writes=["tmpf%d" % ti])
                    self.op("act", _act(AF.Identity, self.hT[:, k, tt * TT:(tt + 1) * TT], tm[:],
                                        scale=sc[:, k, cv:cv + 1], bias=bi[:, k, cv:cv + 1]),
                            reads=["tmpf%d" % ti, "nsc%d_%d" % (l, i), "nbi%d_%d" % (l, i)], writes=["hT"])

    def resid_evac(self, tg, gs, gs_name):
        def evac(blk, tt, ps, pi):
            t0 = tg * TG + tt * TT
            gtt = t0 // TT
            xi = self.xr_i
            self.xr_i = (xi + 1) % len(self.xr)
            xr = self.xr[xi]
            cv = self.cvi(tg, tt)
            self.dma("sp", xr[:], self.xT[blk, :, t0:t0 + TT], reads=[("xT", gtt)], writes=["xr%d" % xi])
            self.op("dve", lambda e: e.scalar_tensor_tensor(out=xr[:], in0=ps[:, :], scalar=gs[:, blk, cv:cv + 1],
                                                            in1=xr[:], op0=ALU.mult, op1=ALU.add),
                    reads=["psb%d" % pi, "xr%d" % xi, gs_name], writes=["xr%d" % xi])
            self.dma("sp", self.xT[blk, :, t0:t0 + TT], xr[:], reads=["xr%d" % xi], writes=[("xT", gtt)])
        return evac

    def ffn(self, tg, l, which):
        i = 0 if which == 0 else 2
        w13 = self.w13[l, which]
        w2 = self.w2[l, which]
        gs = self.gsc[(l, i)]
        NJ = DFF // 128 // 2
        for half in range(2):
            blocks = []
            for jj in range(NJ):
                j = half * NJ + jj
                blocks.append((w13[:, j * 128:(j + 1) * 128], KC, 128))
                blocks.append((w13[:, DFF + j * 128:DFF + (j + 1) * 128], KC, 128))
            sa = {}

            def evac13(blk, tt, ps, pi, sa=sa):
                jj, isb = blk // 2, blk % 2
                if not isb:
                    ti = self.next_tmp()
                    sa[tt] = ti
                    self.op("act", _act(AF.Silu, self.tmpf[ti][:], ps[:, :]), reads=["psb%d" % pi],
                            writes=["tmpf%d" % ti])
                else:
                    ti = sa[tt]
                    self.op("dve", lambda e: e.tensor_tensor(out=self.gT[:, jj, tt * TT:(tt + 1) * TT],
                                                             in0=self.tmpf[ti][:], in1=ps[:, :], op=ALU.mult),
                            reads=["psb%d" % pi, "tmpf%d" % ti], writes=["gT"])
            self.linear(blocks, lambda k, tt: self.hT[:, k, tt * TT:(tt + 1) * TT], ["hT"], [TT] * 3, evac13)
            r0 = half * (DFF // 2)
            blocks2 = [(w2[r0:r0 + DFF // 2, j * 128:(j + 1) * 128], KC, 128) for j in range(KC)]
            self.linear(blocks2, lambda k, tt: self.gT[:, k, tt * TT:(tt + 1) * TT], ["gT"], [TT] * 3,
                        self.resid_evac(tg, gs, "gsc%d_%d" % (l, i)))

    def phase_final(self):
        fg = self.load_vecT("fgT", self.final_g, KC)
        yo, yn = self._yo, self._yn
        for gtt in range(NTT):
            t0 = gtt * TT
            self.rstd_tile(t0)
            for q in range(4):
                xi = self.load_xq(t0, q)
                for kk in range(4):
                    k = q * 4 + kk
                    self.op("dve", lambda e, k=k, kk=kk, xi=xi: e.scalar_tensor_tensor(
                        out=yn[:, k, :], in0=self.xq[xi][:, kk, :], scalar=fg[:, k:k + 1], in1=self.rstd[:],
                        op0=ALU.mult, op1=ALU.mult), reads=["xq%d" % xi, "rstd", "fgT"], writes=["yn"])
            for s_ in range(TT // 128):
                b = (gtt * 4 + s_) % 2
                for q in range(4):
                    pi = self.next_ps()
                    ps = self.psb[pi]

                    def tr(e, q=q, ps=ps, s_=s_):
                        for j in range(4):
                            k = q * 4 + j
                            r = e.transpose(out=ps[:, j * 128:(j + 1) * 128], in_=yn[:, k, s_ * 128:(s_ + 1) * 128],
                                            identity=self.ident[:])
                        return r
                    self.op("pe", tr, reads=["yn", "ident"], writes=["psb%d" % pi])
                    self.op("act", lambda e, b=b, q=q, ps=ps: e.copy(out=yo[b][:, q * 512:(q + 1) * 512], in_=ps[:, :]),
                            reads=["psb%d" % pi], writes=["yo%d" % b])
                r0 = t0 + s_ * 128
                self.dma("sp", self.y_tok[r0:r0 + 128, :], yo[b], reads=["yo%d" % b], writes=[("y", r0)])

    def mixer_setup(self):
        nc = self.nc
        di = lambda n, s: nc.dram_tensor(n, list(s), F32, kind="ExternalInput").ap()
        do = lambda n, s: nc.dram_tensor(n, list(s), F32, kind="ExternalOutput").ap()
        self.cst_in = di("cst", [128, NCST])
        self.ropeC = di("ropeC", [128, 4096])
        self.ropeS = di("ropeS", [128, 4096])
        self.f_bias = di("mlstm_f_bias", [DEPTH, 16])
        self.conv_w = di("conv_w", [DEPTH, 3, 3 * WM])
        self.A_log = di("delta_A_log", [DEPTH, 16])
        self.dt_bias = di("delta_dt_bias", [DEPTH, 16])
        self.lgam = di("ret_log_gamma", [DEPTH, 16])
        self.hng = di("head_norm_g", [DEPTH, 3, HD])
        self.sC = di("st_C", [DEPTH, 2, NH, HD, HD])
        self.sn = di("st_n", [DEPTH, 2, NH, HD])
        self.sm = di("st_m", [DEPTH, 16])
        self.sD = di("st_D", [DEPTH, 2, NH, HD, HD])
        self.sR = di("st_R", [DEPTH, 2, NH, HD, HD])
        self.oC = do("o_C", [2, DEPTH, 2, NH, HD, HD])
        self.on = do("o_n", [2, DEPTH, 2, NH, HD])
        self.om = do("o_m", [2, DEPTH, 16])
        self.oD = do("o_D", [2, DEPTH, 2, NH, HD, HD])
        self.oR = do("o_R", [2, DEPTH, 2, NH, HD, HD])
        self.P = {m: self.dram("P" + m, [24, 128, TC], BF16) for m in "ABC"}
        self.PRE = self.dram("PRE", [24, 128, TC], F32)
        self.GO = self.dram("GO", [3, 8, 128, TC], BF16)
        self.GM = self.dram("GM", [48, 128, TC], BF16)
        self.GA = self.dram("GA", [4, 16, TC], F32)
        self.YS = self.dram("YS", [3, 8, 128, TC], BF16)
        self.OFs = self.dram("OFs", [TC, WM], F32)
        self.MG = self.dram("MG", [KC, 128, TC], BF16)
        self.cst = self.sb("cst_sb", [128, NCST], F32)
        self.dma("sp", self.cst[:], self.cst_in, writes=["cst"])
        self.ident_bf = self.sb("ident_bf", [128, 128], BF16)
        self.op("dve", lambda e: e.tensor_copy(out=self.ident_bf[:], in_=self.ident[:]), reads=["ident"], writes=["ident_bf"])
        self.ones_f = self.sb("ones_f", [128, 256], F32)
        self.op("dve", lambda e: e.memset(self.ones_f[:], 1.0), writes=["ones_f"])
        self.stb = [self.sb("stb%d" % i, [128, TT], BF16) for i in range(3)]
        self.stb_i = 0
        self.binT = self.sb("binT", [128, 160], F32)
        self.gpar = self.sb("gpar", [16, 8], F32)
        self.rC = self.sb("rC", [128, 3, TT], F32)
        self.rS = self.sb("rS", [128, 3, TT], F32)

    def C(self, name, rows=64):
        o, n = CST[name]
        return self.cst[0:rows, o:o + n]

    def next_stb(self):
        i = self.stb_i
        self.stb_i = (i + 1) % 3
        return i

    def w_in_blocks(self, l):
        B = []
        s = HD ** -0.5
        for j in range(8):
            B.append((OFF["qA"] + j * 128, 128, "lin", ("PA", j), 1.0))
        for j in range(8):
            B.append((OFF["kA"] + j * 128, 128, "lin", ("PA", 8 + j), s))
        for j in range(8):
            B.append((OFF["vA"] + j * 128, 128, "lin", ("PA", 16 + j), 1.0))
        for j in range(8):
            B.append((OFF["oA"] + j * 128, 128, "sig", ("GO", 0, j), 1.0))
        B.append((OFF["iA"], 16, "gi", ("GA", 0), 1.0))
        B.append((OFF["fA"], 16, "gf", ("GA", 1), 1.0))
        for j in range(24):
            B.append((OFF["qkvB"] + j * 128, 128, "pre", ("PRE", j), 1.0))
        for j in range(8):
            B.append((OFF["zB"] + j * 128, 128, "silu", ("GO", 1, j), 1.0))
        B.append((OFF["betaB"], 16, "gb", ("GA", 2), 1.0))
        B.append((OFF["aB"], 16, "gg", ("GA", 3), 1.0))
        for j in range(8):
            B.append((OFF["qC"] + j * 128, 128, "rope", ("PC", j), s))
        for j in range(8):
            B.append((OFF["kC"] + j * 128, 128, "rope", ("PC", 8 + j), 1.0))
        for j in range(8):
            B.append((OFF["vC"] + j * 128, 128, "lin", ("PC", 16 + j), 1.0))
        for j in range(8):
            B.append((OFF["gC"] + j * 128, 128, "silu", ("GO", 2, j), 1.0))
        for j in range(48):
            B.append((OFF["gm"] + j * 128, 128, "sig", ("GM", j), 1.0))
        return B

    def dst_ap(self, dst, t0, n, rows=128):
        if dst[0] in ("PA", "PB", "PC"):
            return self.P[dst[0][1]][dst[1], 0:rows, t0:t0 + n]
        if dst[0] == "PRE":
            return self.PRE[dst[1], 0:rows, t0:t0 + n]
        if dst[0] == "GO":
            return self.GO[dst[1], dst[2], 0:rows, t0:t0 + n]
        if dst[0] == "GM":
            return self.GM[dst[1], 0:rows, t0:t0 + n]
        if dst[0] == "GA":
            return self.GA[dst[1], 0:rows, t0:t0 + n]
        raise KeyError(dst)

    def mixer_params(self, l):
        B = self.w_in_blocks(l)
        self._B = B
        for bi, (c0, nco, kind, dst, sc) in enumerate(B):
            self.dma("sp", self.binT[0:nco, bi:bi + 1], self.b_in[l, c0:c0 + nco].rearrange("(p o) -> p o", o=1),
                     writes=["binT"], allow_slow_non_contiguous=True)
        gp = self.gpar
        for j, src in enumerate((self.f_bias, self.A_log, self.dt_bias)):
            self.dma("sp", gp[:, j:j + 1], src[l].rearrange("(p o) -> p o", o=1), writes=["gpar"],
                     allow_slow_non_contiguous=True)
        bi_f = [i for i, b in enumerate(B) if b[2] == "gf"][0]
        bi_g = [i for i, b in enumerate(B) if b[2] == "gg"][0]
        self.op("dve", lambda e: e.scalar_tensor_tensor(out=gp[:, 3:4], in0=gp[:, 0:1], scalar=-1.0,
                                                        in1=self.binT[0:16, bi_f:bi_f + 1], op0=ALU.mult, op1=ALU.subtract),
                reads=["gpar", "binT"], writes=["gpar"])
        self.op("dve", lambda e: e.tensor_tensor(out=gp[:, 4:5], in0=gp[:, 2:3], in1=self.binT[0:16, bi_g:bi_g + 1],
                                                 op=ALU.add), reads=["gpar", "binT"], writes=["gpar"])
        self.op("act", _act(AF.Exp, gp[:, 5:6], gp[:, 1:2]), reads=["gpar"], writes=["gpar"])
        self.op("dve", lambda e: e.tensor_scalar(out=gp[:, 5:6], in0=gp[:, 5:6], scalar1=-1.0, scalar2=None, op0=ALU.mult),
                reads=["gpar"], writes=["gpar"])

    def m1_project(self, l, tg):
        B = self._B
        blocks = [(self.w_in[l][:, c0:c0 + nco], KC, nco) for (c0, nco, kind, dst, sc) in B]
        for tt in range(3):
            if self.cvi(tg, tt):
                p0 = tg * TG + tt * TT - 512
                self.dma("sp", self.rC[:, tt, :], self.ropeC[:, p0:p0 + TT], writes=["rC"])
                self.dma("sp", self.rS[:, tt, :], self.ropeS[:, p0:p0 + TT], writes=["rS"])
        gp = self.gpar

        def evac(blk, tt, ps, pi):
            c0, nco, kind, dst, sc = B[blk]
            t0 = tg * TG + tt * TT
            gtt = t0 // TT
            bias = self.binT[0:nco, blk:blk + 1]
            pr = ["psb%d" % pi, "binT"]
            wr = [(dst, gtt)]
            if kind in ("lin", "sig", "silu") or (kind == "rope" and not self.cvi(tg, tt)):
                si = self.next_stb()
                st = self.stb[si]
                if kind in ("lin", "rope"):
                    self.op("dve", lambda e: e.tensor_scalar(out=st[:], in0=ps[:, :], scalar1=bias, scalar2=sc,
                                                             op0=ALU.add, op1=ALU.mult), reads=pr, writes=["stb%d" % si])
                else:
                    f = AF.Sigmoid if kind == "sig" else AF.Silu
                    self.op("act", _act(f, st[:], ps[:, :], bias=bias), reads=pr, writes=["stb%d" % si])
                self.dma("sp", self.dst_ap(dst, t0, TT), st[:], reads=["stb%d" % si], writes=wr)
            elif kind == "rope":
                ti = self.next_tmp()
                xf = self.tmpf[ti]
                self.op("dve", lambda e: e.tensor_scalar(out=xf[:], in0=ps[:, :], scalar1=bias, scalar2=sc,
                                                         op0=ALU.add, op1=ALU.mult), reads=pr, writes=["tmpf%d" % ti])
                p2 = self.next_ps()
                ps2 = self.psb[p2]
                self.op("pe", lambda e: e.matmul(ps2[:, :], self.C("Rm", 128), xf[:], start=True, stop=True),
                        reads=["tmpf%d" % ti, "cst"], writes=["psb%d" % p2])
                t2 = self.next_tmp()
                x2 = self.tmpf[t2]
                self.op("dve", lambda e: e.tensor_tensor(out=x2[:], in0=ps2[:, :], in1=self.rS[:, tt, :], op=ALU.mult),
                        reads=["psb%d" % p2, "rS"], writes=["tmpf%d" % t2])
                self.op("dve", lambda e: e.tensor_tensor(out=xf[:], in0=xf[:], in1=self.rC[:, tt, :], op=ALU.mult),
                        reads=["tmpf%d" % ti, "rC"], writes=["tmpf%d" % ti])
                si = self.next_stb()
                st = self.stb[si]
                self.op("dve", lambda e: e.tensor_tensor(out=st[:], in0=xf[:], in1=x2[:], op=ALU.add),
                        reads=["tmpf%d" % ti, "tmpf%d" % t2], writes=["stb%d" % si])
                self.dma("sp", self.dst_ap(dst, t0, TT), st[:], reads=["stb%d" % si], writes=wr)
            else:
                ti = self.next_tmp()
                tm = self.tmpf[ti]
                tr = ["tmpf%d" % ti]
                n = nco
                if kind == "pre" or kind == "gi":
                    self.op("dve", lambda e: e.tensor_scalar(out=tm[0:n, :], in0=ps[0:n, :], scalar1=bias, scalar2=None,
                                                             op0=ALU.add), reads=pr, writes=tr)
                elif kind == "gb":
                    self.op("act", _act(AF.Sigmoid, tm[0:n, :], ps[0:n, :], bias=bias), reads=pr, writes=tr)
                elif kind == "gf":
                    self.op("act", _act(AF.Exp, tm[0:n, :], ps[0:n, :], scale=-1.0, bias=gp[:, 3:4]),
                            reads=pr + ["gpar"], writes=tr)
                    self.op("act", _act(AF.Ln, tm[0:n, :], tm[0:n, :], bias=1.0), reads=tr, writes=tr)
                    self.op("dve", lambda e: e.tensor_scalar(out=tm[0:n, :], in0=tm[0:n, :], scalar1=-1.0, scalar2=None,
                                                             op0=ALU.mult), reads=tr, writes=tr)
                elif kind == "gg":
                    self.op("act", _act(AF.Exp, tm[0:n, :], ps[0:n, :], bias=gp[:, 4:5]), reads=pr + ["gpar"], writes=tr)
                    self.op("act", _act(AF.Ln, tm[0:n, :], tm[0:n, :], bias=1.0), reads=tr, writes=tr)
                    self.op("dve", lambda e: e.tensor_scalar(out=tm[0:n, :], in0=tm[0:n, :], scalar1=gp[:, 5:6],
                                                             scalar2=None, op0=ALU.mult), reads=tr + ["gpar"], writes=tr)
                self.dma("sp", self.dst_ap(dst, t0, TT, rows=n), tm[0:n, :], reads=tr, writes=wr)
        self.linear(blocks, lambda k, tt: self.hT[:, k, tt * TT:(tt + 1) * TT], ["hT"], [TT] * 3, evac)

    def conv_pass(self, l):
        cw = [self.load_vecT("cw%d_%d" % (l, j), self.conv_w[l, j], 24) for j in range(3)]
        cv = self.sb("cvt%d" % l, [128, TT + 2], F32) if l == 0 else self._cv
        self._cv = cv
        pieces = [(0, 256, True, True), (256, 256, True, True)]
        for i in range(8):
            pieces.append((512 + i * 512, 512, i == 0, i == 7))
        for blk in range(24):
            for (t0, n, first, last) in pieces:
                lo = t0 - (0 if first else 1)
                hi = t0 + n + (0 if last else 1)
                if first or last:
                    self.op("dve", lambda e: e.memset(cv[:, 0:n + 2], 0.0), writes=["cvt"])
                gts = sorted(set([lo // TT, (hi - 1) // TT]))
                self.dma("sp", cv[:, (lo - t0 + 1):(hi - t0 + 1)], self.PRE[blk, :, lo:hi],
                         reads=[(("PRE", blk), g) for g in gts], writes=["cvt"])
                ti = self.next_tmp()
                y = self.tmpf[ti]
                tr = ["tmpf%d" % ti]
                names = ["cw%d_%d" % (l, j) for j in range(3)]
                self.op("dve", lambda e, y=y, n=n, blk=blk: e.tensor_scalar(
                    out=y[:, 0:n], in0=cv[:, 1:n + 1], scalar1=cw[1][:, blk:blk + 1], scalar2=None, op0=ALU.mult),
                    reads=["cvt"] + names, writes=tr)
                self.op("dve", lambda e, y=y, n=n, blk=blk: e.scalar_tensor_tensor(
                    out=y[:, 0:n], in0=cv[:, 0:n], scalar=cw[0][:, blk:blk + 1], in1=y[:, 0:n], op0=ALU.mult,
                    op1=ALU.add), reads=["cvt"] + names + tr, writes=tr)
                self.op("dve", lambda e, y=y, n=n, blk=blk: e.scalar_tensor_tensor(
                    out=y[:, 0:n], in0=cv[:, 2:n + 2], scalar=cw[2][:, blk:blk + 1], in1=y[:, 0:n], op0=ALU.mult,
                    op1=ALU.add), reads=["cvt"] + names + tr, writes=tr)
                self.op("act", _act(AF.Silu, y[:, 0:n], y[:, 0:n]), reads=tr, writes=tr)
                si = self.next_stb()
                st = self.stb[si]
                if blk < 16:
                    sq = self.sq[0]
                    self.op("act", _act(AF.Square, sq[:, 0, 0:n], y[:, 0:n]), reads=tr, writes=["sq0"])
                    pi = self.next_ps()
                    ps = self.psb[pi]
                    self.op("pe", lambda e, ps=ps, n=n, sq=sq: e.matmul(ps[:, 0:n], self.ones_bf[:], sq[:, 0, 0:n],
                                                                        start=True, stop=True),
                            reads=["sq0", "ones_bf"], writes=["psb%d" % pi])
                    self.op("act", _act(AF.Sqrt, self.rstd[:, 0:n], ps[:, 0:n], bias=self.eps_t[:]),
                            reads=["psb%d" % pi, "eps_t"], writes=["rstd"])
                    self.op("dve", lambda e, n=n: e.reciprocal(out=self.rstd[:, 0:n], in_=self.rstd[:, 0:n]),
                            reads=["rstd"], writes=["rstd"])
                    sc = HD ** -0.5 if blk < 8 else 1.0
                    self.op("dve", lambda e, y=y, n=n, st=st, sc=sc: e.scalar_tensor_tensor(
                        out=st[:, 0:n], in0=y[:, 0:n], scalar=sc, in1=self.rstd[:, 0:n], op0=ALU.mult, op1=ALU.mult),
                        reads=tr + ["rstd"], writes=["stb%d" % si])
                else:
                    self.op("act", lambda e, y=y, n=n, st=st: e.copy(out=st[:, 0:n], in_=y[:, 0:n]), reads=tr,
                            writes=["stb%d" % si])
                self.dma("sp", self.P["B"][blk, :, t0:t0 + n], st[:, 0:n], reads=["stb%d" % si],
                         writes=[(("PB", blk), g) for g in sorted(set([t0 // TT, (t0 + n - 1) // TT]))])

    def m3_merge(self, l, tg):
        hv = self.hraw[:].bitcast(BF16)
        gv = self.graw[:].bitcast(BF16)
        ys = [hv[:, 0:8 * TG].rearrange("p (h t) -> p h t", h=8), hv[:, 8 * TG:16 * TG].rearrange("p (h t) -> p h t", h=8),
              gv[:, 0:8 * TG].rearrange("p (h t) -> p h t", h=8)]
        ysn = ["hT", "hT", "gT"]
        t_0 = tg * TG
        for i in range(3):
            self.dma("sp", ys[i], self.YS[i, :, :, t_0:t_0 + TG].rearrange("h p t -> p h t"),
                     reads=[(("YS", i), (t_0 // TT) + j) for j in range(3)], writes=[ysn[i]])
        acc = {}
        for j in range(KC):
            blocks = [(self.w_br[l, i][:, j * 128:(j + 1) * 128], 8, 128) for i in range(3)]

            def evac(blk, tt, ps, pi, j=j):
                i = blk
                t0 = t_0 + tt * TT
                gtt = t0 // TT
                si = self.next_stb()
                gmt = self.stb[si]
                self.dma("sp", gmt[:], self.GM[i * KC + j, :, t0:t0 + TT], reads=[(("GM", i * KC + j), gtt)],
                         writes=["stb%d" % si])
                if i == 0:
                    ti = self.next_tmp()
                    acc[tt] = ti
                    self.op("dve", lambda e: e.tensor_tensor(out=self.tmpf[ti][:], in0=ps[:, :], in1=gmt[:], op=ALU.mult),
                            reads=["psb%d" % pi, "stb%d" % si], writes=["tmpf%d" % ti])
                else:
                    ti = acc[tt]
                    a = self.tmpf[ti]
                    xi = self.xr_i
                    self.xr_i = (xi + 1) % len(self.xr)
                    t2 = self.xr[xi]
                    self.op("dve", lambda e: e.tensor_tensor(out=t2[:], in0=ps[:, :], in1=gmt[:], op=ALU.mult),
                            reads=["psb%d" % pi, "stb%d" % si], writes=["xr%d" % xi])
                    if i == 1:
                        self.op("dve", lambda e: e.tensor_tensor(out=a[:], in0=a[:], in1=t2[:], op=ALU.add),
                                reads=["tmpf%d" % ti, "xr%d" % xi], writes=["tmpf%d" % ti])
                    else:
                        s2 = self.next_stb()
                        mo = self.stb[s2]
                        self.op("dve", lambda e: e.tensor_tensor(out=mo[:], in0=a[:], in1=t2[:], op=ALU.add),
                                reads=["tmpf%d" % ti, "xr%d" % xi], writes=["stb%d" % s2])
                        self.dma("sp", self.MG[j, :, t0:t0 + TT], mo[:], reads=["stb%d" % s2], writes=[(("MG", j), gtt)])
            self.linear3(blocks, ys, ysn, evac)
        self.dma("sp", self.hT, self.MG[:, :, t_0:t_0 + TG].rearrange("k p t -> p k t"),
                 reads=[(("MG", j), (t_0 // TT) + q) for j in range(KC) for q in range(3)], writes=["hT"])
        blocks = [(self.w_out[l][:, j * 128:(j + 1) * 128], KC, 128) for j in range(KC)]
        self.linear(blocks, lambda k, tt: self.hT[:, k, tt * TT:(tt + 1) * TT], ["hT"], [TT] * 3,
                    self.resid_evac(tg, self.gsc[(l, 1)], "gsc%d_1" % l))

    def linear3(self, blocks, ys, ysn, evac):
        slots = []
        for (ap, kc, ncols) in blocks:
            s = self.wr_i
            self.wr_i = (s + 1) % self.NWR
            slots.append(s)
            self.dma("pool", self.wr[s][:, 0:kc, 0:ncols], ap.rearrange("(k p) n -> p k n", p=128), writes=["wr%d" % s])
        for tt in range(3):
            for i in range(3):
                s = slots[i]
                pi = self.next_ps()
                ps = self.psb[pi]

                def mm(e, s=s, i=i, tt=tt, ps=ps):
                    for k in range(8):
                        r = e.matmul(ps[:, :], self.wr[s][:, k, :], ys[i][:, k, tt * TT:(tt + 1) * TT],
                                     start=(k == 0), stop=(k == 7))
                    return r
                self.op("pe", mm, reads=["wr%d" % s, ysn[i]], writes=["psb%d" % pi])
                evac(i, tt, ps, pi)
    def ar(self, name, parts, free, dt=F32):
        n = int(np.prod(free))
        nf = n if dt == F32 else (n + 1) // 2
        for raw, pname, key in ((self.hraw, "hT", "h"), (self.graw, "gT", "g")):
            off = self._aro[key]
            if off + nf <= 12288:
                self._aro[key] = off + nf
                v = raw[0:parts, off:off + nf]
                if dt != F32:
                    v = v.bitcast(dt)
                if len(free) == 2:
                    v = v.rearrange("p (a b) -> p a b", a=free[0])
                self.alias(name, pname)
                return v
        raise RuntimeError("arena full " + name)

    def scan_setup(self):
        self._aro = {"h": 0, "g": 0}
        a = self.ar
        T = {}
        for n in ("qT", "kT", "vT", "gt"):
            T[n] = a(n, 128, [8, TT], BF16)
        T["S"] = a("S", 128, [8, 128]); T["Sb"] = a("Sb", 128, [8, 128], BF16)
        T["nS"] = a("nS", 128, [8, 2]); T["nb"] = a("nb", 128, [8, 2], BF16)
        T["o"] = a("o_sb", 64, [8, 128]); T["of"] = a("of_sb", 64, [8, 128])
        T["U0"] = T["of"]
        for n in ("gi", "gf", "gb", "gg"):
            T[n] = a(n, 16, [TT])
        for n in ("ktm", "vtm", "khat", "bv", "bk", "kend", "Ub"):
            T[n] = a(n, 64, [8, 128], BF16)
        for n in ("ATb", "qkTb", "TTb"):
            T[n] = a(n, 64, [8, 64], BF16)
        T["WkT"] = a("WkT", 128, [8, 64], BF16)
        T["ysc"] = a("ysc", 128, [8, 64], BF16)
        for n in ("MT", "t64", "dec", "A", "AT", "T", "TT_", "W", "OkT", "qk"):
            T[n] = a(n, 64, [8, 64])
        T["gtm"] = a("gtm", 64, [32])
        for n in ("sa", "sb_", "sc_", "sd_", "se_"):
            T[n] = a(n, 64, [8])
        T["eL"] = a("eL", 128, [8]); T["g64"] = a("g64", 128, [8]); T["lgb"] = a("lgb", 128, [8])
        T["rsc"] = a("rsc", 64, [8]); T["ksc"] = a("ksc", 64, [8]); T["ss"] = a("ss", 64, [8])
        T["m0b"] = a("m0b", 128, [8])
        self.T = T
        self.hg = [[None] * 3 for _ in range(DEPTH)]

    def bank(self, i, rows, shape, dt=F32):
        v = self.psb[i][0:rows, :]
        if dt != F32:
            v = v.bitcast(dt)
        n = int(np.prod(shape))
        v = v[:, 0:n]
        if len(shape) == 2:
            v = v.rearrange("p (a b) -> p a b", a=shape[0])
        return v

    def bc(self, ap2, n):
        return ap2.unsqueeze(2).broadcast_to([ap2.shape[0], ap2.shape[1], n])

    def bm(self, ap2):
        return ap2.unsqueeze(1).broadcast_to([ap2.shape[0], 8, ap2.shape[1]])

    def load_tt(self, m, gtt, bwd):
        T = self.T
        t0 = gtt * TT
        P = self.P["ABC"[m]]
        for i, n in enumerate(("qT", "kT", "vT")):
            self.dma("sp", T[n], P[8 * i:8 * i + 8, :, t0:t0 + TT].rearrange("h p t -> p h t"),
                     reads=[(("P" + "ABC"[m], 8 * i + h), gtt) for h in range(8)], writes=[n])
        if bwd:
            self.dma("sp", T["gt"], self.GO[m, :, :, t0:t0 + TT].rearrange("h p t -> p h t"),
                     reads=[(("GO", m, h), gtt) for h in range(8)], writes=["gt"])
        if m == 0:
            self.dma("sp", T["gi"], self.GA[0, :, t0:t0 + TT], reads=[(("GA", 0), gtt)], writes=["gi"])
            self.dma("sp", T["gf"], self.GA[1, :, t0:t0 + TT], reads=[(("GA", 1), gtt)], writes=["gf"])
        if m == 1:
            self.dma("sp", T["gb"], self.GA[2, :, t0:t0 + TT], reads=[(("GA", 2), gtt)], writes=["gb"])
            self.dma("sp", T["gg"], self.GA[3, :, t0:t0 + TT], reads=[(("GA", 3), gtt)], writes=["gg"])

    def mm8(self, banks, rows, width, fn, reads):
        T = self.T
        per = 8 // len(banks)
        for bi, b in enumerate(banks):
            v = self.bank(b, rows, [per, width])

            def f(e, bi=bi, v=v):
                r = None
                for hh in range(per):
                    r = fn(e, bi * per + hh, v[:, hh, :])
                return r
            self.op("pe", f, reads=reads, writes=["psb%d" % b])

    def evac8(self, eng, banks, rows, width, fn, reads, writes):
        per = 8 // len(banks)
        for bi, b in enumerate(banks):
            v = self.bank(b, rows, [per, width])
            hs = slice(bi * per, (bi + 1) * per)
            self.op(eng, lambda e, v=v, hs=hs: fn(e, v, hs), reads=reads + ["psb%d" % b], writes=writes)

    def kv_tm(self, c0):
        T = self.T
        for src, bk_, dst in (("kT", 2, "ktm"), ("vT", 3, "vtm")):
            v = self.bank(bk_, 64, [8, 128], BF16)

            def f(e, src=src, v=v):
                for h in range(8):
                    r = e.transpose(out=v[:, h, :], in_=T[src][:, h, c0:c0 + L], identity=self.ident_bf[:])
                return r
            self.op("pe", f, reads=[src, "ident_bf"], writes=["psb%d" % bk_])
        v3 = self.bank(3, 64, [8, 128], BF16)
        self.op("act", lambda e: e.copy(out=T["vtm"], in_=v3), reads=["psb3"], writes=["vtm"])

    def scale_k(self, dst, sc_name, sc_ap):
        T = self.T
        v2 = self.bank(2, 64, [8, 128], BF16)
        self.op("dve", lambda e: e.tensor_tensor(out=T[dst], in0=v2, in1=self.bc(sc_ap, 128), op=ALU.mult),
                reads=["psb2", sc_name], writes=[dst])

    def gate_tm(self, rows_a, rows_b, c0):
        T = self.T
        v = self.psb[1]

        def f(e):
            e.transpose(out=v[0:64, 0:16], in_=T[rows_a][:, c0:c0 + L], identity=self.ident[0:16, 0:16])
            return e.transpose(out=v[0:64, 16:32], in_=T[rows_b][:, c0:c0 + L], identity=self.ident[0:16, 0:16])
        self.op("pe", f, reads=[rows_a, rows_b, "ident"], writes=["psb1"])
        self.op("dve", lambda e: e.tensor_copy(out=T["gtm"], in_=v[0:64, 0:32]), reads=["psb1"], writes=["gtm"])

    def state_update(self, banks_src_fn, dec_name, dec_ap, with_n=False):
        T = self.T
        self.op("dve", lambda e: e.tensor_tensor(out=T["S"], in0=T["S"], in1=self.bc(dec_ap, 128), op=ALU.mult),
                reads=["S", dec_name], writes=["S"])
        self.evac8("dve", [6, 7], 128, 128, lambda e, v, hs: e.tensor_tensor(out=T["S"][:, hs, :], in0=T["S"][:, hs, :],
                                                                               in1=v, op=ALU.add), ["S"], ["S"])
        self.op("act", lambda e: e.copy(out=T["Sb"], in_=T["S"]), reads=["S"], writes=["Sb"])

    def finalize(self, l, m, c0, t0):
        T = self.T
        gtt = t0 // TT
        self.dma("sp", T["of"], self.OFs[t0:t0 + L, :].rearrange("p (h e) -> p h e", h=8), reads=[("OF", t0)],
                 writes=["of_sb"])
        self.op("dve", lambda e: e.tensor_tensor(out=T["o"], in0=T["o"], in1=T["of"], op=ALU.add),
                reads=["o_sb", "of_sb"], writes=["o_sb"])
        self.op("dve", lambda e: e.tensor_tensor(out=T["of"], in0=T["o"], in1=T["o"], op=ALU.mult), reads=["o_sb"],
                writes=["of_sb"])
        self.op("dve", lambda e: e.tensor_reduce(out=T["ss"], in_=T["of"], op=ALU.add, axis=AX.X), reads=["of_sb"],
                writes=["ss"])
        self.op("act", _act(AF.Sqrt, T["ss"], T["ss"], scale=1.0 / HD, bias=self.eps_t[0:64, :]), reads=["ss", "eps_t"],
                writes=["ss"])
        self.op("dve", lambda e: e.reciprocal(out=T["ss"], in_=T["ss"]), reads=["ss"], writes=["ss"])
        self.op("dve", lambda e: e.tensor_tensor(out=T["Ub"], in0=T["o"], in1=self.bc(T["ss"], 128), op=ALU.mult),
                reads=["o_sb", "ss"], writes=["Ub"])
        v = self.bank(2, 128, [8, 64], BF16)

        def f(e):
            for h in range(8):
                r = e.transpose(out=v[:, h, :], in_=T["Ub"][:, h, :], identity=self.ident_bf[0:64, 0:64])
            return r
        self.op("pe", f, reads=["Ub", "ident_bf"], writes=["psb2"])
        hg = self.hg[l][m]
        self.op("dve", lambda e: e.scalar_tensor_tensor(out=T["ysc"], in0=v, scalar=hg[:, 0:1],
                                                        in1=T["gt"][:, :, c0:c0 + L], op0=ALU.mult, op1=ALU.mult),
                reads=["psb2", "gt", "hg%d_%d" % (l, m)], writes=["ysc"])
        self.dma("sp", self.YS[m, :, :, t0:t0 + L].rearrange("h p t -> p h t"), T["ysc"], reads=["ysc"],
                 writes=[(("YS", m), gtt)], allow_slow_non_contiguous=True)

    def out_chunk(self, l, m, d, c0, t0):
        T = self.T
        if d == 0:
            self.dma("sp", self.OFs[t0:t0 + L, :].rearrange("p (h e) -> p h e", h=8), T["o"], reads=["o_sb"],
                     writes=[("OF", t0)])
        else:
            self.finalize(l, m, c0, t0)

    def scan(self, l, m):
        T = self.T
        if self.hg[l][m] is None:
            self.hg[l][m] = self.load_vecT("hg%d_%d" % (l, m), self.hng[l, m], 1)
        for d in (0, 1):
            self.dir_setup(l, m, d)
            cur_tt = None
            for si, (s0, sl) in enumerate(SEGS):
                self.state_init(l, m, d, si)
                chunks = list(range(s0, s0 + sl, L))
                if d == 1:
                    chunks = chunks[::-1]
                for t0 in chunks:
                    gtt = t0 // TT
                    if gtt != cur_tt:
                        self.load_tt(m, gtt, d == 1)
                        cur_tt = gtt
                    c0 = t0 - gtt * TT
                    (self.step_mlstm, self.step_delta, self.step_ret)[m](l, d, c0, t0)
                if si < 2:
                    self.state_out(l, m, d, si)

    def dir_setup(self, l, m, d):
        T = self.T
        M = self.C("LE" if d == 0 else "GE")
        self._M = M
        self._Tri = M
        if m == 2:
            self.dma("sp", T["lgb"], self.lgam[l:l + 1, 8 * d:8 * d + 8].partition_broadcast(128), writes=["lgb"])
            p1 = self.C("p1f" if d == 0 else "p1b")
            p2 = self.C("p2f" if d == 0 else "p2b")
            self.op("act", _act(AF.Exp, T["rsc"], T["lgb"][0:64, :], scale=p1), reads=["lgb", "cst"], writes=["rsc"])
            self.op("act", _act(AF.Exp, T["ksc"], T["lgb"][0:64, :], scale=p2), reads=["lgb", "cst"], writes=["ksc"])
            self.op("act", _act(AF.Exp, T["g64"], T["lgb"], scale=64.0), reads=["lgb"], writes=["g64"])
            self.op("dve", lambda e: e.reciprocal(out=T["sa"], in_=T["rsc"]), reads=["rsc"], writes=["sa"])
            self.op("dve", lambda e: e.tensor_tensor(out=T["MT"], in0=self.bc(T["sa"], 64), in1=self.bm(M), op=ALU.mult),
                    reads=["sa", "cst"], writes=["MT"])

    def state_init(self, l, m, d, si):
        T = self.T
        if si < 2:
            self.op("dve", lambda e: e.memset(T["S"], 0.0), writes=["S"])
            self.op("dve", lambda e: e.memset(T["nS"], 0.0), writes=["nS"])
        else:
            src = (self.sC, self.sD, self.sR)[m]
            self.dma("sp", T["S"], src[l, d].rearrange("h k e -> k h e"), writes=["S"])
            if m == 0:
                self.dma("sp", T["nS"][:, :, 0], self.sn[l, d].rearrange("h k -> k h"), writes=["nS"],
                         allow_slow_non_contiguous=True)
                self.dma("sp", T["nS"][:, :, 1], self.sn[l, d].rearrange("h k -> k h"), writes=["nS"],
                         allow_slow_non_contiguous=True)
                self.dma("sp", T["m0b"], self.sm[l:l + 1, 8 * d:8 * d + 8].partition_broadcast(128), writes=["m0b"])
                self.op("act", _act(AF.Exp, T["m0b"], T["m0b"]), reads=["m0b"], writes=["m0b"])
                self.op("dve", lambda e: e.tensor_tensor(out=T["S"], in0=T["S"], in1=self.bc(T["m0b"], 128), op=ALU.mult),
                        reads=["S", "m0b"], writes=["S"])
                self.op("dve", lambda e: e.tensor_tensor(out=T["nS"], in0=T["nS"], in1=self.bc(T["m0b"], 2), op=ALU.mult),
                        reads=["nS", "m0b"], writes=["nS"])
        self.op("act", lambda e: e.copy(out=T["Sb"], in_=T["S"]), reads=["S"], writes=["Sb"])
        self.op("act", lambda e: e.copy(out=T["nb"], in_=T["nS"]), reads=["nS"], writes=["nb"])

    def state_out(self, l, m, d, si):
        T = self.T
        if m == 0:
            self.mlstm_state_out(l, d, si)
            return
        dst = (None, self.oD, self.oR)[m]
        self.dma("sp", dst[si, l, d].rearrange("h k e -> k h e"), T["S"], reads=["S"], writes=[("ost", m, si, l, d)])

    def step_ret(self, l, d, c0, t0):
        T = self.T
        self.mm8([0], 64, 64, lambda e, h, o: e.matmul(o, T["kT"][:, h, c0:c0 + L], T["qT"][:, h, c0:c0 + L],
                                                       start=True, stop=True), ["kT", "qT"])
        self.evac8("dve", [0], 64, 64, lambda e, v, hs: e.tensor_tensor(out=T["ATb"], in0=v, in1=T["MT"], op=ALU.mult),
                   ["MT"], ["ATb"])
        self.kv_tm(c0)
        self.scale_k("khat", "ksc", T["ksc"])

        def o_mm(e, h, o):
            e.matmul(o, T["qT"][:, h, c0:c0 + L], T["Sb"][:, h, :], start=True, stop=False)
            return e.matmul(o, T["ATb"][:, h, :], T["vtm"][:, h, :], start=False, stop=True)
        self.mm8([4, 5], 64, 128, o_mm, ["qT", "Sb", "ATb", "vtm"])
        self.evac8("dve", [4, 5], 64, 128, lambda e, v, hs: e.tensor_tensor(
            out=T["o"][:, hs, :], in0=v, in1=self.bc(T["rsc"][:, hs], 128), op=ALU.mult), ["rsc"], ["o_sb"])
        self.out_chunk(l, 2, d, c0, t0)
        self.mm8([6, 7], 128, 128, lambda e, h, o: e.matmul(o, T["khat"][:, h, :], T["vtm"][:, h, :], start=True,
                                                           stop=True), ["khat", "vtm"])
        self.state_update(None, "g64", T["g64"])

    def step_mlstm(self, l, d, c0, t0):
        T = self.T
        g = T["gtm"]
        self.gate_tm("gi", "gf", c0)
        ig, lf = g[:, 8 * d:8 * d + 8], g[:, 16 + 8 * d:16 + 8 * d + 8]
        v1 = self.psb[1]
        Tri = self._Tri
        Msk = self._M

        def f(e):
            e.matmul(v1[0:64, 32:40], Tri, lf, start=True, stop=True)
            return e.matmul(v1[0:128, 40:48], self.ones_f[0:64, 0:128], lf, start=True, stop=True)
        self.op("pe", f, reads=["gtm", "cst", "ones_f"], writes=["psb1"])
        self.op("dve", lambda e: e.tensor_tensor(out=T["sa"], in0=ig, in1=v1[0:64, 32:40], op=ALU.subtract),
                reads=["gtm", "psb1"], writes=["sa"])
        self.op("act", _act(AF.Exp, T["sa"], T["sa"]), reads=["sa"], writes=["sa"])
        self.op("act", _act(AF.Exp, T["sb_"], v1[0:64, 32:40], scale=-1.0), reads=["psb1"], writes=["sb_"])
        self.op("act", _act(AF.Exp, T["eL"], v1[0:128, 40:48]), reads=["psb1"], writes=["eL"])
        self.op("dve", lambda e: e.tensor_tensor(out=T["sc_"], in0=T["sa"], in1=T["eL"][0:64, :], op=ALU.mult),
                reads=["sa", "eL"], writes=["sc_"])
        self.mm8([0], 64, 64, lambda e, h, o: e.matmul(o, T["kT"][:, h, c0:c0 + L], T["qT"][:, h, c0:c0 + L],
                                                       start=True, stop=True), ["kT", "qT"])
        self.evac8("dve", [0], 64, 64, lambda e, v, hs: e.tensor_tensor(out=T["t64"], in0=v, in1=self.bc(T["sa"], 64),
                                                                        op=ALU.mult), ["sa"], ["t64"])
        self.op("dve", lambda e: e.tensor_tensor(out=T["ATb"], in0=T["t64"], in1=self.bm(Msk), op=ALU.mult),
                reads=["t64", "cst"], writes=["ATb"])
        self.kv_tm(c0)
        self.scale_k("khat", "sc_", T["sc_"])

        def o_mm(e, h, o):
            e.matmul(o, T["qT"][:, h, c0:c0 + L], T["Sb"][:, h, :], start=True, stop=False)
            return e.matmul(o, T["ATb"][:, h, :], T["vtm"][:, h, :], start=False, stop=True)
        self.mm8([4, 5], 64, 128, o_mm, ["qT", "Sb", "ATb", "vtm"])
        den = v1[0:64, 64:80].rearrange("p (h t) -> p h t", h=8)

        def d_mm(e):
            for h in range(8):
                e.matmul(den[:, h, :], T["qT"][:, h, c0:c0 + L], T["nb"][:, h, :], start=True, stop=False)
                r = e.matmul(den[:, h, :], T["ATb"][:, h, :], self.ones_bf[0:64, 0:2], start=False, stop=True)
            return r
        self.op("pe", d_mm, reads=["qT", "nb", "ATb", "ones_bf"], writes=["psb1"])
        self.op("dve", lambda e: e.tensor_scalar(out=T["sd_"], in0=den[:, :, 0], scalar1=-1.0, scalar2=None, op0=ALU.mult),
                reads=["psb1"], writes=["sd_"])
        self.op("dve", lambda e: e.tensor_tensor(out=T["sd_"], in0=T["sd_"], in1=den[:, :, 0], op=ALU.max),
                reads=["psb1", "sd_"], writes=["sd_"])
        self.op("dve", lambda e: e.tensor_tensor(out=T["sd_"], in0=T["sd_"], in1=T["sb_"], op=ALU.max),
                reads=["sd_", "sb_"], writes=["sd_"])
        self.op("dve", lambda e: e.reciprocal(out=T["sd_"], in_=T["sd_"]), reads=["sd_"], writes=["sd_"])
        self.evac8("dve", [4, 5], 64, 128, lambda e, v, hs: e.tensor_tensor(
            out=T["o"][:, hs, :], in0=v, in1=self.bc(T["sd_"][:, hs], 128), op=ALU.mult), ["sd_"], ["o_sb"])
        self.out_chunk(l, 0, d, c0, t0)
        self.mm8([6, 7], 128, 128, lambda e, h, o: e.matmul(o, T["khat"][:, h, :], T["vtm"][:, h, :], start=True,
                                                           stop=True), ["khat", "vtm"])
        dn = v1[0:128, 96:112].rearrange("p (h t) -> p h t", h=8)

        def n_mm(e):
            for h in range(8):
                r = e.matmul(dn[:, h, :], T["khat"][:, h, :], self.ones_bf[0:64, 0:2], start=True, stop=True)
            return r
        self.op("pe", n_mm, reads=["khat", "ones_bf"], writes=["psb1"])
        self.op("dve", lambda e: e.tensor_tensor(out=T["nS"], in0=T["nS"], in1=self.bc(T["eL"], 2), op=ALU.mult),
                reads=["nS", "eL"], writes=["nS"])
        self.op("dve", lambda e: e.tensor_tensor(out=T["nS"], in0=T["nS"], in1=dn, op=ALU.add), reads=["nS", "psb1"],
                writes=["nS"])
        self.op("act", lambda e: e.copy(out=T["nb"], in_=T["nS"]), reads=["nS"], writes=["nb"])
        self.state_update(None, "eL", T["eL"])

    def mlstm_state_out(self, l, d, si):
        T = self.T
        s0 = SEGS[si][0]
        gi, gf = T["gi"], T["gf"]
        n = 256
        P_ = self.tmpf[0][0:16, 0:n]
        E_ = self.tmpf[1][0:16, 0:n]
        tot = T["se_"][0:16, 0:1]
        mx = T["se_"][0:16, 1:2]
        rd = ["gi", "gf"]
        lfv, igv = gf[:, s0:s0 + n], gi[:, s0:s0 + n]
        self.op("dve", lambda e: e.tensor_tensor_scan(out=P_, data0=self.ones_f[0:16, 0:n], data1=lfv, initial=0.0,
                                                      op0=ALU.mult, op1=ALU.add), reads=rd + ["ones_f"], writes=["tmpf0"])
        self.op("dve", lambda e: e.tensor_copy(out=tot, in_=P_[:, n - 1:n]), reads=["tmpf0"], writes=["se_"])
        if d == 0:
            self.op("dve", lambda e: e.tensor_tensor(out=E_, in0=igv, in1=P_, op=ALU.subtract), reads=rd + ["tmpf0"],
                    writes=["tmpf1"])
            self.op("dve", lambda e: e.tensor_scalar(out=E_, in0=E_, scalar1=tot, scalar2=None, op0=ALU.add),
                    reads=["tmpf1", "se_"], writes=["tmpf1"])
        else:
            self.op("dve", lambda e: e.tensor_tensor(out=E_, in0=igv, in1=P_, op=ALU.add), reads=rd + ["tmpf0"],
                    writes=["tmpf1"])
            self.op("dve", lambda e: e.tensor_tensor(out=E_, in0=E_, in1=lfv, op=ALU.subtract), reads=rd + ["tmpf1"],
                    writes=["tmpf1"])
        self.op("dve", lambda e: e.tensor_reduce(out=mx, in_=E_, op=ALU.max, axis=AX.X), reads=["tmpf1"], writes=["se_"])
        self.op("dve", lambda e: e.tensor_tensor(out=mx, in0=mx, in1=tot, op=ALU.max), reads=["se_"], writes=["se_"])
        self.dma("sp", self.om[si, l, :].rearrange("(p o) -> p o", o=1)[8 * d:8 * d + 8, :], T["se_"][8 * d:8 * d + 8, 1:2],
                 reads=["se_"], writes=[("om", si, l, d)], allow_slow_non_contiguous=True)
        mrep = self.tmpf[2][0:16, 0:128]
        self.op("dve", lambda e: e.tensor_scalar(out=mrep, in0=self.ones_f[0:16, 0:128], scalar1=mx, scalar2=None, op0=ALU.mult),
                reads=["se_", "ones_f"], writes=["tmpf2"])
        v1 = self.psb[1]
        self.op("pe", lambda e: e.matmul(v1[0:128, 0:16], mrep, self.ident[0:16, 0:16], start=True, stop=True),
                reads=["tmpf2", "ident"], writes=["psb1"])
        self.op("act", _act(AF.Exp, T["m0b"], v1[0:128, 8 * d:8 * d + 8], scale=-1.0), reads=["psb1"], writes=["m0b"])
        self.op("dve", lambda e: e.tensor_tensor(out=T["S"], in0=T["S"], in1=self.bc(T["m0b"], 128), op=ALU.mult),
                reads=["S", "m0b"], writes=["S"])
        self.op("dve", lambda e: e.tensor_tensor(out=T["nS"], in0=T["nS"], in1=self.bc(T["m0b"], 2), op=ALU.mult),
                reads=["nS", "m0b"], writes=["nS"])
        self.dma("sp", self.oC[si, l, d].rearrange("h k e -> k h e"), T["S"], reads=["S"], writes=[("oC", si, l, d)])
        self.dma("sp", self.on[si, l, d].rearrange("h k -> k h"), T["nS"][:, :, 0], reads=["nS"], writes=[("on", si, l, d)],
                 allow_slow_non_contiguous=True)

    def step_delta(self, l, d, c0, t0):
        T = self.T
        g = T["gtm"]
        self.gate_tm("gb", "gg", c0)
        be, gg = g[:, 8 * d:8 * d + 8], g[:, 16 + 8 * d:16 + 8 * d + 8]
        v1 = self.psb[1]
        Tri = self._Tri
        INC = self.C("GE" if d == 0 else "LE")
        STR = self.C("GT" if d == 0 else "LT")
        self.op("dve", lambda e: e.tensor_tensor(out=T["t64"], in0=self.bc(gg, 64), in1=self.bm(STR), op=ALU.mult),
                reads=["gtm", "cst"], writes=["t64"])

        def f(e):
            e.matmul(v1[0:64, 32:40], Tri, gg, start=True, stop=True)
            return e.matmul(v1[0:128, 40:48], self.ones_f[0:64, 0:128], gg, start=True, stop=True)
        self.op("pe", f, reads=["gtm", "cst", "ones_f"], writes=["psb1"])
        self.op("dve", lambda e: e.tensor_copy(out=T["sa"], in_=v1[0:64, 32:40]), reads=["psb1"], writes=["sa"])
        self.op("act", _act(AF.Exp, T["sb_"], T["sa"]), reads=["sa"], writes=["sb_"])
        self.op("dve", lambda e: e.tensor_tensor(out=T["sc_"], in0=v1[0:64, 40:48], in1=T["sa"], op=ALU.subtract),
                reads=["psb1", "sa"], writes=["sc_"])
        self.op("act", _act(AF.Exp, T["sc_"], T["sc_"]), reads=["sc_"], writes=["sc_"])
        self.op("act", _act(AF.Exp, T["eL"], v1[0:128, 40:48]), reads=["psb1"], writes=["eL"])
        self.op("dve", lambda e: e.tensor_tensor(out=T["sd_"], in0=be, in1=T["sb_"], op=ALU.mult), reads=["gtm", "sb_"],
                writes=["sd_"])
        b0 = self.bank(0, 64, [8, 64])
        self.op("pe", lambda e: e.matmul(self.psb[0][0:64, :], Tri, T["t64"].rearrange("p h m -> p (h m)"), start=True,
                                         stop=True), reads=["t64", "cst"], writes=["psb0"])
        self.op("act", _act(AF.Exp, T["dec"], b0), reads=["psb0"], writes=["dec"])
        self.op("dve", lambda e: e.tensor_tensor(out=T["t64"], in0=T["dec"], in1=self.bm(STR), op=ALU.mult),
                reads=["dec", "cst"], writes=["t64"])
        self.op("dve", lambda e: e.tensor_tensor(out=T["dec"], in0=T["dec"], in1=self.bm(INC), op=ALU.mult),
                reads=["dec", "cst"], writes=["dec"])
        self.op("dve", lambda e: e.tensor_tensor(out=T["t64"], in0=T["t64"], in1=self.bc(be, 64), op=ALU.mult),
                reads=["t64", "gtm"], writes=["t64"])
        self.mm8([0], 64, 64, lambda e, h, o: e.matmul(o, T["kT"][:, h, c0:c0 + L], T["kT"][:, h, c0:c0 + L],
                                                       start=True, stop=True), ["kT"])
        self.evac8("dve", [0], 64, 64, lambda e, v, hs: e.tensor_tensor(out=T["A"], in0=v, in1=T["t64"], op=ALU.mult),
                   ["t64"], ["A"])
        self.mm8([1], 64, 64, lambda e, h, o: e.matmul(o, T["qT"][:, h, c0:c0 + L], T["kT"][:, h, c0:c0 + L],
                                                       start=True, stop=True), ["qT", "kT"])
        self.evac8("dve", [1], 64, 64, lambda e, v, hs: e.tensor_tensor(out=T["qk"], in0=v, in1=T["dec"], op=ALU.mult),
                   ["dec"], ["qk"])
        i64 = self.ident[0:64, 0:64]
        for src, bk_, dst, eng in (("A", 0, "AT", "dve"), ("qk", 1, "qkTb", "act")):
            vb = self.bank(bk_, 64, [8, 64])

            def tr(e, src=src, vb=vb):
                for h in range(8):
                    r = e.transpose(out=vb[:, h, :], in_=T[src][:, h, :], identity=i64)
                return r
            self.op("pe", tr, reads=[src, "ident"], writes=["psb%d" % bk_])
            if eng == "dve":
                self.op("dve", lambda e, vb=vb, dst=dst: e.tensor_copy(out=T[dst], in_=vb), reads=["psb%d" % bk_],
                        writes=[dst])
            else:
                self.op("act", lambda e, vb=vb, dst=dst: e.copy(out=T[dst], in_=vb), reads=["psb%d" % bk_], writes=[dst])
        I8 = self.bm(i64)
        self.op("dve", lambda e: e.tensor_tensor(out=T["W"], in0=T["A"], in1=self.bm(self.C("BM0")), op=ALU.mult),
                reads=["A", "cst"], writes=["W"])
        self.op("dve", lambda e: e.tensor_tensor(out=T["T"], in0=I8, in1=T["W"], op=ALU.subtract), reads=["W", "ident"],
                writes=["T"])
        self.op("dve", lambda e: e.tensor_tensor(out=T["W"], in0=T["AT"], in1=self.bm(self.C("BM0")), op=ALU.mult),
                reads=["AT", "cst"], writes=["W"])
        self.op("dve", lambda e: e.tensor_tensor(out=T["TT_"], in0=I8, in1=T["W"], op=ALU.subtract), reads=["W", "ident"],
                writes=["TT_"])
        for k in range(1, 6):
            self.op("dve", lambda e, k=k: e.tensor_tensor(out=T["OkT"], in0=T["AT"], in1=self.bm(self.C("BM%d" % k)),
                                                          op=ALU.mult), reads=["AT", "cst"], writes=["OkT"])
            self.mm8([0], 64, 64, lambda e, h, o: e.matmul(o, T["OkT"][:, h, :], T["T"][:, h, :], start=True, stop=True),
                     ["OkT", "T"])
            self.evac8("act", [0], 64, 64, lambda e, v, hs: e.copy(out=T["W"], in_=v), [], ["W"])
            if k < 5:
                self.mm8([1], 64, 64, lambda e, h, o: e.matmul(o, T["TT_"][:, h, :], T["W"][:, h, :], start=True,
                                                               stop=True), ["TT_", "W"])
            self.mm8([3], 64, 64, lambda e, h, o: e.matmul(o, T["W"][:, h, :], T["TT_"][:, h, :], start=True, stop=True),
                     ["TT_", "W"])
            if k < 5:
                self.evac8("dve", [1], 64, 64, lambda e, v, hs: e.tensor_tensor(out=T["T"], in0=T["T"], in1=v,
                                                                                op=ALU.subtract), ["T"], ["T"])
            self.evac8("dve", [3], 64, 64, lambda e, v, hs: e.tensor_tensor(out=T["TT_"], in0=T["TT_"], in1=v,
                                                                            op=ALU.subtract), ["TT_"], ["TT_"])
        self.op("act", lambda e: e.copy(out=T["TTb"], in_=T["TT_"]), reads=["TT_"], writes=["TTb"])
        self.kv_tm(c0)
        self.scale_k("bk", "sd_", T["sd_"])
        self.scale_k("kend", "sc_", T["sc_"])
        self.op("dve", lambda e: e.tensor_tensor(out=T["bv"], in0=T["vtm"], in1=self.bc(be, 128), op=ALU.mult),
                reads=["vtm", "gtm"], writes=["bv"])
        self.mm8([4, 5], 64, 128, lambda e, h, o: e.matmul(o, T["TTb"][:, h, :], T["bv"][:, h, :], start=True, stop=True),
                 ["TTb", "bv"])
        self.evac8("act", [4, 5], 64, 128, lambda e, v, hs: e.copy(out=T["U0"][:, hs, :], in_=v), [], ["of_sb"])
        self.mm8([2], 128, 64, lambda e, h, o: e.matmul(o, T["bk"][:, h, :], T["TTb"][:, h, :], start=True, stop=True),
                 ["TTb", "bk"])
        self.evac8("act", [2], 128, 64, lambda e, v, hs: e.copy(out=T["WkT"], in_=v), [], ["WkT"])
        self.mm8([6, 7], 64, 128, lambda e, h, o: e.matmul(o, T["WkT"][:, h, :], T["Sb"][:, h, :], start=True, stop=True),
                 ["WkT", "Sb"])
        self.evac8("dve", [6, 7], 64, 128, lambda e, v, hs: e.tensor_tensor(out=T["Ub"][:, hs, :], in0=T["U0"][:, hs, :],
                                                                             in1=v, op=ALU.subtract), ["of_sb"], ["Ub"])
        self.mm8([4, 5], 64, 128, lambda e, h, o: e.matmul(o, T["qT"][:, h, c0:c0 + L], T["Sb"][:, h, :], start=True,
                                                          stop=True), ["qT", "Sb"])
        self.evac8("dve", [4, 5], 64, 128, lambda e, v, hs: e.tensor_tensor(
            out=T["o"][:, hs, :], in0=v, in1=self.bc(T["sb_"][:, hs], 128), op=ALU.mult), ["sb_"], ["o_sb"])
        self.mm8([6, 7], 64, 128, lambda e, h, o: e.matmul(o, T["qkTb"][:, h, :], T["Ub"][:, h, :], start=True, stop=True),
                 ["qkTb", "Ub"])
        self.evac8("dve", [6, 7], 64, 128, lambda e, v, hs: e.tensor_tensor(out=T["o"][:, hs, :], in0=T["o"][:, hs, :],
                                                                             in1=v, op=ALU.add), ["o_sb"], ["o_sb"])
        self.mm8([6, 7], 128, 128, lambda e, h, o: e.matmul(o, T["kend"][:, h, :], T["Ub"][:, h, :], start=True, stop=True),
                 ["kend", "Ub"])
        self.state_update(None, "eL", T["eL"])
        self.out_chunk(l, 1, d, c0, t0)

    def mixer(self, l):
        if l == 0:
            self.mixer_setup()
        self.mixer_params(l)
        for tg in range(NTG):
            self.norm_to_hT(tg, l, 1)
            self.m1_project(l, tg)
        self.conv_pass(l)
        self.scan_setup()
        for m in range(3):
            self.scan(l, m)
        for tg in range(NTG):
            self.m3_merge(l, tg)


_CACHE = {}


def _get_nc(cfg_key):
    if cfg_key not in _CACHE:
        k = Kern(dict(cfg_key))
        _CACHE[cfg_key] = k.build()
    return _CACHE[cfg_key]


def kernel(x_prompt, x_sample, state_mlstm_C, state_mlstm_n, state_mlstm_m, state_delta_S, state_ret_S,
           c, c_ctx, norm_g, final_norm_g, w_ada, b_ada, w_in, b_in, mlstm_f_bias, conv_w, delta_A_log,
           delta_dt_bias, ret_log_gamma, head_norm_g, w_br, w_out, ffn_w13, ffn_w2, _cfg=None):
    cfg = dict(_cfg or {})
    nc = _get_nc(tuple(sorted(cfg.items())))
    f = lambda a: np.ascontiguousarray(np.asarray(a, dtype=np.float32))
    in_maps = []
    ident = np.eye(128, dtype=np.float32)
    full = cfg.get("mixer", True)
    rc, rs = _rope_np()
    for cidx in range(8):
        b = cidx // 4
        x_tok = np.concatenate([x_prompt[2 * cidx], x_prompt[2 * cidx + 1], x_sample[b]], axis=0)
        cvec = np.stack([c_ctx, c[b]], axis=0)
        m = {
            "x_tok": f(x_tok), "cvec": f(cvec), "norm_g": f(norm_g), "final_norm_g": f(final_norm_g),
            "w_ada": f(w_ada), "b_ada": f(b_ada), "w_in": f(w_in), "b_in": f(b_in), "w_br": f(w_br),
            "w_out": f(w_out), "ffn_w13": f(ffn_w13), "ffn_w2": f(ffn_w2), "ident": ident,
        }
        if full:
            m.update({
                "cst": _CST_NP, "ropeC": rc, "ropeS": rs,
                "mlstm_f_bias": f(mlstm_f_bias).reshape(DEPTH, 16), "conv_w": f(conv_w),
                "delta_A_log": f(delta_A_log).reshape(DEPTH, 16), "delta_dt_bias": f(delta_dt_bias).reshape(DEPTH, 16),
                "ret_log_gamma": f(ret_log_gamma).reshape(DEPTH, 16), "head_norm_g": f(head_norm_g),
                "st_C": f(state_mlstm_C[b]), "st_n": f(state_mlstm_n[b]), "st_m": f(state_mlstm_m[b]).reshape(DEPTH, 16),
                "st_D": f(state_delta_S[b]), "st_R": f(state_ret_S[b]),
            })
        in_maps.append(m)
    res = run_bass_kernel_spmd(nc, in_maps, core_ids=list(range(8)))
    outs = res.results
    y_prompt = np.zeros((16, 256, D), np.float32)
    y_sample = np.zeros((2, 4096, D), np.float32)
    z = lambda *s: np.zeros(s, np.float32)
    nC, nn, nm, nD, nR = z(16, 2, 2, 8, 128, 128), z(16, 2, 2, 8, 128), z(16, 2, 2, 8), z(16, 2, 2, 8, 128, 128), z(16, 2, 2, 8, 128, 128)
    for cidx in range(8):
        y = outs[cidx]["y_tok"]
        y_prompt[2 * cidx] = y[0:256]
        y_prompt[2 * cidx + 1] = y[256:512]
        if cidx % 4 == 0:
            y_sample[cidx // 4] = y[512:]
        if full:
            for si in range(2):
                nC[2 * cidx + si] = outs[cidx]["o_C"][si]
                nn[2 * cidx + si] = outs[cidx]["o_n"][si]
                nm[2 * cidx + si] = outs[cidx]["o_m"][si].reshape(DEPTH, 2, 8)
                nD[2 * cidx + si] = outs[cidx]["o_D"][si]
                nR[2 * cidx + si] = outs[cidx]["o_R"][si]
    return (y_prompt, y_sample, nC, nn, nm, nD, nR)
```

```python
import numpy as np
from contextlib import ExitStack
import concourse.bass as bass
import concourse.mybir as mybir
from concourse.bass_utils import run_bass_kernel_spmd

F32 = mybir.dt.float32
BF16 = mybir.dt.bfloat16
AF = mybir.ActivationFunctionType
ALU = mybir.AluOpType
AX = mybir.AxisListType

D = 2048
KC = 16
DEPTH = 2
HD = 128
NH = 8
WM = 1024
DFF = 4096
NMOD = 9
EPS = 1e-6
L = 64
N_IN = 18496
OFF = dict(qA=0, kA=1024, vA=2048, oA=3072, iA=4096, fA=4112, qkvB=4128, zB=7200, betaB=8224, aB=8240,
           qC=8256, kC=9280, vC=10304, gC=11328, gm=12352)
SEGS = [(0, 256), (256, 256), (512, 4096)]
TC = 4608
TT = 512
NTT = TC // TT
TG = 1536
NTG = TC // TG


def _build_cst():
    r = np.arange(64)
    LE = (r[:, None] <= r[None, :]).astype(np.float32)
    M = {"LE": LE, "GE": LE.T.copy(), "LT": (r[:, None] < r[None, :]).astype(np.float32),
         "GT": (r[:, None] > r[None, :]).astype(np.float32)}
    for k in range(6):
        M["BM%d" % k] = ((r[:, None] >> (k + 1) == r[None, :] >> (k + 1)) & (r[:, None] >> k != r[None, :] >> k)).astype(np.float32)
    cols = {}
    arr = []
    off = 0
    for n, m in M.items():
        a = np.zeros((128, 64), np.float32); a[:64] = m
        arr.append(a); cols[n] = (off, 64); off += 64
    for n, v in (("p1f", r + 1.0), ("p1b", 64.0 - r), ("p2f", 63.0 - r), ("p2b", r * 1.0)):
        a = np.zeros((128, 1), np.float32); a[:64, 0] = v
        arr.append(a); cols[n] = (off, 1); off += 1
    Rm = np.zeros((128, 128), np.float32)
    for dp in range(64):
        Rm[dp + 64, dp] = -1.0
        Rm[dp, dp + 64] = 1.0
    arr.append(Rm); cols["Rm"] = (off, 128); off += 128
    return np.concatenate(arr, axis=1), cols


_CST_NP, CST = _build_cst()
NCST = _CST_NP.shape[1]


def _rope_np():
    T = 4096
    pos_r = np.repeat(np.arange(T // 64), 64).astype(np.float32)
    pos_c = np.tile(np.arange(64), T // 64).astype(np.float32)
    nf = HD // 4
    freqs = (10000.0 ** (-np.arange(nf, dtype=np.float32) / nf)).astype(np.float32)
    ang = np.concatenate([pos_r[:, None] * freqs, pos_c[:, None] * freqs], axis=-1)
    ang = np.concatenate([ang, ang], axis=-1)
    return np.ascontiguousarray(np.cos(ang).T.astype(np.float32)), np.ascontiguousarray(np.sin(ang).T.astype(np.float32))


class Res:
    __slots__ = ("name", "w", "rs", "parent", "kids")

    def __init__(self, name):
        self.name = name
        self.w = None
        self.rs = []
        self.parent = None
        self.kids = []


class Sched:
    ENGS = ("pe", "act", "dve", "pool", "sp")

    def __init__(self, nc, es):
        self.nc = nc
        self.es = es
        self.ops = []
        self.per_eng = {e: [] for e in self.ENGS}

    def _deps(self, reads, writes, idx):
        deps = set()
        for r in reads:
            if r.w is not None:
                deps.add(r.w)
            if r.parent is not None and r.parent.w is not None:
                deps.add(r.parent.w)
            for k in r.kids:
                if k.w is not None:
                    deps.add(k.w)
        for w in writes:
            rel = [w] + w.kids + ([w.parent] if w.parent is not None else [])
            for x in rel:
                if x.w is not None:
                    deps.add(x.w)
                deps.update(x.rs)
        for r in reads:
            r.rs.append(idx)
        for w in writes:
            w.w = idx
            w.rs = []
        deps.discard(idx)
        return deps

    def op(self, eng, fn, reads=(), writes=()):
        idx = len(self.ops)
        deps = self._deps(reads, writes, idx)
        self.ops.append((eng, fn, deps, False))
        return idx

    def dma(self, eng, fn, reads=(), writes=()):
        idx = len(self.ops)
        deps = self._deps(reads, writes, idx)
        self.ops.append((eng, fn, deps, True))
        return idx

    def emit(self, final_waits=True):
        nc, es = self.nc, self.es
        NDS = {"sp": 24, "pool": 12, "act": 4}
        sem = {e: es.enter_context(nc.semaphore("s_" + e)) for e in ("pe", "act", "dve", "pool")}
        dsem = {q: [es.enter_context(nc.semaphore("d_%s%d" % (q, i))) for i in range(n)] for q, n in NDS.items()}
        cnt = {e: 0 for e in sem}
        dcnt = {q: [0] * n for q, n in NDS.items()}
        dnext = {q: 0 for q in NDS}
        handle = {}
        known = {e: {} for e in self.ENGS}
        prog = {e: [] for e in self.ENGS}
        last_dma = {}

        def need(eng, s, v):
            k = known[eng]
            if k.get(id(s), 0) < v:
                k[id(s)] = v
                prog[eng].append(("wait", s, v))

        for idx, (eng, fn, deps, is_dma) in enumerate(self.ops):
            for d in sorted(deps):
                deng = self.ops[d][0]
                if (not self.ops[d][3]) and deng == eng and eng == "pe":
                    continue
                s, v = handle[d]
                need(eng, s, v)
            if is_dma:
                q = eng
                slot = dnext[q]
                dnext[q] = (slot + 1) % len(dsem[q])
                s = dsem[q][slot]
                if dcnt[q][slot] > 0:
                    need(eng, s, 16 * dcnt[q][slot])
                dcnt[q][slot] += 1
                v = 16 * dcnt[q][slot]
                handle[idx] = (s, v)
                prog[eng].append(("dma", fn, s))
                last_dma[(q, slot)] = (s, v)
            else:
                cnt[eng] += 1
                handle[idx] = (sem[eng], cnt[eng])
                prog[eng].append(("op", fn, sem[eng]))
        for (q, slot), (s, v) in last_dma.items():
            need("sp", s, v)
        for e in ("pe", "act", "dve", "pool"):
            if cnt[e]:
                need("sp", sem[e], cnt[e])

        block = es.enter_context(nc.Block())

        def run(engobj, items):
            for it in items:
                if it[0] == "wait":
                    engobj.wait_ge(it[1], it[2])
                elif it[0] == "dma":
                    it[1](engobj).then_inc(it[2], 16)
                else:
                    it[1](engobj).then_inc(it[2], 1)

        @block.sync
        def _(e):
            run(e, prog["sp"])

        @block.gpsimd
        def _(e):
            run(e, prog["pool"])

        @block.scalar
        def _(e):
            run(e, prog["act"])

        @block.vector
        def _(e):
            run(e, prog["dve"])

        @block.tensor
        def _(e):
            run(e, prog["pe"])


class Builder:
    def __init__(self, cfg):
        self.cfg = cfg
        self.nc = bass.Bass("TRN2", target_bir_lowering=False)
        self.es = ExitStack()
        self.S = Sched(self.nc, self.es)
        self.res = {}
        self.tiles = {}

    def R(self, key):
        r = self.res.get(key)
        if r is None:
            r = self.res[key] = Res(str(key))
        return r

    def alias(self, child, parent):
        c, p = self.R(child), self.R(parent)
        c.parent = p
        p.kids.append(c)

    def sb(self, name, shape, dt=F32):
        t = self.es.enter_context(self.nc.sbuf_tensor(name, list(shape), dt))
        self.tiles[name] = t
        return t

    def ps(self, name, shape, dt=F32):
        t = self.es.enter_context(self.nc.psum_tensor(name, list(shape), dt))
        self.tiles[name] = t
        return t

    def dram(self, name, shape, dt=F32, kind="Internal"):
        return self.nc.dram_tensor(name, list(shape), dt, kind=kind).ap()

    def op(self, eng, fn, reads=(), writes=()):
        return self.S.op(eng, fn, [self.R(r) for r in reads], [self.R(w) for w in writes])

    def dma(self, q, out, in_, reads=(), writes=(), **kw):
        return self.S.dma(q, lambda e: e.dma_start(out=out, in_=in_, **kw),
                          [self.R(r) for r in reads], [self.R(w) for w in writes])


def _act(func, out, in_, **kw):
    return lambda e: e.activation(out=out, in_=in_, func=func, **kw)


class Kern(Builder):
    def build(self):
        nc = self.nc
        cfg = self.cfg
        di = lambda n, s: nc.dram_tensor(n, list(s), F32, kind="ExternalInput").ap()
        do = lambda n, s: nc.dram_tensor(n, list(s), F32, kind="ExternalOutput").ap()
        self.x_tok = di("x_tok", [TC, D])
        self.cvec = di("cvec", [2, D])
        self.norm_g = di("norm_g", [DEPTH, 3, D])
        self.final_g = di("final_norm_g", [D])
        self.w_ada = di("w_ada", [DEPTH, D, NMOD * D])
        self.b_ada = di("b_ada", [DEPTH, NMOD * D])
        self.w_in = di("w_in", [DEPTH, D, N_IN])
        self.b_in = di("b_in", [DEPTH, N_IN])
        self.w_br = di("w_br", [DEPTH, 3, WM, D])
        self.w_out = di("w_out", [DEPTH, D, D])
        self.w13 = di("ffn_w13", [DEPTH, 2, D, 2 * DFF])
        self.w2 = di("ffn_w2", [DEPTH, 2, DFF, D])
        self.ident_in = di("ident", [128, 128])
        self.y_tok = do("y_tok", [TG, D])
        self.rmask_in = di("rmask", [128, 4])
        self.xT = self.dram("xT", [KC, 128, TC])

        self.ident = self.sb("ident_sb", [128, 128], F32)
        self.dma("sp", self.ident[:], self.ident_in, writes=["ident"])
        self.ones_bf = self.sb("ones_bf", [128, 128], BF16)
        self.op("dve", lambda e: e.memset(self.ones_bf[:], 1.0), writes=["ones_bf"])
        self.eps_t = self.sb("eps_t", [128, 1], F32)
        self.op("dve", lambda e: e.memset(self.eps_t[:], EPS), writes=["eps_t"])

        self.NWR = 4
        self.wr = [self.sb("wr%d" % i, [128, KC, 128], BF16) for i in range(self.NWR)]
        self.wr_i = 0
        self.kx, self.kys, self.kgm = "xT", "YS", "GM"
        self.bg = None
        self.bg_n = 0
        self.NPS = 4
        self.psb = [self.ps("psb%d" % i, [128, 512], F32) for i in range(8)]
        self.ps_i = 0
        hraw = self.sb("hT", [128, KC * TG // 2], F32)
        self.hraw = hraw
        self.hT = hraw[:].bitcast(BF16).rearrange("p (k t) -> p k t", k=KC)

        graw = self.sb("gT", [128, KC * TG // 2], F32)
        self.graw = graw
        gf = graw[:]
        self.gT = graw[:].bitcast(BF16).rearrange("p (k t) -> p k t", k=KC)
        self._xl = [gf[:, i * 2048:(i + 1) * 2048] for i in range(2)]
        self._xo = [gf[:, 4096 + i * 2048:4096 + (i + 1) * 2048].rearrange("p (k t) -> p k t", k=KC) for i in range(2)]
        self._yn = gf[:, 0:8192].rearrange("p (k t) -> p k t", k=KC)
        self._yo = [gf[:, 8192 + i * 2048:8192 + (i + 1) * 2048] for i in range(2)]
        for nm in ("xl0", "xl1", "xo0", "xo1", "yn", "yo0", "yo1"):
            self.alias(nm, "gT")
        self.xq = [self.sb("xq%d" % i, [128, 4, TT], F32) for i in range(3)]
        self.xq_i = 0
        self.sq = [self.sb("sq%d" % i, [128, 4, TT], BF16) for i in range(2)]
        self.rstd = self.sb("rstd", [128, TT], F32)
        self.tmpf = [self.sb("tmpf%d" % i, [128, TT], F32) for i in range(3)]
        self.tmp_i = 0
        self.xr = [self.sb("xr%d" % i, [128, TT], F32) for i in range(4)]
        self.xr_i = 0

        self.phase_load_x()
        self.phase_adaln()
        for l in range(DEPTH):
            for tg in range(NTG):
                self.norm_to_hT(tg, l, 0)
                self.ffn(tg, l, 0)
            last = (l == DEPTH - 1) and cfg.get("mixer", True)
            if cfg.get("mixer", True):
                self.mixer(l, do_m3=not last)
            if last:
                self.select_own()
                self.m3_merge(l, 0)
                self.norm_to_hT(0, l, 2)
                self.ffn(0, l, 1)
            else:
                for tg in range(NTG):
                    self.norm_to_hT(tg, l, 2)
                    self.ffn(tg, l, 1)
        self.phase_final(TG // TT if cfg.get("mixer", True) else TG // TT)
        self.S.emit()
        return nc

    def next_ps(self):
        i = self.ps_i
        self.ps_i = (i + 1) % self.NPS
        return i

    def next_tmp(self):
        i = self.tmp_i
        self.tmp_i = (i + 1) % len(self.tmpf)
        return i

    def phase_load_x(self):
        xl, xo = self._xl, self._xo
        for t in range(TC // 128):
            b = t % 2
            self.dma("sp", xl[b], self.x_tok[t * 128:(t + 1) * 128, :], writes=["xl%d" % b])
            for q in range(4):
                pi = self.next_ps()
                ps = self.psb[pi]

                def tr(e, b=b, q=q, ps=ps):
                    for j in range(4):
                        k = q * 4 + j
                        r = e.transpose(out=ps[:, j * 128:(j + 1) * 128], in_=xl[b][:, k * 128:(k + 1) * 128],
                                        identity=self.ident[:])
                    return r
                self.op("pe", tr, reads=["xl%d" % b, "ident"], writes=["psb%d" % pi])
                self.op("dve", lambda e, b=b, q=q, ps=ps: e.tensor_copy(
                    out=xo[b][:, q * 4:(q + 1) * 4, :], in_=ps[:].rearrange("p (a b) -> p a b", a=4)),
                    reads=["psb%d" % pi], writes=["xo%d" % b])
            self.dma("sp", self.xT[:, :, t * 128:(t + 1) * 128].rearrange("k p t -> p k t"), xo[b],
                     reads=["xo%d" % b], writes=[("xT", t // 4)])

    def load_vecT(self, name, src_1d, nblk):
        t = self.sb(name, [128, nblk], F32)
        self.dma("sp", t[:], src_1d.rearrange("(j p) -> p j", p=128), writes=[name],
                 allow_slow_non_contiguous=True)
        return t

    def phase_adaln(self):
        nc = self.nc
        cT = self.sb("cT", [128, KC, 2], F32)
        for b in range(2):
            self.dma("sp", cT[:, :, b], self.cvec[b].rearrange("(k p) -> p k", p=128), writes=["cT"],
                     allow_slow_non_contiguous=True)
        cS = self.sb("cS", [128, KC, 2], BF16)
        self.op("act", _act(AF.Silu, cS[:], cT[:]), reads=["cT"], writes=["cS"])
        self.mod = []
        self.nsc, self.nbi, self.gsc = {}, {}, {}
        for l in range(DEPTH):
            bT = self.load_vecT("badaT%d" % l, self.b_ada[l], NMOD * KC)
            gT_ = [self.load_vecT("ng%d_%d" % (l, i), self.norm_g[l, i], KC) for i in range(3)]
            mod = self.sb("mod%d" % l, [128, NMOD * KC, 2], F32)
            self.mod.append(mod)

            def evac(blk, tt, ps, pi, mod=mod, bT=bT, l=l):
                self.op("dve", lambda e: e.tensor_scalar(out=mod[:, blk, :], in0=ps[:, 0:2], scalar1=bT[:, blk:blk + 1],
                                                         scalar2=None, op0=ALU.add),
                        reads=["psb%d" % pi, "badaT%d" % l], writes=["mod%d" % l])
            blocks = [(self.w_ada[l][:, j * 128:(j + 1) * 128], KC, 128) for j in range(NMOD * KC)]
            self.linear(blocks, lambda k, tt: cS[:, k, :], ["cS"], [2], evac)
            for i in range(3):
                sc = self.sb("nsc%d_%d" % (l, i), [128, KC, 2], F32)
                bi = self.sb("nbi%d_%d" % (l, i), [128, KC, 2], F32)
                gs = self.sb("gsc%d_%d" % (l, i), [128, KC, 2], F32)
                m0 = mod[:, (3 * i) * KC:(3 * i + 1) * KC, :]
                m1 = mod[:, (3 * i + 1) * KC:(3 * i + 2) * KC, :]
                m2 = mod[:, (3 * i + 2) * KC:(3 * i + 3) * KC, :]
                for b in range(2):
                    self.op("dve", lambda e, sc=sc, m1=m1, b=b, g=gT_[i]: e.scalar_tensor_tensor(
                        out=sc[:, :, b], in0=m1[:, :, b], scalar=1.0, in1=g[:], op0=ALU.add, op1=ALU.mult),
                        reads=["mod%d" % l, "ng%d_%d" % (l, i)], writes=["nsc%d_%d" % (l, i)])
                self.op("dve", lambda e, bi=bi, m0=m0: e.tensor_copy(out=bi[:], in_=m0), reads=["mod%d" % l],
                        writes=["nbi%d_%d" % (l, i)])
                fac = 1.0 if i == 1 else 0.5
                self.op("dve", lambda e, gs=gs, m2=m2, fac=fac: e.tensor_scalar(
                    out=gs[:], in0=m2, scalar1=fac, scalar2=None, op0=ALU.mult), reads=["mod%d" % l],
                    writes=["gsc%d_%d" % (l, i)])
                self.nsc[(l, i)], self.nbi[(l, i)], self.gsc[(l, i)] = sc, bi, gs

    def linear(self, blocks, in_fn, in_res, ntoks, evac, pf=3):
        nb = len(blocks)
        slots = {}

        def issue(j):
            ap, kc, ncols = blocks[j]
            s = self.wr_i
            self.wr_i = (s + 1) % self.NWR
            slots[j] = s
            self.dma("pool", self.wr[s][:, 0:kc, 0:ncols], ap.rearrange("(k p) n -> p k n", p=128),
                     writes=["wr%d" % s])
        for j in range(min(pf, nb)):
            issue(j)
        for j in range(nb):
            if j + pf < nb:
                issue(j + pf)
            ap, kc, ncols = blocks[j]
            s = slots[j]
            for tt, nt in enumerate(ntoks):
                pi = self.next_ps()
                ps = self.psb[pi]

                def mm(e, s=s, kc=kc, ncols=ncols, tt=tt, nt=nt, ps=ps):
                    for k in range(kc):
                        r = e.matmul(ps[0:ncols, 0:nt], self.wr[s][:, k, 0:ncols], in_fn(k, tt),
                                     start=(k == 0), stop=(k == kc - 1))
                    return r
                self.op("pe", mm, reads=["wr%d" % s] + list(in_res), writes=["psb%d" % pi])
                evac(j, tt, ps, pi)
                if self.bg is not None:
                    self.bg_n += 1
                    if self.bg_n % 3 == 0:
                        self.bg_step()

    def bg_step(self):
        try:
            next(self.bg)
        except StopIteration:
            self.bg = None

    def bg_drain(self):
        while self.bg is not None:
            self.bg_step()

    def cvi(self, tg, tt):
        return 0 if (tg == 0 and tt == 0) else 1

    def load_xq(self, t0, q):
        xi = self.xq_i
        self.xq_i = (xi + 1) % len(self.xq)
        self.dma("sp", self.xq[xi][:], self.xT[q * 4:(q + 1) * 4, :, t0:t0 + TT].rearrange("k p t -> p k t"),
                 reads=[(self.kx, t0 // TT)], writes=["xq%d" % xi])
        return xi

    def rstd_tile(self, t0):
        pi = self.next_ps()
        ps = self.psb[pi]
        for q in range(4):
            xi = self.load_xq(t0, q)
            sqt = self.sq[q % 2]
            self.op("act", _act(AF.Square, sqt[:], self.xq[xi][:]), reads=["xq%d" % xi], writes=["sq%d" % (q % 2)])

            def mm(e, ps=ps, q=q, sqt=sqt):
                for k in range(4):
                    r = e.matmul(ps[:, :], self.ones_bf[:], sqt[:, k, :], start=(q == 0 and k == 0),
                                 stop=(q == 3 and k == 3))
                return r
            self.op("pe", mm, reads=["sq%d" % (q % 2), "ones_bf"], writes=["psb%d" % pi])
        self.op("act", _act(AF.Sqrt, self.rstd[:], ps[:, :], scale=1.0 / D, bias=self.eps_t[:]),
                reads=["psb%d" % pi, "eps_t"], writes=["rstd"])
        self.op("dve", lambda e: e.reciprocal(out=self.rstd[:], in_=self.rstd[:]), reads=["rstd"], writes=["rstd"])

    def norm_to_hT(self, tg, l, i):
        sc, bi = self.nsc[(l, i)], self.nbi[(l, i)]
        for tt in range(TG // TT):
            t0 = tg * TG + tt * TT
            self.rstd_tile(t0)
            cv = self.cvi(tg, tt)
            for q in range(4):
                xi = self.load_xq(t0, q)
                for kk in range(4):
                    k = q * 4 + kk
                    ti = self.next_tmp()
                    tm = self.tmpf[ti]
                    self.op("dve", lambda e, kk=kk, tm=tm, xi=xi: e.tensor_tensor(
                        out=tm[:], in0=self.xq[xi][:, kk, :], in1=self.rstd[:], op=ALU.mult),
                        reads=["xq%d" % xi, "rstd"], writes=["tmpf%d" % ti])
                    self.op("act", _act(AF.Identity, self.hT[:, k, tt * TT:(tt + 1) * TT], tm[:],
                                        scale=sc[:, k, cv:cv + 1], bias=bi[:, k, cv:cv + 1]),
                            reads=["tmpf%d" % ti, "nsc%d_%d" % (l, i), "nbi%d_%d" % (l, i)], writes=["hT"])

    def resid_evac(self, tg, gs, gs_name):
        def evac(blk, tt, ps, pi):
            t0 = tg * TG + tt * TT
            gtt = t0 // TT
            xi = self.xr_i
            self.xr_i = (xi + 1) % len(self.xr)
            xr = self.xr[xi]
            cv = self.cvi(tg, tt)
            self.dma("sp", xr[:], self.xT[blk, :, t0:t0 + TT], reads=[(self.kx, gtt)], writes=["xr%d" % xi])
            self.op("dve", lambda e: e.scalar_tensor_tensor(out=xr[:], in0=ps[:, :], scalar=gs[:, blk, cv:cv + 1],
                                                            in1=xr[:], op0=ALU.mult, op1=ALU.add),
                    reads=["psb%d" % pi, "xr%d" % xi, gs_name], writes=["xr%d" % xi])
            self.dma("sp", self.xT[blk, :, t0:t0 + TT], xr[:], reads=["xr%d" % xi], writes=[(self.kx, gtt)])
        return evac

    def ffn(self, tg, l, which):
        i = 0 if which == 0 else 2
        w13 = self.w13[l, which]
        w2 = self.w2[l, which]
        gs = self.gsc[(l, i)]
        NJ = DFF // 128 // 2
        for half in range(2):
            blocks = []
            for jj in range(NJ):
                j = half * NJ + jj
                blocks.append((w13[:, j * 128:(j + 1) * 128], KC, 128))
                blocks.append((w13[:, DFF + j * 128:DFF + (j + 1) * 128], KC, 128))
            sa = {}

            def evac13(blk, tt, ps, pi, sa=sa):
                jj, isb = blk // 2, blk % 2
                if not isb:
                    ti = self.next_tmp()
                    sa[tt] = ti
                    self.op("act", _act(AF.Silu, self.tmpf[ti][:], ps[:, :]), reads=["psb%d" % pi],
                            writes=["tmpf%d" % ti])
                else:
                    ti = sa[tt]
                    self.op("dve", lambda e: e.tensor_tensor(out=self.gT[:, jj, tt * TT:(tt + 1) * TT],
                                                             in0=self.tmpf[ti][:], in1=ps[:, :], op=ALU.mult),
                            reads=["psb%d" % pi, "tmpf%d" % ti], writes=["gT"])
            self.linear(blocks, lambda k, tt: self.hT[:, k, tt * TT:(tt + 1) * TT], ["hT"], [TT] * 3, evac13)
            r0 = half * (DFF // 2)
            blocks2 = [(w2[r0:r0 + DFF // 2, j * 128:(j + 1) * 128], KC, 128) for j in range(KC)]
            self.linear(blocks2, lambda k, tt: self.gT[:, k, tt * TT:(tt + 1) * TT], ["gT"], [TT] * 3,
                        self.resid_evac(tg, gs, "gsc%d_%d" % (l, i)))

    def phase_final(self, ntt):
        fg = self.load_vecT("fgT", self.final_g, KC)
        yo, yn = self._yo, self._yn
        for gtt in range(ntt):
            t0 = gtt * TT
            self.rstd_tile(t0)
            for q in range(4):
                xi = self.load_xq(t0, q)
                for kk in range(4):
                    k = q * 4 + kk
                    self.op("dve", lambda e, k=k, kk=kk, xi=xi: e.scalar_tensor_tensor(
                        out=yn[:, k, :], in0=self.xq[xi][:, kk, :], scalar=fg[:, k:k + 1], in1=self.rstd[:],
                        op0=ALU.mult, op1=ALU.mult), reads=["xq%d" % xi, "rstd", "fgT"], writes=["yn"])
            for s_ in range(TT // 128):
                b = (gtt * 4 + s_) % 2
                for q in range(4):
                    pi = self.next_ps()
                    ps = self.psb[pi]

                    def tr(e, q=q, ps=ps, s_=s_):
                        for j in range(4):
                            k = q * 4 + j
                            r = e.transpose(out=ps[:, j * 128:(j + 1) * 128], in_=yn[:, k, s_ * 128:(s_ + 1) * 128],
                                            identity=self.ident[:])
                        return r
                    self.op("pe", tr, reads=["yn", "ident"], writes=["psb%d" % pi])
                    self.op("act", lambda e, b=b, q=q, ps=ps: e.copy(out=yo[b][:, q * 512:(q + 1) * 512], in_=ps[:, :]),
                            reads=["psb%d" % pi], writes=["yo%d" % b])
                r0 = t0 + s_ * 128
                self.dma("sp", self.y_tok[r0:r0 + 128, :], yo[b], reads=["yo%d" % b], writes=[("y", r0)])

    def mixer_setup(self):
        nc = self.nc
        di = lambda n, s: nc.dram_tensor(n, list(s), F32, kind="ExternalInput").ap()
        do = lambda n, s: nc.dram_tensor(n, list(s), F32, kind="ExternalOutput").ap()
        self.cst_in = di("cst", [128, NCST])
        self.ropeC = di("ropeC", [128, 4096])
        self.ropeS = di("ropeS", [128, 4096])
        self.f_bias = di("mlstm_f_bias", [DEPTH, 16])
        self.conv_w = di("conv_w", [DEPTH, 3, 3 * WM])
        self.A_log = di("delta_A_log", [DEPTH, 16])
        self.dt_bias = di("delta_dt_bias", [DEPTH, 16])
        self.lgam = di("ret_log_gamma", [DEPTH, 16])
        self.hng = di("head_norm_g", [DEPTH, 3, HD])
        self.sC = di("st_C", [DEPTH, 2, NH, HD, HD])
        self.sn = di("st_n", [DEPTH, 2, NH, HD])
        self.sm = di("st_m", [DEPTH, 16])
        self.sD = di("st_D", [DEPTH, 2, NH, HD, HD])
        self.sR = di("st_R", [DEPTH, 2, NH, HD, HD])
        self.oC = do("o_C", [2, DEPTH, 2, NH, HD, HD])
        self.on = do("o_n", [2, DEPTH, 2, NH, HD])
        self.om = do("o_m", [2, DEPTH, 16])
        self.oD = do("o_D", [2, DEPTH, 2, NH, HD, HD])
        self.oR = do("o_R", [2, DEPTH, 2, NH, HD, HD])
        self.P = {m: self.dram("P" + m, [24, 128, TC], BF16) for m in "ABC"}
        self.PRE = self.dram("PRE", [24, 128, TC], F32)
        self.GO = self.dram("GO", [3, 8, 128, TC], BF16)
        self.GM = self.dram("GM", [48, 128, TC], BF16)
        self.GA = self.dram("GA", [4, 16, TC], F32)
        self.YS = self.dram("YS", [3, 8, 128, TC], BF16)
        self.OFs = self.dram("OFs", [TC, WM], F32)
        self.MG = self.dram("MG", [KC, 128, TC], BF16)
        self.cst = self.sb("cst_sb", [128, NCST], F32)
        self.dma("sp", self.cst[:], self.cst_in, writes=["cst"])
        self.ident_bf = self.sb("ident_bf", [128, 128], BF16)
        self.op("dve", lambda e: e.tensor_copy(out=self.ident_bf[:], in_=self.ident[:]), reads=["ident"], writes=["ident_bf"])
        self.ones_f = self.sb("ones_f", [128, 256], F32)
        self.op("dve", lambda e: e.memset(self.ones_f[:], 1.0), writes=["ones_f"])
        self.stb = [self.sb("stb%d" % i, [128, TT], BF16) for i in range(3)]
        self.stb_i = 0
        self.binT = self.sb("binT", [128, 160], F32)
        self.gpar = self.sb("gpar", [16, 8], F32)
        self.rC = self.sb("rC", [128, 3, TT], F32)
        self.rS = self.sb("rS", [128, 3, TT], F32)

    def C(self, name, rows=64):
        o, n = CST[name]
        return self.cst[0:rows, o:o + n]

    def next_stb(self):
        i = self.stb_i
        self.stb_i = (i + 1) % 3
        return i

    def w_in_blocks(self, l):
        B = []
        s = HD ** -0.5
        for j in range(8):
            B.append((OFF["qA"] + j * 128, 128, "lin", ("PA", j), 1.0))
        for j in range(8):
            B.append((OFF["kA"] + j * 128, 128, "lin", ("PA", 8 + j), s))
        for j in range(8):
            B.append((OFF["vA"] + j * 128, 128, "lin", ("PA", 16 + j), 1.0))
        for j in range(8):
            B.append((OFF["oA"] + j * 128, 128, "sig", ("GO", 0, j), 1.0))
        B.append((OFF["iA"], 16, "gi", ("GA", 0), 1.0))
        B.append((OFF["fA"], 16, "gf", ("GA", 1), 1.0))
        for j in range(24):
            B.append((OFF["qkvB"] + j * 128, 128, "pre", ("PRE", j), 1.0))
        for j in range(8):
            B.append((OFF["zB"] + j * 128, 128, "silu", ("GO", 1, j), 1.0))
        B.append((OFF["betaB"], 16, "gb", ("GA", 2), 1.0))
        B.append((OFF["aB"], 16, "gg", ("GA", 3), 1.0))
        for j in range(8):
            B.append((OFF["qC"] + j * 128, 128, "rope", ("PC", j), s))
        for j in range(8):
            B.append((OFF["kC"] + j * 128, 128, "rope", ("PC", 8 + j), 1.0))
        for j in range(8):
            B.append((OFF["vC"] + j * 128, 128, "lin", ("PC", 16 + j), 1.0))
        for j in range(8):
            B.append((OFF["gC"] + j * 128, 128, "silu", ("GO", 2, j), 1.0))
        for j in range(48):
            B.append((OFF["gm"] + j * 128, 128, "sig", ("GM", j), 1.0))
        return B

    def dst_ap(self, dst, t0, n, rows=128):
        if dst[0] in ("PA", "PB", "PC"):
            return self.P[dst[0][1]][dst[1], 0:rows, t0:t0 + n]
        if dst[0] == "PRE":
            return self.PRE[dst[1], 0:rows, t0:t0 + n]
        if dst[0] == "GO":
            return self.GO[dst[1], dst[2], 0:rows, t0:t0 + n]
        if dst[0] == "GM":
            return self.GM[dst[1], 0:rows, t0:t0 + n]
        if dst[0] == "GA":
            return self.GA[dst[1], 0:rows, t0:t0 + n]
        raise KeyError(dst)

    def mixer_params(self, l):
        B = self.w_in_blocks(l)
        self._B = B
        for bi, (c0, nco, kind, dst, sc) in enumerate(B):
            self.dma("sp", self.binT[0:nco, bi:bi + 1], self.b_in[l, c0:c0 + nco].rearrange("(p o) -> p o", o=1),
                     writes=["binT"], allow_slow_non_contiguous=True)
        gp = self.gpar
        for j, src in enumerate((self.f_bias, self.A_log, self.dt_bias)):
            self.dma("sp", gp[:, j:j + 1], src[l].rearrange("(p o) -> p o", o=1), writes=["gpar"],
                     allow_slow_non_contiguous=True)
        bi_f = [i for i, b in enumerate(B) if b[2] == "gf"][0]
        bi_g = [i for i, b in enumerate(B) if b[2] == "gg"][0]
        self.op("dve", lambda e: e.scalar_tensor_tensor(out=gp[:, 3:4], in0=gp[:, 0:1], scalar=-1.0,
                                                        in1=self.binT[0:16, bi_f:bi_f + 1], op0=ALU.mult, op1=ALU.subtract),
                reads=["gpar", "binT"], writes=["gpar"])
        self.op("dve", lambda e: e.tensor_tensor(out=gp[:, 4:5], in0=gp[:, 2:3], in1=self.binT[0:16, bi_g:bi_g + 1],
                                                 op=ALU.add), reads=["gpar", "binT"], writes=["gpar"])
        self.op("act", _act(AF.Exp, gp[:, 5:6], gp[:, 1:2]), reads=["gpar"], writes=["gpar"])
        self.op("dve", lambda e: e.tensor_scalar(out=gp[:, 5:6], in0=gp[:, 5:6], scalar1=-1.0, scalar2=None, op0=ALU.mult),
                reads=["gpar"], writes=["gpar"])

    def m1_project(self, l, tg):
        B = self._B
        blocks = [(self.w_in[l][:, c0:c0 + nco], KC, nco) for (c0, nco, kind, dst, sc) in B]
        for tt in range(3):
            if self.cvi(tg, tt):
                p0 = tg * TG + tt * TT - 512
                self.dma("sp", self.rC[:, tt, :], self.ropeC[:, p0:p0 + TT], writes=["rC"])
                self.dma("sp", self.rS[:, tt, :], self.ropeS[:, p0:p0 + TT], writes=["rS"])
        gp = self.gpar

        def evac(blk, tt, ps, pi):
            c0, nco, kind, dst, sc = B[blk]
            t0 = tg * TG + tt * TT
            gtt = t0 // TT
            bias = self.binT[0:nco, blk:blk + 1]
            pr = ["psb%d" % pi, "binT"]
            wr = [(dst, gtt)]
            if kind in ("lin", "sig", "silu") or (kind == "rope" and not self.cvi(tg, tt)):
                si = self.next_stb()
                st = self.stb[si]
                if kind in ("lin", "rope"):
                    self.op("dve", lambda e: e.tensor_scalar(out=st[:], in0=ps[:, :], scalar1=bias, scalar2=sc,
                                                             op0=ALU.add, op1=ALU.mult), reads=pr, writes=["stb%d" % si])
                else:
                    f = AF.Sigmoid if kind == "sig" else AF.Silu
                    self.op("act", _act(f, st[:], ps[:, :], bias=bias), reads=pr, writes=["stb%d" % si])
                self.dma("sp", self.dst_ap(dst, t0, TT), st[:], reads=["stb%d" % si], writes=wr)
            elif kind == "rope":
                ti = self.next_tmp()
                xf = self.tmpf[ti]
                self.op("dve", lambda e: e.tensor_scalar(out=xf[:], in0=ps[:, :], scalar1=bias, scalar2=sc,
                                                         op0=ALU.add, op1=ALU.mult), reads=pr, writes=["tmpf%d" % ti])
                p2 = self.next_ps()
                ps2 = self.psb[p2]
                self.op("pe", lambda e: e.matmul(ps2[:, :], self.C("Rm", 128), xf[:], start=True, stop=True),
                        reads=["tmpf%d" % ti, "cst"], writes=["psb%d" % p2])
                t2 = self.next_tmp()
                x2 = self.tmpf[t2]
                self.op("dve", lambda e: e.tensor_tensor(out=x2[:], in0=ps2[:, :], in1=self.rS[:, tt, :], op=ALU.mult),
                        reads=["psb%d" % p2, "rS"], writes=["tmpf%d" % t2])
                self.op("dve", lambda e: e.tensor_tensor(out=xf[:], in0=xf[:], in1=self.rC[:, tt, :], op=ALU.mult),
                        reads=["tmpf%d" % ti, "rC"], writes=["tmpf%d" % ti])
                si = self.next_stb()
                st = self.stb[si]
                self.op("dve", lambda e: e.tensor_tensor(out=st[:], in0=xf[:], in1=x2[:], op=ALU.add),
                        reads=["tmpf%d" % ti, "tmpf%d" % t2], writes=["stb%d" % si])
                self.dma("sp", self.dst_ap(dst, t0, TT), st[:], reads=["stb%d" % si], writes=wr)
            else:
                ti = self.next_tmp()
                tm = self.tmpf[ti]
                tr = ["tmpf%d" % ti]
                n = nco
                if kind == "pre" or kind == "gi":
                    self.op("dve", lambda e: e.tensor_scalar(out=tm[0:n, :], in0=ps[0:n, :], scalar1=bias, scalar2=None,
                                                             op0=ALU.add), reads=pr, writes=tr)
                elif kind == "gb":
                    self.op("act", _act(AF.Sigmoid, tm[0:n, :], ps[0:n, :], bias=bias), reads=pr, writes=tr)
                elif kind == "gf":
                    self.op("act", _act(AF.Exp, tm[0:n, :], ps[0:n, :], scale=-1.0, bias=gp[:, 3:4]),
                            reads=pr + ["gpar"], writes=tr)
                    self.op("act", _act(AF.Ln, tm[0:n, :], tm[0:n, :], bias=1.0), reads=tr, writes=tr)
                    self.op("dve", lambda e: e.tensor_scalar(out=tm[0:n, :], in0=tm[0:n, :], scalar1=-1.0, scalar2=None,
                                                             op0=ALU.mult), reads=tr, writes=tr)
                elif kind == "gg":
                    self.op("act", _act(AF.Exp, tm[0:n, :], ps[0:n, :], bias=gp[:, 4:5]), reads=pr + ["gpar"], writes=tr)
                    self.op("act", _act(AF.Ln, tm[0:n, :], tm[0:n, :], bias=1.0), reads=tr, writes=tr)
                    self.op("dve", lambda e: e.tensor_scalar(out=tm[0:n, :], in0=tm[0:n, :], scalar1=gp[:, 5:6],
                                                             scalar2=None, op0=ALU.mult), reads=tr + ["gpar"], writes=tr)
                self.dma("sp", self.dst_ap(dst, t0, TT, rows=n), tm[0:n, :], reads=tr, writes=wr)
        self.linear(blocks, lambda k, tt: self.hT[:, k, tt * TT:(tt + 1) * TT], ["hT"], [TT] * 3, evac)

    def conv_prep(self, l):
        self._cw = [self.load_vecT("cw%d_%d" % (l, j), self.conv_w[l, j], 24) for j in range(3)]
        if l == 0:
            self._cv = self.sb("cvt0", [128, TT + 2], F32)

    def conv_gen(self, l, grp):
        cw = self._cw
        cv = self._cv
        pieces = [(0, 256, True, True), (256, 256, True, True)]
        for i in range(8):
            pieces.append((512 + i * 512, 512, i == 0, i == 7))
        pieces = [p for p in pieces if (p[0] + p[1] + (0 if p[3] else 1) - 1) // TG == grp]
        for blk in range(24):
            for (t0, n, first, last) in pieces:
                lo = t0 - (0 if first else 1)
                hi = t0 + n + (0 if last else 1)
                if first or last:
                    self.op("dve", lambda e: e.memset(cv[:, 0:n + 2], 0.0), writes=["cvt"])
                gts = sorted(set([lo // TT, (hi - 1) // TT]))
                self.dma("sp", cv[:, (lo - t0 + 1):(hi - t0 + 1)], self.PRE[blk, :, lo:hi],
                         reads=[(("PRE", blk), g) for g in gts], writes=["cvt"])
                ti = self.next_tmp()
                y = self.tmpf[ti]
                tr = ["tmpf%d" % ti]
                names = ["cw%d_%d" % (l, j) for j in range(3)]
                self.op("dve", lambda e, y=y, n=n, blk=blk: e.tensor_scalar(
                    out=y[:, 0:n], in0=cv[:, 1:n + 1], scalar1=cw[1][:, blk:blk + 1], scalar2=None, op0=ALU.mult),
                    reads=["cvt"] + names, writes=tr)
                self.op("dve", lambda e, y=y, n=n, blk=blk: e.scalar_tensor_tensor(
                    out=y[:, 0:n], in0=cv[:, 0:n], scalar=cw[0][:, blk:blk + 1], in1=y[:, 0:n], op0=ALU.mult,
                    op1=ALU.add), reads=["cvt"] + names + tr, writes=tr)
                self.op("dve", lambda e, y=y, n=n, blk=blk: e.scalar_tensor_tensor(
                    out=y[:, 0:n], in0=cv[:, 2:n + 2], scalar=cw[2][:, blk:blk + 1], in1=y[:, 0:n], op0=ALU.mult,
                    op1=ALU.add), reads=["cvt"] + names + tr, writes=tr)
                self.op("act", _act(AF.Silu, y[:, 0:n], y[:, 0:n]), reads=tr, writes=tr)
                si = self.next_stb()
                st = self.stb[si]
                if blk < 16:
                    sq = self.sq[0]
                    self.op("act", _act(AF.Square, sq[:, 0, 0:n], y[:, 0:n]), reads=tr, writes=["sq0"])
                    pi = self.next_ps()
                    ps = self.psb[pi]
                    self.op("pe", lambda e, ps=ps, n=n, sq=sq: e.matmul(ps[:, 0:n], self.ones_bf[:], sq[:, 0, 0:n],
                                                                        start=True, stop=True),
                            reads=["sq0", "ones_bf"], writes=["psb%d" % pi])
                    self.op("act", _act(AF.Sqrt, self.rstd[:, 0:n], ps[:, 0:n], bias=self.eps_t[:]),
                            reads=["psb%d" % pi, "eps_t"], writes=["rstd"])
                    self.op("dve", lambda e, n=n: e.reciprocal(out=self.rstd[:, 0:n], in_=self.rstd[:, 0:n]),
                            reads=["rstd"], writes=["rstd"])
                    sc = HD ** -0.5 if blk < 8 else 1.0
                    self.op("dve", lambda e, y=y, n=n, st=st, sc=sc: e.scalar_tensor_tensor(
                        out=st[:, 0:n], in0=y[:, 0:n], scalar=sc, in1=self.rstd[:, 0:n], op0=ALU.mult, op1=ALU.mult),
                        reads=tr + ["rstd"], writes=["stb%d" % si])
                else:
                    self.op("act", lambda e, y=y, n=n, st=st: e.copy(out=st[:, 0:n], in_=y[:, 0:n]), reads=tr,
                            writes=["stb%d" % si])
                self.dma("sp", self.P["B"][blk, :, t0:t0 + n], st[:, 0:n], reads=["stb%d" % si],
                         writes=[(("PB", blk), g) for g in sorted(set([t0 // TT, (t0 + n - 1) // TT]))])
                yield

    def m3_merge(self, l, tg):
        hv = self.hraw[:].bitcast(BF16)
        gv = self.graw[:].bitcast(BF16)
        ys = [hv[:, 0:8 * TG].rearrange("p (h t) -> p h t", h=8), hv[:, 8 * TG:16 * TG].rearrange("p (h t) -> p h t", h=8),
              gv[:, 0:8 * TG].rearrange("p (h t) -> p h t", h=8)]
        ysn = ["hT", "hT", "gT"]
        t_0 = tg * TG
        for i in range(3):
            self.dma("sp", ys[i], self.YS[i, :, :, t_0:t_0 + TG].rearrange("h p t -> p h t"),
                     reads=[((self.kys, i), (t_0 // TT) + j) for j in range(3)], writes=[ysn[i]])
        acc = {}
        for j in range(KC):
            blocks = [(self.w_br[l, i][:, j * 128:(j + 1) * 128], 8, 128) for i in range(3)]

            def evac(blk, tt, ps, pi, j=j):
                i = blk
                t0 = t_0 + tt * TT
                gtt = t0 // TT
                si = self.next_stb()
                gmt = self.stb[si]
                self.dma("sp", gmt[:], self.GM[i * KC + j, :, t0:t0 + TT], reads=[((self.kgm, i * KC + j), gtt)],
                         writes=["stb%d" % si])
                if i == 0:
                    ti = self.next_tmp()
                    acc[tt] = ti
                    self.op("dve", lambda e: e.tensor_tensor(out=self.tmpf[ti][:], in0=ps[:, :], in1=gmt[:], op=ALU.mult),
                            reads=["psb%d" % pi, "stb%d" % si], writes=["tmpf%d" % ti])
                else:
                    ti = acc[tt]
                    a = self.tmpf[ti]
                    xi = self.xr_i
                    self.xr_i = (xi + 1) % len(self.xr)
                    t2 = self.xr[xi]
                    self.op("dve", lambda e: e.tensor_tensor(out=t2[:], in0=ps[:, :], in1=gmt[:], op=ALU.mult),
                            reads=["psb%d" % pi, "stb%d" % si], writes=["xr%d" % xi])
                    if i == 1:
                        self.op("dve", lambda e: e.tensor_tensor(out=a[:], in0=a[:], in1=t2[:], op=ALU.add),
                                reads=["tmpf%d" % ti, "xr%d" % xi], writes=["tmpf%d" % ti])
                    else:
                        s2 = self.next_stb()
                        mo = self.stb[s2]
                        self.op("dve", lambda e: e.tensor_tensor(out=mo[:], in0=a[:], in1=t2[:], op=ALU.add),
                                reads=["tmpf%d" % ti, "xr%d" % xi], writes=["stb%d" % s2])
                        self.dma("sp", self.MG[j, :, t0:t0 + TT], mo[:], reads=["stb%d" % s2], writes=[(("MG", j), gtt)])
            self.linear3(blocks, ys, ysn, evac)
        self.dma("sp", self.hT, self.MG[:, :, t_0:t_0 + TG].rearrange("k p t -> p k t"),
                 reads=[(("MG", j), (t_0 // TT) + q) for j in range(KC) for q in range(3)], writes=["hT"])
        blocks = [(self.w_out[l][:, j * 128:(j + 1) * 128], KC, 128) for j in range(KC)]
        self.linear(blocks, lambda k, tt: self.hT[:, k, tt * TT:(tt + 1) * TT], ["hT"], [TT] * 3,
                    self.resid_evac(tg, self.gsc[(l, 1)], "gsc%d_1" % l))

    def linear3(self, blocks, ys, ysn, evac):
        slots = []
        for (ap, kc, ncols) in blocks:
            s = self.wr_i
            self.wr_i = (s + 1) % self.NWR
            slots.append(s)
            self.dma("pool", self.wr[s][:, 0:kc, 0:ncols], ap.rearrange("(k p) n -> p k n", p=128), writes=["wr%d" % s])
        for tt in range(3):
            for i in range(3):
                s = slots[i]
                pi = self.next_ps()
                ps = self.psb[pi]

                def mm(e, s=s, i=i, tt=tt, ps=ps):
                    for k in range(8):
                        r = e.matmul(ps[:, :], self.wr[s][:, k, :], ys[i][:, k, tt * TT:(tt + 1) * TT],
                                     start=(k == 0), stop=(k == 7))
                    return r
                self.op("pe", mm, reads=["wr%d" % s, ysn[i]], writes=["psb%d" % pi])
                evac(i, tt, ps, pi)
    def ar(self, name, parts, free, dt=F32):
        n = int(np.prod(free))
        nf = n if dt == F32 else (n + 1) // 2
        for raw, pname, key in ((self.hraw, "hT", "h"), (self.graw, "gT", "g")):
            off = self._aro[key]
            if off + nf <= 12288:
                self._aro[key] = off + nf
                v = raw[0:parts, off:off + nf]
                if dt != F32:
                    v = v.bitcast(dt)
                if len(free) == 2:
                    v = v.rearrange("p (a b) -> p a b", a=free[0])
                self.alias(name, pname)
                return v
        raise RuntimeError("arena full " + name)

    def scan_setup(self):
        self._aro = {"h": 0, "g": 0}
        a = self.ar
        T = {}
        for n in ("qT", "kT", "vT", "gt"):
            T[n] = a(n, 128, [8, TT], BF16)
        T["S"] = a("S", 128, [8, 128]); T["Sb"] = a("Sb", 128, [8, 128], BF16)
        T["nS"] = a("nS", 128, [8, 2]); T["nb"] = a("nb", 128, [8, 2], BF16)
        T["o"] = a("o_sb", 64, [8, 128]); T["of"] = a("of_sb", 64, [8, 128])
        T["U0"] = T["of"]
        for n in ("gi", "gf", "gb", "gg"):
            T[n] = a(n, 16, [TT])
        for n in ("ktm", "vtm", "khat", "bv", "bk", "kend", "Ub"):
            T[n] = a(n, 64, [8, 128], BF16)
        for n in ("ATb", "qkTb", "TTb"):
            T[n] = a(n, 64, [8, 64], BF16)
        T["WkT"] = a("WkT", 128, [8, 64], BF16)
        T["ysc"] = a("ysc", 128, [8, 64], BF16)
        for n in ("MT", "t64", "dec", "A", "AT", "T", "TT_", "W", "OkT", "qk"):
            T[n] = a(n, 64, [8, 64])
        T["gtm"] = a("gtm", 64, [32])
        for n in ("sa", "sb_", "sc_", "sd_", "se_"):
            T[n] = a(n, 64, [8])
        T["eL"] = a("eL", 128, [8]); T["g64"] = a("g64", 128, [8]); T["lgb"] = a("lgb", 128, [8])
        T["rsc"] = a("rsc", 64, [8]); T["ksc"] = a("ksc", 64, [8]); T["ss"] = a("ss", 64, [8])
        T["m0b"] = a("m0b", 128, [8])
        self.T = T
        self.hg = [[None] * 3 for _ in range(DEPTH)]

    def bank(self, i, rows, shape, dt=F32):
        v = self.psb[i][0:rows, :]
        if dt != F32:
            v = v.bitcast(dt)
        n = int(np.prod(shape))
        v = v[:, 0:n]
        if len(shape) == 2:
            v = v.rearrange("p (a b) -> p a b", a=shape[0])
        return v

    def bc(self, ap2, n):
        return ap2.unsqueeze(2).broadcast_to([ap2.shape[0], ap2.shape[1], n])

    def bm(self, ap2):
        return ap2.unsqueeze(1).broadcast_to([ap2.shape[0], 8, ap2.shape[1]])

    def load_tt(self, m, gtt, bwd):
        T = self.T
        t0 = gtt * TT
        P = self.P["ABC"[m]]
        for i, n in enumerate(("qT", "kT", "vT")):
            self.dma("sp", T[n], P[8 * i:8 * i + 8, :, t0:t0 + TT].rearrange("h p t -> p h t"),
                     reads=[(("P" + "ABC"[m], 8 * i + h), gtt) for h in range(8)], writes=[n])
        if bwd:
            self.dma("sp", T["gt"], self.GO[m, :, :, t0:t0 + TT].rearrange("h p t -> p h t"),
                     reads=[(("GO", m, h), gtt) for h in range(8)], writes=["gt"])
        if m == 0:
            self.dma("sp", T["gi"], self.GA[0, :, t0:t0 + TT], reads=[(("GA", 0), gtt)], writes=["gi"])
            self.dma("sp", T["gf"], self.GA[1, :, t0:t0 + TT], reads=[(("GA", 1), gtt)], writes=["gf"])
        if m == 1:
            self.dma("sp", T["gb"], self.GA[2, :, t0:t0 + TT], reads=[(("GA", 2), gtt)], writes=["gb"])
            self.dma("sp", T["gg"], self.GA[3, :, t0:t0 + TT], reads=[(("GA", 3), gtt)], writes=["gg"])

    def mm8(self, banks, rows, width, fn, reads):
        T = self.T
        per = 8 // len(banks)
        for bi, b in enumerate(banks):
            v = self.bank(b, rows, [per, width])

            def f(e, bi=bi, v=v):
                r = None
                for hh in range(per):
                    r = fn(e, bi * per + hh, v[:, hh, :])
                return r
            self.op("pe", f, reads=reads, writes=["psb%d" % b])

    def evac8(self, eng, banks, rows, width, fn, reads, writes):
        per = 8 // len(banks)
        for bi, b in enumerate(banks):
            v = self.bank(b, rows, [per, width])
            hs = slice(bi * per, (bi + 1) * per)
            self.op(eng, lambda e, v=v, hs=hs: fn(e, v, hs), reads=reads + ["psb%d" % b], writes=writes)

    def kv_tm(self, c0):
        T = self.T
        for src, bk_, dst in (("kT", 2, "ktm"), ("vT", 3, "vtm")):
            v = self.bank(bk_, 64, [8, 128], BF16)

            def f(e, src=src, v=v):
                for h in range(8):
                    r = e.transpose(out=v[:, h, :], in_=T[src][:, h, c0:c0 + L], identity=self.ident_bf[:])
                return r
            self.op("pe", f, reads=[src, "ident_bf"], writes=["psb%d" % bk_])
        v3 = self.bank(3, 64, [8, 128], BF16)
        self.op("act", lambda e: e.copy(out=T["vtm"], in_=v3), reads=["psb3"], writes=["vtm"])

    def scale_k(self, dst, sc_name, sc_ap):
        T = self.T
        v2 = self.bank(2, 64, [8, 128], BF16)
        self.op("dve", lambda e: e.tensor_tensor(out=T[dst], in0=v2, in1=self.bc(sc_ap, 128), op=ALU.mult),
                reads=["psb2", sc_name], writes=[dst])

    def gate_tm(self, rows_a, rows_b, c0):
        T = self.T
        v = self.psb[1]

        def f(e):
            e.transpose(out=v[0:64, 0:16], in_=T[rows_a][:, c0:c0 + L], identity=self.ident[0:16, 0:16])
            return e.transpose(out=v[0:64, 16:32], in_=T[rows_b][:, c0:c0 + L], identity=self.ident[0:16, 0:16])
        self.op("pe", f, reads=[rows_a, rows_b, "ident"], writes=["psb1"])
        self.op("dve", lambda e: e.tensor_copy(out=T["gtm"], in_=v[0:64, 0:32]), reads=["psb1"], writes=["gtm"])

    def state_update(self, banks_src_fn, dec_name, dec_ap, with_n=False):
        T = self.T
        self.op("dve", lambda e: e.tensor_tensor(out=T["S"], in0=T["S"], in1=self.bc(dec_ap, 128), op=ALU.mult),
                reads=["S", dec_name], writes=["S"])
        self.evac8("dve", [6, 7], 128, 128, lambda e, v, hs: e.tensor_tensor(out=T["S"][:, hs, :], in0=T["S"][:, hs, :],
                                                                               in1=v, op=ALU.add), ["S"], ["S"])
        self.op("act", lambda e: e.copy(out=T["Sb"], in_=T["S"]), reads=["S"], writes=["Sb"])

    def finalize(self, l, m, c0, t0):
        T = self.T
        gtt = t0 // TT
        self.dma("sp", T["of"], self.OFs[t0:t0 + L, :].rearrange("p (h e) -> p h e", h=8), reads=[("OF", t0)],
                 writes=["of_sb"])
        self.op("dve", lambda e: e.tensor_tensor(out=T["o"], in0=T["o"], in1=T["of"], op=ALU.add),
                reads=["o_sb", "of_sb"], writes=["o_sb"])
        self.op("dve", lambda e: e.tensor_tensor(out=T["of"], in0=T["o"], in1=T["o"], op=ALU.mult), reads=["o_sb"],
                writes=["of_sb"])
        self.op("dve", lambda e: e.tensor_reduce(out=T["ss"], in_=T["of"], op=ALU.add, axis=AX.X), reads=["of_sb"],
                writes=["ss"])
        self.op("act", _act(AF.Sqrt, T["ss"], T["ss"], scale=1.0 / HD, bias=self.eps_t[0:64, :]), reads=["ss", "eps_t"],
                writes=["ss"])
        self.op("dve", lambda e: e.reciprocal(out=T["ss"], in_=T["ss"]), reads=["ss"], writes=["ss"])
        self.op("dve", lambda e: e.tensor_tensor(out=T["Ub"], in0=T["o"], in1=self.bc(T["ss"], 128), op=ALU.mult),
                reads=["o_sb", "ss"], writes=["Ub"])
        v = self.bank(2, 128, [8, 64], BF16)

        def f(e):
            for h in range(8):
                r = e.transpose(out=v[:, h, :], in_=T["Ub"][:, h, :], identity=self.ident_bf[0:64, 0:64])
            return r
        self.op("pe", f, reads=["Ub", "ident_bf"], writes=["psb2"])
        hg = self.hg[l][m]
        self.op("dve", lambda e: e.scalar_tensor_tensor(out=T["ysc"], in0=v, scalar=hg[:, 0:1],
                                                        in1=T["gt"][:, :, c0:c0 + L], op0=ALU.mult, op1=ALU.mult),
                reads=["psb2", "gt", "hg%d_%d" % (l, m)], writes=["ysc"])
        self.dma("sp", self.YS[m, :, :, t0:t0 + L].rearrange("h p t -> p h t"), T["ysc"], reads=["ysc"],
                 writes=[(("YS", m), gtt)], allow_slow_non_contiguous=True)

    def out_chunk(self, l, m, d, c0, t0):
        T = self.T
        if d == 0:
            self.dma("sp", self.OFs[t0:t0 + L, :].rearrange("p (h e) -> p h e", h=8), T["o"], reads=["o_sb"],
                     writes=[("OF", t0)])
        else:
            self.finalize(l, m, c0, t0)

    def scan(self, l, m):
        T = self.T
        if self.hg[l][m] is None:
            self.hg[l][m] = self.load_vecT("hg%d_%d" % (l, m), self.hng[l, m], 1)
        for d in (0, 1):
            self.dir_setup(l, m, d)
            cur_tt = None
            for si, (s0, sl) in enumerate(SEGS):
                self.state_init(l, m, d, si)
                chunks = list(range(s0, s0 + sl, L))
                if d == 1:
                    chunks = chunks[::-1]
                for t0 in chunks:
                    gtt = t0 // TT
                    if gtt != cur_tt:
                        self.load_tt(m, gtt, d == 1)
                        cur_tt = gtt
                    c0 = t0 - gtt * TT
                    (self.step_mlstm, self.step_delta, self.step_ret)[m](l, d, c0, t0)
                if si < 2:
                    self.state_out(l, m, d, si)

    def dir_setup(self, l, m, d):
        T = self.T
        M = self.C("LE" if d == 0 else "GE")
        self._M = M
        self._Tri = M
        if m == 2:
            self.dma("sp", T["lgb"], self.lgam[l:l + 1, 8 * d:8 * d + 8].partition_broadcast(128), writes=["lgb"])
            p1 = self.C("p1f" if d == 0 else "p1b")
            p2 = self.C("p2f" if d == 0 else "p2b")
            self.op("act", _act(AF.Exp, T["rsc"], T["lgb"][0:64, :], scale=p1), reads=["lgb", "cst"], writes=["rsc"])
            self.op("act", _act(AF.Exp, T["ksc"], T["lgb"][0:64, :], scale=p2), reads=["lgb", "cst"], writes=["ksc"])
            self.op("act", _act(AF.Exp, T["g64"], T["lgb"], scale=64.0), reads=["lgb"], writes=["g64"])
            self.op("dve", lambda e: e.reciprocal(out=T["sa"], in_=T["rsc"]), reads=["rsc"], writes=["sa"])
            self.op("dve", lambda e: e.tensor_tensor(out=T["MT"], in0=self.bc(T["sa"], 64), in1=self.bm(M), op=ALU.mult),
                    reads=["sa", "cst"], writes=["MT"])

    def state_init(self, l, m, d, si):
        T = self.T
        if si < 2:
            self.op("dve", lambda e: e.memset(T["S"], 0.0), writes=["S"])
            self.op("dve", lambda e: e.memset(T["nS"], 0.0), writes=["nS"])
        else:
            src = (self.sC, self.sD, self.sR)[m]
            self.dma("sp", T["S"], src[l, d].rearrange("h k e -> k h e"), writes=["S"])
            if m == 0:
                self.dma("sp", T["nS"][:, :, 0], self.sn[l, d].rearrange("h k -> k h"), writes=["nS"],
                         allow_slow_non_contiguous=True)
                self.dma("sp", T["nS"][:, :, 1], self.sn[l, d].rearrange("h k -> k h"), writes=["nS"],
                         allow_slow_non_contiguous=True)
                self.dma("sp", T["m0b"], self.sm[l:l + 1, 8 * d:8 * d + 8].partition_broadcast(128), writes=["m0b"])
                self.op("act", _act(AF.Exp, T["m0b"], T["m0b"]), reads=["m0b"], writes=["m0b"])
                self.op("dve", lambda e: e.tensor_tensor(out=T["S"], in0=T["S"], in1=self.bc(T["m0b"], 128), op=ALU.mult),
                        reads=["S", "m0b"], writes=["S"])
                self.op("dve", lambda e: e.tensor_tensor(out=T["nS"], in0=T["nS"], in1=self.bc(T["m0b"], 2), op=ALU.mult),
                        reads=["nS", "m0b"], writes=["nS"])
        self.op("act", lambda e: e.copy(out=T["Sb"], in_=T["S"]), reads=["S"], writes=["Sb"])
        self.op("act", lambda e: e.copy(out=T["nb"], in_=T["nS"]), reads=["nS"], writes=["nb"])

    def state_out(self, l, m, d, si):
        T = self.T
        if m == 0:
            self.mlstm_state_out(l, d, si)
            return
        dst = (None, self.oD, self.oR)[m]
        self.dma("sp", dst[si, l, d].rearrange("h k e -> k h e"), T["S"], reads=["S"], writes=[("ost", m, si, l, d)])

    def step_ret(self, l, d, c0, t0):
        T = self.T
        self.mm8([0], 64, 64, lambda e, h, o: e.matmul(o, T["kT"][:, h, c0:c0 + L], T["qT"][:, h, c0:c0 + L],
                                                       start=True, stop=True), ["kT", "qT"])
        self.evac8("dve", [0], 64, 64, lambda e, v, hs: e.tensor_tensor(out=T["ATb"], in0=v, in1=T["MT"], op=ALU.mult),
                   ["MT"], ["ATb"])
        self.kv_tm(c0)
        self.scale_k("khat", "ksc", T["ksc"])

        def o_mm(e, h, o):
            e.matmul(o, T["qT"][:, h, c0:c0 + L], T["Sb"][:, h, :], start=True, stop=False)
            return e.matmul(o, T["ATb"][:, h, :], T["vtm"][:, h, :], start=False, stop=True)
        self.mm8([4, 5], 64, 128, o_mm, ["qT", "Sb", "ATb", "vtm"])
        self.evac8("dve", [4, 5], 64, 128, lambda e, v, hs: e.tensor_tensor(
            out=T["o"][:, hs, :], in0=v, in1=self.bc(T["rsc"][:, hs], 128), op=ALU.mult), ["rsc"], ["o_sb"])
        self.out_chunk(l, 2, d, c0, t0)
        self.mm8([6, 7], 128, 128, lambda e, h, o: e.matmul(o, T["khat"][:, h, :], T["vtm"][:, h, :], start=True,
                                                           stop=True), ["khat", "vtm"])
        self.state_update(None, "g64", T["g64"])

    def step_mlstm(self, l, d, c0, t0):
        T = self.T
        g = T["gtm"]
        self.gate_tm("gi", "gf", c0)
        ig, lf = g[:, 8 * d:8 * d + 8], g[:, 16 + 8 * d:16 + 8 * d + 8]
        v1 = self.psb[1]
        Tri = self._Tri
        Msk = self._M

        def f(e):
            e.matmul(v1[0:64, 32:40], Tri, lf, start=True, stop=True)
            return e.matmul(v1[0:128, 40:48], self.ones_f[0:64, 0:128], lf, start=True, stop=True)
        self.op("pe", f, reads=["gtm", "cst", "ones_f"], writes=["psb1"])
        self.op("dve", lambda e: e.tensor_tensor(out=T["sa"], in0=ig, in1=v1[0:64, 32:40], op=ALU.subtract),
                reads=["gtm", "psb1"], writes=["sa"])
        self.op("act", _act(AF.Exp, T["sa"], T["sa"]), reads=["sa"], writes=["sa"])
        self.op("act", _act(AF.Exp, T["sb_"], v1[0:64, 32:40], scale=-1.0), reads=["psb1"], writes=["sb_"])
        self.op("act", _act(AF.Exp, T["eL"], v1[0:128, 40:48]), reads=["psb1"], writes=["eL"])
        self.op("dve", lambda e: e.tensor_tensor(out=T["sc_"], in0=T["sa"], in1=T["eL"][0:64, :], op=ALU.mult),
                reads=["sa", "eL"], writes=["sc_"])
        self.mm8([0], 64, 64, lambda e, h, o: e.matmul(o, T["kT"][:, h, c0:c0 + L], T["qT"][:, h, c0:c0 + L],
                                                       start=True, stop=True), ["kT", "qT"])
        self.evac8("dve", [0], 64, 64, lambda e, v, hs: e.tensor_tensor(out=T["t64"], in0=v, in1=self.bc(T["sa"], 64),
                                                                        op=ALU.mult), ["sa"], ["t64"])
        self.op("dve", lambda e: e.tensor_tensor(out=T["ATb"], in0=T["t64"], in1=self.bm(Msk), op=ALU.mult),
                reads=["t64", "cst"], writes=["ATb"])
        self.kv_tm(c0)
        self.scale_k("khat", "sc_", T["sc_"])

        def o_mm(e, h, o):
            e.matmul(o, T["qT"][:, h, c0:c0 + L], T["Sb"][:, h, :], start=True, stop=False)
            return e.matmul(o, T["ATb"][:, h, :], T["vtm"][:, h, :], start=False, stop=True)
        self.mm8([4, 5], 64, 128, o_mm, ["qT", "Sb", "ATb", "vtm"])
        den = v1[0:64, 64:80].rearrange("p (h t) -> p h t", h=8)

        def d_mm(e):
            for h in range(8):
                e.matmul(den[:, h, :], T["qT"][:, h, c0:c0 + L], T["nb"][:, h, :], start=True, stop=False)
                r = e.matmul(den[:, h, :], T["ATb"][:, h, :], self.ones_bf[0:64, 0:2], start=False, stop=True)
            return r
        self.op("pe", d_mm, reads=["qT", "nb", "ATb", "ones_bf"], writes=["psb1"])
        self.op("dve", lambda e: e.tensor_scalar(out=T["sd_"], in0=den[:, :, 0], scalar1=-1.0, scalar2=None, op0=ALU.mult),
                reads=["psb1"], writes=["sd_"])
        self.op("dve", lambda e: e.tensor_tensor(out=T["sd_"], in0=T["sd_"], in1=den[:, :, 0], op=ALU.max),
                reads=["psb1", "sd_"], writes=["sd_"])
        self.op("dve", lambda e: e.tensor_tensor(out=T["sd_"], in0=T["sd_"], in1=T["sb_"], op=ALU.max),
                reads=["sd_", "sb_"], writes=["sd_"])
        self.op("dve", lambda e: e.reciprocal(out=T["sd_"], in_=T["sd_"]), reads=["sd_"], writes=["sd_"])
        self.evac8("dve", [4, 5], 64, 128, lambda e, v, hs: e.tensor_tensor(
            out=T["o"][:, hs, :], in0=v, in1=self.bc(T["sd_"][:, hs], 128), op=ALU.mult), ["sd_"], ["o_sb"])
        self.out_chunk(l, 0, d, c0, t0)
        self.mm8([6, 7], 128, 128, lambda e, h, o: e.matmul(o, T["khat"][:, h, :], T["vtm"][:, h, :], start=True,
                                                           stop=True), ["khat", "vtm"])
        dn = v1[0:128, 96:112].rearrange("p (h t) -> p h t", h=8)

        def n_mm(e):
            for h in range(8):
                r = e.matmul(dn[:, h, :], T["khat"][:, h, :], self.ones_bf[0:64, 0:2], start=True, stop=True)
            return r
        self.op("pe", n_mm, reads=["khat", "ones_bf"], writes=["psb1"])
        self.op("dve", lambda e: e.tensor_tensor(out=T["nS"], in0=T["nS"], in1=self.bc(T["eL"], 2), op=ALU.mult),
                reads=["nS", "eL"], writes=["nS"])
        self.op("dve", lambda e: e.tensor_tensor(out=T["nS"], in0=T["nS"], in1=dn, op=ALU.add), reads=["nS", "psb1"],
                writes=["nS"])
        self.op("act", lambda e: e.copy(out=T["nb"], in_=T["nS"]), reads=["nS"], writes=["nb"])
        self.state_update(None, "eL", T["eL"])

    def mlstm_state_out(self, l, d, si):
        T = self.T
        s0 = SEGS[si][0]
        gi, gf = T["gi"], T["gf"]
        n = 256
        P_ = self.tmpf[0][0:16, 0:n]
        E_ = self.tmpf[1][0:16, 0:n]
        tot = T["se_"][0:16, 0:1]
        mx = T["se_"][0:16, 1:2]
        rd = ["gi", "gf"]
        lfv, igv = gf[:, s0:s0 + n], gi[:, s0:s0 + n]
        self.op("dve", lambda e: e.tensor_tensor_scan(out=P_, data0=self.ones_f[0:16, 0:n], data1=lfv, initial=0.0,
                                                      op0=ALU.mult, op1=ALU.add), reads=rd + ["ones_f"], writes=["tmpf0"])
        self.op("dve", lambda e: e.tensor_copy(out=tot, in_=P_[:, n - 1:n]), reads=["tmpf0"], writes=["se_"])
        if d == 0:
            self.op("dve", lambda e: e.tensor_tensor(out=E_, in0=igv, in1=P_, op=ALU.subtract), reads=rd + ["tmpf0"],
                    writes=["tmpf1"])
            self.op("dve", lambda e: e.tensor_scalar(out=E_, in0=E_, scalar1=tot, scalar2=None, op0=ALU.add),
                    reads=["tmpf1", "se_"], writes=["tmpf1"])
        else:
            self.op("dve", lambda e: e.tensor_tensor(out=E_, in0=igv, in1=P_, op=ALU.add), reads=rd + ["tmpf0"],
                    writes=["tmpf1"])
            self.op("dve", lambda e: e.tensor_tensor(out=E_, in0=E_, in1=lfv, op=ALU.subtract), reads=rd + ["tmpf1"],
                    writes=["tmpf1"])
        self.op("dve", lambda e: e.tensor_reduce(out=mx, in_=E_, op=ALU.max, axis=AX.X), reads=["tmpf1"], writes=["se_"])
        self.op("dve", lambda e: e.tensor_tensor(out=mx, in0=mx, in1=tot, op=ALU.max), reads=["se_"], writes=["se_"])
        self.dma("sp", self.om[si, l, :].rearrange("(p o) -> p o", o=1)[8 * d:8 * d + 8, :], T["se_"][8 * d:8 * d + 8, 1:2],
                 reads=["se_"], writes=[("om", si, l, d)], allow_slow_non_contiguous=True)
        mrep = self.tmpf[2][0:16, 0:128]
        self.op("dve", lambda e: e.tensor_scalar(out=mrep, in0=self.ones_f[0:16, 0:128], scalar1=mx, scalar2=None, op0=ALU.mult),
                reads=["se_", "ones_f"], writes=["tmpf2"])
        v1 = self.psb[1]
        self.op("pe", lambda e: e.matmul(v1[0:128, 0:16], mrep, self.ident[0:16, 0:16], start=True, stop=True),
                reads=["tmpf2", "ident"], writes=["psb1"])
        self.op("act", _act(AF.Exp, T["m0b"], v1[0:128, 8 * d:8 * d + 8], scale=-1.0), reads=["psb1"], writes=["m0b"])
        self.op("dve", lambda e: e.tensor_tensor(out=T["S"], in0=T["S"], in1=self.bc(T["m0b"], 128), op=ALU.mult),
                reads=["S", "m0b"], writes=["S"])
        self.op("dve", lambda e: e.tensor_tensor(out=T["nS"], in0=T["nS"], in1=self.bc(T["m0b"], 2), op=ALU.mult),
                reads=["nS", "m0b"], writes=["nS"])
        self.dma("sp", self.oC[si, l, d].rearrange("h k e -> k h e"), T["S"], reads=["S"], writes=[("oC", si, l, d)])
        self.dma("sp", self.on[si, l, d].rearrange("h k -> k h"), T["nS"][:, :, 0], reads=["nS"], writes=[("on", si, l, d)],
                 allow_slow_non_contiguous=True)

    def step_delta(self, l, d, c0, t0):
        T = self.T
        g = T["gtm"]
        self.gate_tm("gb", "gg", c0)
        be, gg = g[:, 8 * d:8 * d + 8], g[:, 16 + 8 * d:16 + 8 * d + 8]
        v1 = self.psb[1]
        Tri = self._Tri
        INC = self.C("GE" if d == 0 else "LE")
        STR = self.C("GT" if d == 0 else "LT")
        self.op("dve", lambda e: e.tensor_tensor(out=T["t64"], in0=self.bc(gg, 64), in1=self.bm(STR), op=ALU.mult),
                reads=["gtm", "cst"], writes=["t64"])

        def f(e):
            e.matmul(v1[0:64, 32:40], Tri, gg, start=True, stop=True)
            return e.matmul(v1[0:128, 40:48], self.ones_f[0:64, 0:128], gg, start=True, stop=True)
        self.op("pe", f, reads=["gtm", "cst", "ones_f"], writes=["psb1"])
        self.op("dve", lambda e: e.tensor_copy(out=T["sa"], in_=v1[0:64, 32:40]), reads=["psb1"], writes=["sa"])
        self.op("act", _act(AF.Exp, T["sb_"], T["sa"]), reads=["sa"], writes=["sb_"])
        self.op("dve", lambda e: e.tensor_tensor(out=T["sc_"], in0=v1[0:64, 40:48], in1=T["sa"], op=ALU.subtract),
                reads=["psb1", "sa"], writes=["sc_"])
        self.op("act", _act(AF.Exp, T["sc_"], T["sc_"]), reads=["sc_"], writes=["sc_"])
        self.op("act", _act(AF.Exp, T["eL"], v1[0:128, 40:48]), reads=["psb1"], writes=["eL"])
        self.op("dve", lambda e: e.tensor_tensor(out=T["sd_"], in0=be, in1=T["sb_"], op=ALU.mult), reads=["gtm", "sb_"],
                writes=["sd_"])
        b0 = self.bank(0, 64, [8, 64])
        self.op("pe", lambda e: e.matmul(self.psb[0][0:64, :], Tri, T["t64"].rearrange("p h m -> p (h m)"), start=True,
                                         stop=True), reads=["t64", "cst"], writes=["psb0"])
        self.op("act", _act(AF.Exp, T["dec"], b0), reads=["psb0"], writes=["dec"])
        self.op("dve", lambda e: e.tensor_tensor(out=T["t64"], in0=T["dec"], in1=self.bm(STR), op=ALU.mult),
                reads=["dec", "cst"], writes=["t64"])
        self.op("dve", lambda e: e.tensor_tensor(out=T["dec"], in0=T["dec"], in1=self.bm(INC), op=ALU.mult),
                reads=["dec", "cst"], writes=["dec"])
        self.op("dve", lambda e: e.tensor_tensor(out=T["t64"], in0=T["t64"], in1=self.bc(be, 64), op=ALU.mult),
                reads=["t64", "gtm"], writes=["t64"])
        self.mm8([0], 64, 64, lambda e, h, o: e.matmul(o, T["kT"][:, h, c0:c0 + L], T["kT"][:, h, c0:c0 + L],
                                                       start=True, stop=True), ["kT"])
        self.evac8("dve", [0], 64, 64, lambda e, v, hs: e.tensor_tensor(out=T["A"], in0=v, in1=T["t64"], op=ALU.mult),
                   ["t64"], ["A"])
        self.mm8([1], 64, 64, lambda e, h, o: e.matmul(o, T["qT"][:, h, c0:c0 + L], T["kT"][:, h, c0:c0 + L],
                                                       start=True, stop=True), ["qT", "kT"])
        self.evac8("dve", [1], 64, 64, lambda e, v, hs: e.tensor_tensor(out=T["qk"], in0=v, in1=T["dec"], op=ALU.mult),
                   ["dec"], ["qk"])
        i64 = self.ident[0:64, 0:64]
        for src, bk_, dst, eng in (("A", 0, "AT", "dve"), ("qk", 1, "qkTb", "act")):
            vb = self.bank(bk_, 64, [8, 64])

            def tr(e, src=src, vb=vb):
                for h in range(8):
                    r = e.transpose(out=vb[:, h, :], in_=T[src][:, h, :], identity=i64)
                return r
            self.op("pe", tr, reads=[src, "ident"], writes=["psb%d" % bk_])
            if eng == "dve":
                self.op("dve", lambda e, vb=vb, dst=dst: e.tensor_copy(out=T[dst], in_=vb), reads=["psb%d" % bk_],
                        writes=[dst])
            else:
                self.op("act", lambda e, vb=vb, dst=dst: e.copy(out=T[dst], in_=vb), reads=["psb%d" % bk_], writes=[dst])
        I8 = self.bm(i64)
        self.op("dve", lambda e: e.tensor_tensor(out=T["W"], in0=T["A"], in1=self.bm(self.C("BM0")), op=ALU.mult),
                reads=["A", "cst"], writes=["W"])
        self.op("dve", lambda e: e.tensor_tensor(out=T["T"], in0=I8, in1=T["W"], op=ALU.subtract), reads=["W", "ident"],
                writes=["T"])
        self.op("dve", lambda e: e.tensor_tensor(out=T["W"], in0=T["AT"], in1=self.bm(self.C("BM0")), op=ALU.mult),
                reads=["AT", "cst"], writes=["W"])
        self.op("dve", lambda e: e.tensor_tensor(out=T["TT_"], in0=I8, in1=T["W"], op=ALU.subtract), reads=["W", "ident"],
                writes=["TT_"])
        okn = ["OkT", "dec"]

        def mask_level(k):
            nm = okn[k % 2]
            self.op("dve", lambda e, k=k, nm=nm: e.tensor_tensor(out=T[nm], in0=T["AT"], in1=self.bm(self.C("BM%d" % k)),
                                                                 op=ALU.mult), reads=["AT", "cst"], writes=[nm])
        mask_level(1)
        for k in range(1, 6):
            ok = okn[k % 2]
            self.mm8([0], 64, 64, lambda e, h, o, ok=ok: e.matmul(o, T[ok][:, h, :], T["T"][:, h, :], start=True,
                                                                  stop=True), [ok, "T"])
            if k < 5:
                mask_level(k + 1)
            self.evac8("act", [0], 64, 64, lambda e, v, hs: e.copy(out=T["W"], in_=v), [], ["W"])
            if k < 5:
                self.mm8([1], 64, 64, lambda e, h, o: e.matmul(o, T["TT_"][:, h, :], T["W"][:, h, :], start=True,
                                                               stop=True), ["TT_", "W"])
            self.mm8([3], 64, 64, lambda e, h, o: e.matmul(o, T["W"][:, h, :], T["TT_"][:, h, :], start=True, stop=True),
                     ["TT_", "W"])
            if k < 5:
                self.evac8("dve", [1], 64, 64, lambda e, v, hs: e.tensor_tensor(out=T["T"], in0=T["T"], in1=v,
                                                                                op=ALU.subtract), ["T"], ["T"])
            self.evac8("dve", [3], 64, 64, lambda e, v, hs: e.tensor_tensor(out=T["TT_"], in0=T["TT_"], in1=v,
                                                                            op=ALU.subtract), ["TT_"], ["TT_"])
        self.op("act", lambda e: e.copy(out=T["TTb"], in_=T["TT_"]), reads=["TT_"], writes=["TTb"])
        self.kv_tm(c0)
        self.scale_k("bk", "sd_", T["sd_"])
        self.scale_k("kend", "sc_", T["sc_"])
        self.op("dve", lambda e: e.tensor_tensor(out=T["bv"], in0=T["vtm"], in1=self.bc(be, 128), op=ALU.mult),
                reads=["vtm", "gtm"], writes=["bv"])
        self.mm8([4, 5], 64, 128, lambda e, h, o: e.matmul(o, T["TTb"][:, h, :], T["bv"][:, h, :], start=True, stop=True),
                 ["TTb", "bv"])
        self.evac8("act", [4, 5], 64, 128, lambda e, v, hs: e.copy(out=T["U0"][:, hs, :], in_=v), [], ["of_sb"])
        self.mm8([2], 128, 64, lambda e, h, o: e.matmul(o, T["bk"][:, h, :], T["TTb"][:, h, :], start=True, stop=True),
                 ["TTb", "bk"])
        self.evac8("act", [2], 128, 64, lambda e, v, hs: e.copy(out=T["WkT"], in_=v), [], ["WkT"])
        self.mm8([6, 7], 64, 128, lambda e, h, o: e.matmul(o, T["WkT"][:, h, :], T["Sb"][:, h, :], start=True, stop=True),
                 ["WkT", "Sb"])
        self.evac8("dve", [6, 7], 64, 128, lambda e, v, hs: e.tensor_tensor(out=T["Ub"][:, hs, :], in0=T["U0"][:, hs, :],
                                                                             in1=v, op=ALU.subtract), ["of_sb"], ["Ub"])
        self.mm8([4, 5], 64, 128, lambda e, h, o: e.matmul(o, T["qT"][:, h, c0:c0 + L], T["Sb"][:, h, :], start=True,
                                                          stop=True), ["qT", "Sb"])
        self.evac8("dve", [4, 5], 64, 128, lambda e, v, hs: e.tensor_tensor(
            out=T["o"][:, hs, :], in0=v, in1=self.bc(T["sb_"][:, hs], 128), op=ALU.mult), ["sb_"], ["o_sb"])
        self.mm8([6, 7], 64, 128, lambda e, h, o: e.matmul(o, T["qkTb"][:, h, :], T["Ub"][:, h, :], start=True, stop=True),
                 ["qkTb", "Ub"])
        self.evac8("dve", [6, 7], 64, 128, lambda e, v, hs: e.tensor_tensor(out=T["o"][:, hs, :], in0=T["o"][:, hs, :],
                                                                             in1=v, op=ALU.add), ["o_sb"], ["o_sb"])
        self.mm8([6, 7], 128, 128, lambda e, h, o: e.matmul(o, T["kend"][:, h, :], T["Ub"][:, h, :], start=True, stop=True),
                 ["kend", "Ub"])
        self.state_update(None, "eL", T["eL"])
        self.out_chunk(l, 1, d, c0, t0)

    def select_own(self):
        rm = self.sb("rmask_sb", [128, 4], F32)
        self.dma("sp", rm[:], self.rmask_in, writes=["rmask"])
        xO = self.dram("xO", [KC, 128, TG])
        YSO = self.dram("YSO", [3, 8, 128, TG], BF16)
        GMO = self.dram("GMO", [48, 128, TG], BF16)
        hf = self.hraw[:]
        self.op("dve", lambda e: e.memset(hf[:, 0:2], 0.0), writes=["hT"])
        ld = [hf[:, i * 1024:(i + 1) * 1024] for i in range(4)]
        ac = [hf[:, (4 + i) * 1024:(5 + i) * 1024] for i in range(2)]
        for i in range(4):
            self.alias("selL%d" % i, "hT")
        for i in range(2):
            self.alias("selA%d" % i, "hT")
        cnt = [0, 0]

        def sel(src, dst, dt, rkeys, wkeys):
            n = 1024
            def view(t, w):
                return t[:, 0:w] if dt == F32 else t[:, 0:w // 2].bitcast(BF16)
            li = cnt[0] % 4
            cnt[0] += 1
            self.dma("sp", view(ld[li], 512), src[:, 0:512], reads=[rkeys(0)], writes=["selL%d" % li])
            self.dma("sp", dst[:, 0:512], view(ld[li], 512), reads=["selL%d" % li], writes=[wkeys(0)])
            ai = cnt[1] % 2
            cnt[1] += 1
            a = view(ac[ai], n)
            for q in range(4):
                li = cnt[0] % 4
                cnt[0] += 1
                t = view(ld[li], n)
                c0 = 512 + q * n
                self.dma("sp", t, src[:, c0:c0 + n], reads=[rkeys(c0 // TT), rkeys(c0 // TT + 1)],
                         writes=["selL%d" % li])
                if q == 0:
                    self.op("dve", lambda e, t=t, a=a: e.tensor_scalar(out=a, in0=t, scalar1=rm[:, 0:1], scalar2=None,
                                                                       op0=ALU.mult),
                            reads=["selL%d" % li, "rmask"], writes=["selA%d" % ai])
                else:
                    self.op("dve", lambda e, t=t, a=a, q=q: e.scalar_tensor_tensor(
                        out=a, in0=t, scalar=rm[:, q:q + 1], in1=a, op0=ALU.mult, op1=ALU.add),
                        reads=["selL%d" % li, "selA%d" % ai, "rmask"], writes=["selA%d" % ai])
            self.dma("sp", dst[:, 512:512 + n], a, reads=["selA%d" % ai], writes=[wkeys(1), wkeys(2)])
        for k in range(KC):
            sel(self.xT[k], xO[k], F32, lambda g: ("xT", g), lambda g: ("xO", g))
        for m in range(3):
            for h in range(8):
                sel(self.YS[m, h], YSO[m, h], BF16, lambda g, m=m: (("YS", m), g), lambda g, m=m: (("YSO", m), g))
        for j in range(48):
            sel(self.GM[j], GMO[j], BF16, lambda g, j=j: (("GM", j), g), lambda g, j=j: (("GMO", j), g))
        self.xT, self.YS, self.GM = xO, YSO, GMO
        self.kx, self.kys, self.kgm = "xO", "YSO", "GMO"

    def mixer(self, l, do_m3=True):
        if l == 0:
            self.mixer_setup()
        self.mixer_params(l)
        self.conv_prep(l)
        for tg in range(NTG):
            self.norm_to_hT(tg, l, 1)
            self.m1_project(l, tg)
            self.bg_drain()
            self.bg = self.conv_gen(l, tg)
        self.bg_drain()
        self.scan_setup()
        for m in range(3):
            self.scan(l, m)
        if do_m3:
            for tg in range(NTG):
                self.m3_merge(l, tg)


_CACHE = {}


def _get_nc(cfg_key):
    if cfg_key not in _CACHE:
        k = Kern(dict(cfg_key))
        _CACHE[cfg_key] = k.build()
    return _CACHE[cfg_key]


def kernel(x_prompt, x_sample, state_mlstm_C, state_mlstm_n, state_mlstm_m, state_delta_S, state_ret_S,
           c, c_ctx, norm_g, final_norm_g, w_ada, b_ada, w_in, b_in, mlstm_f_bias, conv_w, delta_A_log,
           delta_dt_bias, ret_log_gamma, head_norm_g, w_br, w_out, ffn_w13, ffn_w2, _cfg=None):
    cfg = dict(_cfg or {})
    nc = _get_nc(tuple(sorted(cfg.items())))
    f = lambda a: np.ascontiguousarray(np.asarray(a, dtype=np.float32))
    in_maps = []
    ident = np.eye(128, dtype=np.float32)
    full = cfg.get("mixer", True)
    rc, rs = _rope_np()
    for cidx in range(8):
        b = cidx // 4
        x_tok = np.concatenate([x_prompt[2 * cidx], x_prompt[2 * cidx + 1], x_sample[b]], axis=0)
        cvec = np.stack([c_ctx, c[b]], axis=0)
        m = {
            "x_tok": f(x_tok), "cvec": f(cvec), "norm_g": f(norm_g), "final_norm_g": f(final_norm_g),
            "w_ada": f(w_ada), "b_ada": f(b_ada), "w_in": f(w_in), "b_in": f(b_in), "w_br": f(w_br),
            "w_out": f(w_out), "ffn_w13": f(ffn_w13), "ffn_w2": f(ffn_w2), "ident": ident,
        }
        rmk = np.zeros((128, 4), np.float32)
        rmk[:, cidx % 4] = 1.0
        m["rmask"] = rmk
        if full:
            m.update({
                "cst": _CST_NP, "ropeC": rc, "ropeS": rs,
                "mlstm_f_bias": f(mlstm_f_bias).reshape(DEPTH, 16), "conv_w": f(conv_w),
                "delta_A_log": f(delta_A_log).reshape(DEPTH, 16), "delta_dt_bias": f(delta_dt_bias).reshape(DEPTH, 16),
                "ret_log_gamma": f(ret_log_gamma).reshape(DEPTH, 16), "head_norm_g": f(head_norm_g),
                "st_C": f(state_mlstm_C[b]), "st_n": f(state_mlstm_n[b]), "st_m": f(state_mlstm_m[b]).reshape(DEPTH, 16),
                "st_D": f(state_delta_S[b]), "st_R": f(state_ret_S[b]),
            })
        in_maps.append(m)
    res = run_bass_kernel_spmd(nc, in_maps, core_ids=list(range(8)))
    outs = res.results
    y_prompt = np.zeros((16, 256, D), np.float32)
    y_sample = np.zeros((2, 4096, D), np.float32)
    z = lambda *s: np.zeros(s, np.float32)
    nC, nn, nm, nD, nR = z(16, 2, 2, 8, 128, 128), z(16, 2, 2, 8, 128), z(16, 2, 2, 8), z(16, 2, 2, 8, 128, 128), z(16, 2, 2, 8, 128, 128)
    for cidx in range(8):
        y = outs[cidx]["y_tok"]
        y_prompt[2 * cidx] = y[0:256]
        y_prompt[2 * cidx + 1] = y[256:512]
        r = cidx % 4
        y_sample[cidx // 4, 1024 * r:1024 * (r + 1)] = y[512:1536]
        if full:
            for si in range(2):
                nC[2 * cidx + si] = outs[cidx]["o_C"][si]
                nn[2 * cidx + si] = outs[cidx]["o_n"][si]
                nm[2 * cidx + si] = outs[cidx]["o_m"][si].reshape(DEPTH, 2, 8)
                nD[2 * cidx + si] = outs[cidx]["o_D"][si]
                nR[2 * cidx + si] = outs[cidx]["o_R"][si]
    return (y_prompt, y_sample, nC, nn, nm, nD, nR)
```

```python
import numpy as np
from contextlib import ExitStack
import concourse.bass as bass
import concourse.mybir as mybir
from concourse.bass_utils import run_bass_kernel_spmd

F32 = mybir.dt.float32
BF16 = mybir.dt.bfloat16
AF = mybir.ActivationFunctionType
ALU = mybir.AluOpType
AX = mybir.AxisListType

D = 2048
KC = 16
DEPTH = 2
HD = 128
NH = 8
WM = 1024
DFF = 4096
NMOD = 9
EPS = 1e-6
L = 64
N_IN = 18496
OFF = dict(qA=0, kA=1024, vA=2048, oA=3072, iA=4096, fA=4112, qkvB=4128, zB=7200, betaB=8224, aB=8240,
           qC=8256, kC=9280, vC=10304, gC=11328, gm=12352)
SEGS = [(0, 256), (256, 256), (512, 4096)]
TC = 4608
TT = 512
NTT = TC // TT
TG = 1536
NTG = TC // TG


def _build_cst():
    r = np.arange(64)
    LE = (r[:, None] <= r[None, :]).astype(np.float32)
    M = {"LE": LE, "GE": LE.T.copy(), "LT": (r[:, None] < r[None, :]).astype(np.float32),
         "GT": (r[:, None] > r[None, :]).astype(np.float32)}
    for k in range(6):
        M["BM%d" % k] = ((r[:, None] >> (k + 1) == r[None, :] >> (k + 1)) & (r[:, None] >> k != r[None, :] >> k)).astype(np.float32)
    cols = {}
    arr = []
    off = 0
    for n, m in M.items():
        a = np.zeros((128, 64), np.float32); a[:64] = m
        arr.append(a); cols[n] = (off, 64); off += 64
    for n, v in (("p1f", r + 1.0), ("p1b", 64.0 - r), ("p2f", 63.0 - r), ("p2b", r * 1.0)):
        a = np.zeros((128, 1), np.float32); a[:64, 0] = v
        arr.append(a); cols[n] = (off, 1); off += 1
    Rm = np.zeros((128, 128), np.float32)
    for dp in range(64):
        Rm[dp + 64, dp] = -1.0
        Rm[dp, dp + 64] = 1.0
    arr.append(Rm); cols["Rm"] = (off, 128); off += 128
    return np.concatenate(arr, axis=1), cols


_CST_NP, CST = _build_cst()
NCST = _CST_NP.shape[1]


def _rope_np():
    T = 4096
    pos_r = np.repeat(np.arange(T // 64), 64).astype(np.float32)
    pos_c = np.tile(np.arange(64), T // 64).astype(np.float32)
    nf = HD // 4
    freqs = (10000.0 ** (-np.arange(nf, dtype=np.float32) / nf)).astype(np.float32)
    ang = np.concatenate([pos_r[:, None] * freqs, pos_c[:, None] * freqs], axis=-1)
    ang = np.concatenate([ang, ang], axis=-1)
    return np.ascontiguousarray(np.cos(ang).T.astype(np.float32)), np.ascontiguousarray(np.sin(ang).T.astype(np.float32))


class Res:
    __slots__ = ("name", "w", "rs", "parent", "kids")

    def __init__(self, name):
        self.name = name
        self.w = None
        self.rs = []
        self.parent = None
        self.kids = []


class Sched:
    ENGS = ("pe", "act", "dve", "pool", "sp")

    def __init__(self, nc, es):
        self.nc = nc
        self.es = es
        self.ops = []
        self.per_eng = {e: [] for e in self.ENGS}

    def _deps(self, reads, writes, idx):
        deps = set()
        for r in reads:
            if r.w is not None:
                deps.add(r.w)
            if r.parent is not None and r.parent.w is not None:
                deps.add(r.parent.w)
            for k in r.kids:
                if k.w is not None:
                    deps.add(k.w)
        for w in writes:
            rel = [w] + w.kids + ([w.parent] if w.parent is not None else [])
            for x in rel:
                if x.w is not None:
                    deps.add(x.w)
                deps.update(x.rs)
        for r in reads:
            r.rs.append(idx)
        for w in writes:
            w.w = idx
            w.rs = []
        deps.discard(idx)
        return deps

    def op(self, eng, fn, reads=(), writes=()):
        idx = len(self.ops)
        deps = self._deps(reads, writes, idx)
        self.ops.append((eng, fn, deps, False))
        return idx

    def dma(self, eng, fn, reads=(), writes=()):
        idx = len(self.ops)
        deps = self._deps(reads, writes, idx)
        self.ops.append((eng, fn, deps, True))
        return idx

    def emit(self, final_waits=True):
        nc, es = self.nc, self.es
        NDS = {"sp": 24, "pool": 12, "act": 4}
        sem = {e: es.enter_context(nc.semaphore("s_" + e)) for e in ("pe", "act", "dve", "pool")}
        dsem = {q: [es.enter_context(nc.semaphore("d_%s%d" % (q, i))) for i in range(n)] for q, n in NDS.items()}
        cnt = {e: 0 for e in sem}
        dcnt = {q: [0] * n for q, n in NDS.items()}
        dnext = {q: 0 for q in NDS}
        handle = {}
        known = {e: {} for e in self.ENGS}
        prog = {e: [] for e in self.ENGS}
        last_dma = {}

        def need(eng, s, v):
            k = known[eng]
            if k.get(id(s), 0) < v:
                k[id(s)] = v
                prog[eng].append(("wait", s, v))

        for idx, (eng, fn, deps, is_dma) in enumerate(self.ops):
            for d in sorted(deps):
                deng = self.ops[d][0]
                if (not self.ops[d][3]) and deng == eng and eng == "pe":
                    continue
                s, v = handle[d]
                need(eng, s, v)
            if is_dma:
                q = eng
                slot = dnext[q]
                dnext[q] = (slot + 1) % len(dsem[q])
                s = dsem[q][slot]
                if dcnt[q][slot] > 0:
                    need(eng, s, 16 * dcnt[q][slot])
                dcnt[q][slot] += 1
                v = 16 * dcnt[q][slot]
                handle[idx] = (s, v)
                prog[eng].append(("dma", fn, s))
                last_dma[(q, slot)] = (s, v)
            else:
                cnt[eng] += 1
                handle[idx] = (sem[eng], cnt[eng])
                prog[eng].append(("op", fn, sem[eng]))
        for (q, slot), (s, v) in last_dma.items():
            need("sp", s, v)
        for e in ("pe", "act", "dve", "pool"):
            if cnt[e]:
                need("sp", sem[e], cnt[e])

        block = es.enter_context(nc.Block())

        def run(engobj, items):
            for it in items:
                if it[0] == "wait":
                    engobj.wait_ge(it[1], it[2])
                elif it[0] == "dma":
                    it[1](engobj).then_inc(it[2], 16)
                else:
                    it[1](engobj).then_inc(it[2], 1)

        @block.sync
        def _(e):
            run(e, prog["sp"])

        @block.gpsimd
        def _(e):
            run(e, prog["pool"])

        @block.scalar
        def _(e):
            run(e, prog["act"])

        @block.vector
        def _(e):
            run(e, prog["dve"])

        @block.tensor
        def _(e):
            run(e, prog["pe"])


class Builder:
    def __init__(self, cfg):
        self.cfg = cfg
        self.nc = bass.Bass("TRN2", target_bir_lowering=False)
        self.es = ExitStack()
        self.S = Sched(self.nc, self.es)
        self.res = {}
        self.tiles = {}

    def R(self, key):
        r = self.res.get(key)
        if r is None:
            r = self.res[key] = Res(str(key))
        return r

    def alias(self, child, parent):
        c, p = self.R(child), self.R(parent)
        c.parent = p
        p.kids.append(c)

    def sb(self, name, shape, dt=F32):
        t = self.es.enter_context(self.nc.sbuf_tensor(name, list(shape), dt))
        self.tiles[name] = t
        return t

    def ps(self, name, shape, dt=F32):
        t = self.es.enter_context(self.nc.psum_tensor(name, list(shape), dt))
        self.tiles[name] = t
        return t

    def dram(self, name, shape, dt=F32, kind="Internal"):
        return self.nc.dram_tensor(name, list(shape), dt, kind=kind).ap()

    def op(self, eng, fn, reads=(), writes=()):
        return self.S.op(eng, fn, [self.R(r) for r in reads], [self.R(w) for w in writes])

    def dma(self, q, out, in_, reads=(), writes=(), **kw):
        return self.S.dma(q, lambda e: e.dma_start(out=out, in_=in_, **kw),
                          [self.R(r) for r in reads], [self.R(w) for w in writes])


def _act(func, out, in_, **kw):
    return lambda e: e.activation(out=out, in_=in_, func=func, **kw)


class Kern(Builder):
    def build(self):
        nc = self.nc
        cfg = self.cfg
        di = lambda n, s: nc.dram_tensor(n, list(s), F32, kind="ExternalInput").ap()
        do = lambda n, s: nc.dram_tensor(n, list(s), F32, kind="ExternalOutput").ap()
        self.x_tok = di("x_tok", [TC, D])
        self.cvec = di("cvec", [2, D])
        self.norm_g = di("norm_g", [DEPTH, 3, D])
        self.final_g = di("final_norm_g", [D])
        self.w_ada = di("w_ada", [DEPTH, D, NMOD * D])
        self.b_ada = di("b_ada", [DEPTH, NMOD * D])
        self.w_in = di("w_in", [DEPTH, D, N_IN])
        self.b_in = di("b_in", [DEPTH, N_IN])
        self.w_br = di("w_br", [DEPTH, 3, WM, D])
        self.w_out = di("w_out", [DEPTH, D, D])
        self.w13 = di("ffn_w13", [DEPTH, 2, D, 2 * DFF])
        self.w2 = di("ffn_w2", [DEPTH, 2, DFF, D])
        self.ident_in = di("ident", [128, 128])
        self.y_tok = do("y_tok", [TG, D])
        self.rmask_in = di("rmask", [128, 4])
        self.xT = self.dram("xT", [KC, 128, TC])

        self.ident = self.sb("ident_sb", [128, 128], F32)
        self.dma("sp", self.ident[:], self.ident_in, writes=["ident"])
        self.ones_bf = self.sb("ones_bf", [128, 128], BF16)
        self.op("dve", lambda e: e.memset(self.ones_bf[:], 1.0), writes=["ones_bf"])
        self.eps_t = self.sb("eps_t", [128, 1], F32)
        self.op("dve", lambda e: e.memset(self.eps_t[:], EPS), writes=["eps_t"])

        self.NWR = 4
        self.wr = [self.sb("wr%d" % i, [128, KC, 128], BF16) for i in range(self.NWR)]
        self.wr_i = 0
        self.kx, self.kys, self.kgm = "xT", "YS", "GM"
        self.bg = None
        self.bg_n = 0
        self._start_conv = False
        self._conv_args = None
        self.NPS = 4
        self.psb = [self.ps("psb%d" % i, [128, 512], F32) for i in range(8)]
        self.ps_i = 0
        hraw = self.sb("hT", [128, KC * TG // 2], F32)
        self.hraw = hraw
        self.hT = hraw[:].bitcast(BF16).rearrange("p (k t) -> p k t", k=KC)

        graw = self.sb("gT", [128, KC * TG // 2], F32)
        self.graw = graw
        gf = graw[:]
        self.gT = graw[:].bitcast(BF16).rearrange("p (k t) -> p k t", k=KC)
        self._xl = [gf[:, i * 2048:(i + 1) * 2048] for i in range(2)]
        self._xo = [gf[:, 4096 + i * 2048:4096 + (i + 1) * 2048].rearrange("p (k t) -> p k t", k=KC) for i in range(2)]
        self._yn = gf[:, 0:8192].rearrange("p (k t) -> p k t", k=KC)
        self._yo = [gf[:, 8192 + i * 2048:8192 + (i + 1) * 2048] for i in range(2)]
        for nm in ("xl0", "xl1", "xo0", "xo1", "yn", "yo0", "yo1"):
            self.alias(nm, "gT")
        self.xq = [self.sb("xq%d" % i, [128, 4, TT], F32) for i in range(3)]
        self.xq_i = 0
        self.sq = [self.sb("sq%d" % i, [128, 4, TT], BF16) for i in range(2)]
        self.rstd = self.sb("rstd", [128, TT], F32)
        self.tmpf = [self.sb("tmpf%d" % i, [128, TT], F32) for i in range(3)]
        self.tmp_i = 0
        self.xr = [self.sb("xr%d" % i, [128, TT], F32) for i in range(4)]
        self.xr_i = 0

        self.phase_load_x()
        self.phase_adaln()
        for l in range(DEPTH):
            for tg in range(NTG):
                self.norm_to_hT(tg, l, 0)
                self.ffn(tg, l, 0)
            last = (l == DEPTH - 1) and cfg.get("mixer", True)
            if cfg.get("mixer", True):
                self.mixer(l, do_m3=not last)
            if last:
                self.select_own()
                self.norm_to_hT(0, l, 1)
                self.m1_project(l, 0, [i for i, b in enumerate(self._B) if b[3][0] == "GM"], conv_bg=False)
                self.m3_merge(l, 0)
                self.norm_to_hT(0, l, 2)
                self.ffn(0, l, 1)
            else:
                for tg in range(NTG):
                    self.norm_to_hT(tg, l, 2)
                    self.ffn(tg, l, 1)
        self.phase_final(TG // TT if cfg.get("mixer", True) else TG // TT)
        self.S.emit()
        return nc

    def next_ps(self):
        i = self.ps_i
        self.ps_i = (i + 1) % self.NPS
        return i

    def next_tmp(self):
        i = self.tmp_i
        self.tmp_i = (i + 1) % len(self.tmpf)
        return i

    def phase_load_x(self):
        xl, xo = self._xl, self._xo
        for t in range(TC // 128):
            b = t % 2
            self.dma("sp", xl[b], self.x_tok[t * 128:(t + 1) * 128, :], writes=["xl%d" % b])
            for q in range(4):
                pi = self.next_ps()
                ps = self.psb[pi]

                def tr(e, b=b, q=q, ps=ps):
                    for j in range(4):
                        k = q * 4 + j
                        r = e.transpose(out=ps[:, j * 128:(j + 1) * 128], in_=xl[b][:, k * 128:(k + 1) * 128],
                                        identity=self.ident[:])
                    return r
                self.op("pe", tr, reads=["xl%d" % b, "ident"], writes=["psb%d" % pi])
                self.op("dve", lambda e, b=b, q=q, ps=ps: e.tensor_copy(
                    out=xo[b][:, q * 4:(q + 1) * 4, :], in_=ps[:].rearrange("p (a b) -> p a b", a=4)),
                    reads=["psb%d" % pi], writes=["xo%d" % b])
            self.dma("sp", self.xT[:, :, t * 128:(t + 1) * 128].rearrange("k p t -> p k t"), xo[b],
                     reads=["xo%d" % b], writes=[("xT", t // 4)])

    def load_vecT(self, name, src_1d, nblk):
        t = self.sb(name, [128, nblk], F32)
        self.dma("sp", t[:], src_1d.rearrange("(j p) -> p j", p=128), writes=[name],
                 allow_slow_non_contiguous=True)
        return t

    def phase_adaln(self):
        nc = self.nc
        cT = self.sb("cT", [128, KC, 2], F32)
        for b in range(2):
            self.dma("sp", cT[:, :, b], self.cvec[b].rearrange("(k p) -> p k", p=128), writes=["cT"],
                     allow_slow_non_contiguous=True)
        cS = self.sb("cS", [128, KC, 2], BF16)
        self.op("act", _act(AF.Silu, cS[:], cT[:]), reads=["cT"], writes=["cS"])
        self.mod = []
        self.nsc, self.nbi, self.gsc = {}, {}, {}
        for l in range(DEPTH):
            bT = self.load_vecT("badaT%d" % l, self.b_ada[l], NMOD * KC)
            gT_ = [self.load_vecT("ng%d_%d" % (l, i), self.norm_g[l, i], KC) for i in range(3)]
            mod = self.sb("mod%d" % l, [128, NMOD * KC, 2], F32)
            self.mod.append(mod)

            def evac(blk, tt, ps, pi, mod=mod, bT=bT, l=l):
                self.op("dve", lambda e: e.tensor_scalar(out=mod[:, blk, :], in0=ps[:, 0:2], scalar1=bT[:, blk:blk + 1],
                                                         scalar2=None, op0=ALU.add),
                        reads=["psb%d" % pi, "badaT%d" % l], writes=["mod%d" % l])
            blocks = [(self.w_ada[l][:, j * 128:(j + 1) * 128], KC, 128) for j in range(NMOD * KC)]
            self.linear(blocks, lambda k, tt: cS[:, k, :], ["cS"], [2], evac)
            for i in range(3):
                sc = self.sb("nsc%d_%d" % (l, i), [128, KC, 2], F32)
                bi = self.sb("nbi%d_%d" % (l, i), [128, KC, 2], F32)
                gs = self.sb("gsc%d_%d" % (l, i), [128, KC, 2], F32)
                m0 = mod[:, (3 * i) * KC:(3 * i + 1) * KC, :]
                m1 = mod[:, (3 * i + 1) * KC:(3 * i + 2) * KC, :]
                m2 = mod[:, (3 * i + 2) * KC:(3 * i + 3) * KC, :]
                for b in range(2):
                    self.op("dve", lambda e, sc=sc, m1=m1, b=b, g=gT_[i]: e.scalar_tensor_tensor(
                        out=sc[:, :, b], in0=m1[:, :, b], scalar=1.0, in1=g[:], op0=ALU.add, op1=ALU.mult),
                        reads=["mod%d" % l, "ng%d_%d" % (l, i)], writes=["nsc%d_%d" % (l, i)])
                self.op("dve", lambda e, bi=bi, m0=m0: e.tensor_copy(out=bi[:], in_=m0), reads=["mod%d" % l],
                        writes=["nbi%d_%d" % (l, i)])
                fac = 1.0 if i == 1 else 0.5
                self.op("dve", lambda e, gs=gs, m2=m2, fac=fac: e.tensor_scalar(
                    out=gs[:], in0=m2, scalar1=fac, scalar2=None, op0=ALU.mult), reads=["mod%d" % l],
                    writes=["gsc%d_%d" % (l, i)])
                self.nsc[(l, i)], self.nbi[(l, i)], self.gsc[(l, i)] = sc, bi, gs

    def linear(self, blocks, in_fn, in_res, ntoks, evac, pf=3):
        nb = len(blocks)
        slots = {}

        def issue(j):
            ap, kc, ncols = blocks[j]
            s = self.wr_i
            self.wr_i = (s + 1) % self.NWR
            slots[j] = s
            self.dma("pool", self.wr[s][:, 0:kc, 0:ncols], ap.rearrange("(k p) n -> p k n", p=128),
                     writes=["wr%d" % s])
        for j in range(min(pf, nb)):
            issue(j)
        for j in range(nb):
            if j + pf < nb:
                issue(j + pf)
            ap, kc, ncols = blocks[j]
            s = slots[j]
            for tt, nt in enumerate(ntoks):
                pi = self.next_ps()
                ps = self.psb[pi]

                def mm(e, s=s, kc=kc, ncols=ncols, tt=tt, nt=nt, ps=ps):
                    for k in range(kc):
                        r = e.matmul(ps[0:ncols, 0:nt], self.wr[s][:, k, 0:ncols], in_fn(k, tt),
                                     start=(k == 0), stop=(k == kc - 1))
                    return r
                self.op("pe", mm, reads=["wr%d" % s] + list(in_res), writes=["psb%d" % pi])
                evac(j, tt, ps, pi)
                if self._start_conv:
                    self._start_conv = False
                    self.bg_drain()
                    self.bg = self.conv_gen(*self._conv_args)
                if self.bg is not None:
                    self.bg_n += 1
                    if self.bg_n % 3 == 0:
                        self.bg_step()

    def bg_step(self):
        try:
            next(self.bg)
        except StopIteration:
            self.bg = None

    def bg_drain(self):
        while self.bg is not None:
            self.bg_step()

    def cvi(self, tg, tt):
        return 0 if (tg == 0 and tt == 0) else 1

    def load_xq(self, t0, q):
        xi = self.xq_i
        self.xq_i = (xi + 1) % len(self.xq)
        self.dma("sp", self.xq[xi][:], self.xT[q * 4:(q + 1) * 4, :, t0:t0 + TT].rearrange("k p t -> p k t"),
                 reads=[(self.kx, t0 // TT)], writes=["xq%d" % xi])
        return xi

    def rstd_tile(self, t0):
        pi = self.next_ps()
        ps = self.psb[pi]
        for q in range(4):
            xi = self.load_xq(t0, q)
            sqt = self.sq[q % 2]
            self.op("act", _act(AF.Square, sqt[:], self.xq[xi][:]), reads=["xq%d" % xi], writes=["sq%d" % (q % 2)])

            def mm(e, ps=ps, q=q, sqt=sqt):
                for k in range(4):
                    r = e.matmul(ps[:, :], self.ones_bf[:], sqt[:, k, :], start=(q == 0 and k == 0),
                                 stop=(q == 3 and k == 3))
                return r
            self.op("pe", mm, reads=["sq%d" % (q % 2), "ones_bf"], writes=["psb%d" % pi])
        self.op("act", _act(AF.Sqrt, self.rstd[:], ps[:, :], scale=1.0 / D, bias=self.eps_t[:]),
                reads=["psb%d" % pi, "eps_t"], writes=["rstd"])
        self.op("dve", lambda e: e.reciprocal(out=self.rstd[:], in_=self.rstd[:]), reads=["rstd"], writes=["rstd"])

    def norm_to_hT(self, tg, l, i):
        sc, bi = self.nsc[(l, i)], self.nbi[(l, i)]
        for tt in range(TG // TT):
            t0 = tg * TG + tt * TT
            self.rstd_tile(t0)
            cv = self.cvi(tg, tt)
            for q in range(4):
                xi = self.load_xq(t0, q)
                for kk in range(4):
                    k = q * 4 + kk
                    ti = self.next_tmp()
                    tm = self.tmpf[ti]
                    self.op("dve", lambda e, kk=kk, tm=tm, xi=xi: e.tensor_tensor(
                        out=tm[:], in0=self.xq[xi][:, kk, :], in1=self.rstd[:], op=ALU.mult),
                        reads=["xq%d" % xi, "rstd"], writes=["tmpf%d" % ti])
                    self.op("act", _act(AF.Identity, self.hT[:, k, tt * TT:(tt + 1) * TT], tm[:],
                                        scale=sc[:, k, cv:cv + 1], bias=bi[:, k, cv:cv + 1]),
                            reads=["tmpf%d" % ti, "nsc%d_%d" % (l, i), "nbi%d_%d" % (l, i)], writes=["hT"])

    def resid_evac(self, tg, gs, gs_name):
        def evac(blk, tt, ps, pi):
            t0 = tg * TG + tt * TT
            gtt = t0 // TT
            xi = self.xr_i
            self.xr_i = (xi + 1) % len(self.xr)
            xr = self.xr[xi]
            cv = self.cvi(tg, tt)
            self.dma("sp", xr[:], self.xT[blk, :, t0:t0 + TT], reads=[(self.kx, gtt)], writes=["xr%d" % xi])
            self.op("dve", lambda e: e.scalar_tensor_tensor(out=xr[:], in0=ps[:, :], scalar=gs[:, blk, cv:cv + 1],
                                                            in1=xr[:], op0=ALU.mult, op1=ALU.add),
                    reads=["psb%d" % pi, "xr%d" % xi, gs_name], writes=["xr%d" % xi])
            self.dma("sp", self.xT[blk, :, t0:t0 + TT], xr[:], reads=["xr%d" % xi], writes=[(self.kx, gtt)])
        return evac

    def ffn(self, tg, l, which):
        i = 0 if which == 0 else 2
        w13 = self.w13[l, which]
        w2 = self.w2[l, which]
        gs = self.gsc[(l, i)]
        NJ = DFF // 128 // 2
        for half in range(2):
            blocks = []
            for jj in range(NJ):
                j = half * NJ + jj
                blocks.append((w13[:, j * 128:(j + 1) * 128], KC, 128))
                blocks.append((w13[:, DFF + j * 128:DFF + (j + 1) * 128], KC, 128))
            sa = {}

            def evac13(blk, tt, ps, pi, sa=sa):
                jj, isb = blk // 2, blk % 2
                if not isb:
                    ti = self.next_tmp()
                    sa[tt] = ti
                    self.op("act", _act(AF.Silu, self.tmpf[ti][:], ps[:, :]), reads=["psb%d" % pi],
                            writes=["tmpf%d" % ti])
                else:
                    ti = sa[tt]
                    self.op("dve", lambda e: e.tensor_tensor(out=self.gT[:, jj, tt * TT:(tt + 1) * TT],
                                                             in0=self.tmpf[ti][:], in1=ps[:, :], op=ALU.mult),
                            reads=["psb%d" % pi, "tmpf%d" % ti], writes=["gT"])
            self.linear(blocks, lambda k, tt: self.hT[:, k, tt * TT:(tt + 1) * TT], ["hT"], [TT] * 3, evac13)
            r0 = half * (DFF // 2)
            blocks2 = [(w2[r0:r0 + DFF // 2, j * 128:(j + 1) * 128], KC, 128) for j in range(KC)]
            self.linear(blocks2, lambda k, tt: self.gT[:, k, tt * TT:(tt + 1) * TT], ["gT"], [TT] * 3,
                        self.resid_evac(tg, gs, "gsc%d_%d" % (l, i)))

    def phase_final(self, ntt):
        fg = self.load_vecT("fgT", self.final_g, KC)
        yo, yn = self._yo, self._yn
        for gtt in range(ntt):
            t0 = gtt * TT
            self.rstd_tile(t0)
            for q in range(4):
                xi = self.load_xq(t0, q)
                for kk in range(4):
                    k = q * 4 + kk
                    self.op("dve", lambda e, k=k, kk=kk, xi=xi: e.scalar_tensor_tensor(
                        out=yn[:, k, :], in0=self.xq[xi][:, kk, :], scalar=fg[:, k:k + 1], in1=self.rstd[:],
                        op0=ALU.mult, op1=ALU.mult), reads=["xq%d" % xi, "rstd", "fgT"], writes=["yn"])
            for s_ in range(TT // 128):
                b = (gtt * 4 + s_) % 2
                for q in range(4):
                    pi = self.next_ps()
                    ps = self.psb[pi]

                    def tr(e, q=q, ps=ps, s_=s_):
                        for j in range(4):
                            k = q * 4 + j
                            r = e.transpose(out=ps[:, j * 128:(j + 1) * 128], in_=yn[:, k, s_ * 128:(s_ + 1) * 128],
                                            identity=self.ident[:])
                        return r
                    self.op("pe", tr, reads=["yn", "ident"], writes=["psb%d" % pi])
                    self.op("act", lambda e, b=b, q=q, ps=ps: e.copy(out=yo[b][:, q * 512:(q + 1) * 512], in_=ps[:, :]),
                            reads=["psb%d" % pi], writes=["yo%d" % b])
                r0 = t0 + s_ * 128
                self.dma("sp", self.y_tok[r0:r0 + 128, :], yo[b], reads=["yo%d" % b], writes=[("y", r0)])

    def mixer_setup(self):
        nc = self.nc
        di = lambda n, s: nc.dram_tensor(n, list(s), F32, kind="ExternalInput").ap()
        do = lambda n, s: nc.dram_tensor(n, list(s), F32, kind="ExternalOutput").ap()
        self.cst_in = di("cst", [128, NCST])
        self.ropeC = di("ropeC", [128, 4096])
        self.ropeS = di("ropeS", [128, 4096])
        self.f_bias = di("mlstm_f_bias", [DEPTH, 16])
        self.conv_w = di("conv_w", [DEPTH, 3, 3 * WM])
        self.A_log = di("delta_A_log", [DEPTH, 16])
        self.dt_bias = di("delta_dt_bias", [DEPTH, 16])
        self.lgam = di("ret_log_gamma", [DEPTH, 16])
        self.hng = di("head_norm_g", [DEPTH, 3, HD])
        self.sC = di("st_C", [DEPTH, 2, NH, HD, HD])
        self.sn = di("st_n", [DEPTH, 2, NH, HD])
        self.sm = di("st_m", [DEPTH, 16])
        self.sD = di("st_D", [DEPTH, 2, NH, HD, HD])
        self.sR = di("st_R", [DEPTH, 2, NH, HD, HD])
        self.oC = do("o_C", [2, DEPTH, 2, NH, HD, HD])
        self.on = do("o_n", [2, DEPTH, 2, NH, HD])
        self.om = do("o_m", [2, DEPTH, 16])
        self.oD = do("o_D", [2, DEPTH, 2, NH, HD, HD])
        self.oR = do("o_R", [2, DEPTH, 2, NH, HD, HD])
        self.P = {m: self.dram("P" + m, [24, 128, TC], BF16) for m in "ABC"}
        self.PRE = self.dram("PRE", [24, 128, TC], F32)
        self.GO = self.dram("GO", [3, 8, 128, TC], BF16)
        self.GM = self.dram("GM", [48, 128, TC], BF16)
        self.GA = self.dram("GA", [4, 16, TC], F32)
        self.YS = self.dram("YS", [3, 8, 128, TC], BF16)
        self.OFs = self.dram("OFs", [TC, WM], F32)
        self.MG = self.dram("MG", [KC, 128, TC], BF16)
        self.cst = self.sb("cst_sb", [128, NCST], F32)
        self.dma("sp", self.cst[:], self.cst_in, writes=["cst"])
        self.ident_bf = self.sb("ident_bf", [128, 128], BF16)
        self.op("dve", lambda e: e.tensor_copy(out=self.ident_bf[:], in_=self.ident[:]), reads=["ident"], writes=["ident_bf"])
        self.ones_f = self.sb("ones_f", [128, 256], F32)
        self.op("dve", lambda e: e.memset(self.ones_f[:], 1.0), writes=["ones_f"])
        self.stb = [self.sb("stb%d" % i, [128, TT], BF16) for i in range(3)]
        self.stb_i = 0
        self.binT = self.sb("binT", [128, 160], F32)
        self.gpar = self.sb("gpar", [16, 8], F32)
        self.rC = self.sb("rC", [128, 3, TT], F32)
        self.rS = self.sb("rS", [128, 3, TT], F32)

    def C(self, name, rows=64):
        o, n = CST[name]
        return self.cst[0:rows, o:o + n]

    def next_stb(self):
        i = self.stb_i
        self.stb_i = (i + 1) % 3
        return i

    def w_in_blocks(self, l):
        B = []
        s = HD ** -0.5
        for j in range(24):
            B.append((OFF["qkvB"] + j * 128, 128, "pre", ("PRE", j), 1.0))
        for j in range(8):
            B.append((OFF["qA"] + j * 128, 128, "lin", ("PA", j), 1.0))
        for j in range(8):
            B.append((OFF["kA"] + j * 128, 128, "lin", ("PA", 8 + j), s))
        for j in range(8):
            B.append((OFF["vA"] + j * 128, 128, "lin", ("PA", 16 + j), 1.0))
        for j in range(8):
            B.append((OFF["oA"] + j * 128, 128, "sig", ("GO", 0, j), 1.0))
        B.append((OFF["iA"], 16, "gi", ("GA", 0), 1.0))
        B.append((OFF["fA"], 16, "gf", ("GA", 1), 1.0))
        for j in range(8):
            B.append((OFF["zB"] + j * 128, 128, "silu", ("GO", 1, j), 1.0))
        B.append((OFF["betaB"], 16, "gb", ("GA", 2), 1.0))
        B.append((OFF["aB"], 16, "gg", ("GA", 3), 1.0))
        for j in range(8):
            B.append((OFF["qC"] + j * 128, 128, "rope", ("PC", j), s))
        for j in range(8):
            B.append((OFF["kC"] + j * 128, 128, "rope", ("PC", 8 + j), 1.0))
        for j in range(8):
            B.append((OFF["vC"] + j * 128, 128, "lin", ("PC", 16 + j), 1.0))
        for j in range(8):
            B.append((OFF["gC"] + j * 128, 128, "silu", ("GO", 2, j), 1.0))
        for j in range(48):
            B.append((OFF["gm"] + j * 128, 128, "sig", ("GM", j), 1.0))
        return B

    def dst_ap(self, dst, t0, n, rows=128):
        if dst[0] in ("PA", "PB", "PC"):
            return self.P[dst[0][1]][dst[1], 0:rows, t0:t0 + n]
        if dst[0] == "PRE":
            return self.PRE[dst[1], 0:rows, t0:t0 + n]
        if dst[0] == "GO":
            return self.GO[dst[1], dst[2], 0:rows, t0:t0 + n]
        if dst[0] == "GM":
            return self.GM[dst[1], 0:rows, t0:t0 + n]
        if dst[0] == "GA":
            return self.GA[dst[1], 0:rows, t0:t0 + n]
        raise KeyError(dst)

    def mixer_params(self, l):
        B = self.w_in_blocks(l)
        self._B = B
        for bi, (c0, nco, kind, dst, sc) in enumerate(B):
            self.dma("sp", self.binT[0:nco, bi:bi + 1], self.b_in[l, c0:c0 + nco].rearrange("(p o) -> p o", o=1),
                     writes=["binT"], allow_slow_non_contiguous=True)
        gp = self.gpar
        for j, src in enumerate((self.f_bias, self.A_log, self.dt_bias)):
            self.dma("sp", gp[:, j:j + 1], src[l].rearrange("(p o) -> p o", o=1), writes=["gpar"],
                     allow_slow_non_contiguous=True)
        bi_f = [i for i, b in enumerate(B) if b[2] == "gf"][0]
        bi_g = [i for i, b in enumerate(B) if b[2] == "gg"][0]
        self.op("dve", lambda e: e.scalar_tensor_tensor(out=gp[:, 3:4], in0=gp[:, 0:1], scalar=-1.0,
                                                        in1=self.binT[0:16, bi_f:bi_f + 1], op0=ALU.mult, op1=ALU.subtract),
                reads=["gpar", "binT"], writes=["gpar"])
        self.op("dve", lambda e: e.tensor_tensor(out=gp[:, 4:5], in0=gp[:, 2:3], in1=self.binT[0:16, bi_g:bi_g + 1],
                                                 op=ALU.add), reads=["gpar", "binT"], writes=["gpar"])
        self.op("act", _act(AF.Exp, gp[:, 5:6], gp[:, 1:2]), reads=["gpar"], writes=["gpar"])
        self.op("dve", lambda e: e.tensor_scalar(out=gp[:, 5:6], in0=gp[:, 5:6], scalar1=-1.0, scalar2=None, op0=ALU.mult),
                reads=["gpar"], writes=["gpar"])

    def m1_project(self, l, tg, idxs=None, conv_bg=True):
        B0 = self._B
        if idxs is None:
            idxs = list(range(len(B0)))
        B = [B0[i] for i in idxs]
        blocks = [(self.w_in[l][:, c0:c0 + nco], KC, nco) for (c0, nco, kind, dst, sc) in B]
        for tt in range(3):
            if conv_bg and self.cvi(tg, tt):
                p0 = tg * TG + tt * TT - 512
                self.dma("sp", self.rC[:, tt, :], self.ropeC[:, p0:p0 + TT], writes=["rC"])
                self.dma("sp", self.rS[:, tt, :], self.ropeS[:, p0:p0 + TT], writes=["rS"])
        gp = self.gpar

        def evac(blk, tt, ps, pi):
            c0, nco, kind, dst, sc = B[blk]
            t0 = tg * TG + tt * TT
            gtt = t0 // TT
            bias = self.binT[0:nco, idxs[blk]:idxs[blk] + 1]
            pr = ["psb%d" % pi, "binT"]
            wr = [(dst, gtt)]
            if dst[0] == "GM":
                wr = [((self.kgm, dst[1]), gtt)]
            if conv_bg and kind == "pre" and dst[1] == 23 and tt == 2:
                self._start_conv = True
            if kind in ("lin", "sig", "silu") or (kind == "rope" and not self.cvi(tg, tt)):
                si = self.next_stb()
                st = self.stb[si]
                if kind in ("lin", "rope"):
                    self.op("dve", lambda e: e.tensor_scalar(out=st[:], in0=ps[:, :], scalar1=bias, scalar2=sc,
                                                             op0=ALU.add, op1=ALU.mult), reads=pr, writes=["stb%d" % si])
                else:
                    f = AF.Sigmoid if kind == "sig" else AF.Silu
                    self.op("act", _act(f, st[:], ps[:, :], bias=bias), reads=pr, writes=["stb%d" % si])
                self.dma("sp", self.dst_ap(dst, t0, TT), st[:], reads=["stb%d" % si], writes=wr)
            elif kind == "rope":
                ti = self.next_tmp()
                xf = self.tmpf[ti]
                self.op("dve", lambda e: e.tensor_scalar(out=xf[:], in0=ps[:, :], scalar1=bias, scalar2=sc,
                                                         op0=ALU.add, op1=ALU.mult), reads=pr, writes=["tmpf%d" % ti])
                p2 = self.next_ps()
                ps2 = self.psb[p2]
                self.op("pe", lambda e: e.matmul(ps2[:, :], self.C("Rm", 128), xf[:], start=True, stop=True),
                        reads=["tmpf%d" % ti, "cst"], writes=["psb%d" % p2])
                t2 = self.next_tmp()
                x2 = self.tmpf[t2]
                self.op("dve", lambda e: e.tensor_tensor(out=x2[:], in0=ps2[:, :], in1=self.rS[:, tt, :], op=ALU.mult),
                        reads=["psb%d" % p2, "rS"], writes=["tmpf%d" % t2])
                self.op("dve", lambda e: e.tensor_tensor(out=xf[:], in0=xf[:], in1=self.rC[:, tt, :], op=ALU.mult),
                        reads=["tmpf%d" % ti, "rC"], writes=["tmpf%d" % ti])
                si = self.next_stb()
                st = self.stb[si]
                self.op("dve", lambda e: e.tensor_tensor(out=st[:], in0=xf[:], in1=x2[:], op=ALU.add),
                        reads=["tmpf%d" % ti, "tmpf%d" % t2], writes=["stb%d" % si])
                self.dma("sp", self.dst_ap(dst, t0, TT), st[:], reads=["stb%d" % si], writes=wr)
            else:
                ti = self.next_tmp()
                tm = self.tmpf[ti]
                tr = ["tmpf%d" % ti]
                n = nco
                if kind == "pre" or kind == "gi":
                    self.op("dve", lambda e: e.tensor_scalar(out=tm[0:n, :], in0=ps[0:n, :], scalar1=bias, scalar2=None,
                                                             op0=ALU.add), reads=pr, writes=tr)
                elif kind == "gb":
                    self.op("act", _act(AF.Sigmoid, tm[0:n, :], ps[0:n, :], bias=bias), reads=pr, writes=tr)
                elif kind == "gf":
                    self.op("act", _act(AF.Exp, tm[0:n, :], ps[0:n, :], scale=-1.0, bias=gp[:, 3:4]),
                            reads=pr + ["gpar"], writes=tr)
                    self.op("act", _act(AF.Ln, tm[0:n, :], tm[0:n, :], bias=1.0), reads=tr, writes=tr)
                    self.op("dve", lambda e: e.tensor_scalar(out=tm[0:n, :], in0=tm[0:n, :], scalar1=-1.0, scalar2=None,
                                                             op0=ALU.mult), reads=tr, writes=tr)
                elif kind == "gg":
                    self.op("act", _act(AF.Exp, tm[0:n, :], ps[0:n, :], bias=gp[:, 4:5]), reads=pr + ["gpar"], writes=tr)
                    self.op("act", _act(AF.Ln, tm[0:n, :], tm[0:n, :], bias=1.0), reads=tr, writes=tr)
                    self.op("dve", lambda e: e.tensor_scalar(out=tm[0:n, :], in0=tm[0:n, :], scalar1=gp[:, 5:6],
                                                             scalar2=None, op0=ALU.mult), reads=tr + ["gpar"], writes=tr)
                self.dma("sp", self.dst_ap(dst, t0, TT, rows=n), tm[0:n, :], reads=tr, writes=wr)
        self.linear(blocks, lambda k, tt: self.hT[:, k, tt * TT:(tt + 1) * TT], ["hT"], [TT] * 3, evac)

    def conv_prep(self, l):
        self._cw = [self.load_vecT("cw%d_%d" % (l, j), self.conv_w[l, j], 24) for j in range(3)]
        if l == 0:
            self._cv = self.sb("cvt0", [128, TT + 2], F32)

    def conv_gen(self, l, grp):
        cw = self._cw
        cv = self._cv
        pieces = [(0, 256, True, True), (256, 256, True, True)]
        for i in range(8):
            pieces.append((512 + i * 512, 512, i == 0, i == 7))
        pieces = [p for p in pieces if (p[0] + p[1] + (0 if p[3] else 1) - 1) // TG == grp]
        for blk in range(24):
            for (t0, n, first, last) in pieces:
                lo = t0 - (0 if first else 1)
                hi = t0 + n + (0 if last else 1)
                if first or last:
                    self.op("dve", lambda e: e.memset(cv[:, 0:n + 2], 0.0), writes=["cvt"])
                gts = sorted(set([lo // TT, (hi - 1) // TT]))
                self.dma("sp", cv[:, (lo - t0 + 1):(hi - t0 + 1)], self.PRE[blk, :, lo:hi],
                         reads=[(("PRE", blk), g) for g in gts], writes=["cvt"])
                ti = self.next_tmp()
                y = self.tmpf[ti]
                tr = ["tmpf%d" % ti]
                names = ["cw%d_%d" % (l, j) for j in range(3)]
                self.op("dve", lambda e, y=y, n=n, blk=blk: e.tensor_scalar(
                    out=y[:, 0:n], in0=cv[:, 1:n + 1], scalar1=cw[1][:, blk:blk + 1], scalar2=None, op0=ALU.mult),
                    reads=["cvt"] + names, writes=tr)
                self.op("dve", lambda e, y=y, n=n, blk=blk: e.scalar_tensor_tensor(
                    out=y[:, 0:n], in0=cv[:, 0:n], scalar=cw[0][:, blk:blk + 1], in1=y[:, 0:n], op0=ALU.mult,
                    op1=ALU.add), reads=["cvt"] + names + tr, writes=tr)
                self.op("dve", lambda e, y=y, n=n, blk=blk: e.scalar_tensor_tensor(
                    out=y[:, 0:n], in0=cv[:, 2:n + 2], scalar=cw[2][:, blk:blk + 1], in1=y[:, 0:n], op0=ALU.mult,
                    op1=ALU.add), reads=["cvt"] + names + tr, writes=tr)
                self.op("act", _act(AF.Silu, y[:, 0:n], y[:, 0:n]), reads=tr, writes=tr)
                si = self.next_stb()
                st = self.stb[si]
                if blk < 16:
                    sq = self.sq[0]
                    self.op("act", _act(AF.Square, sq[:, 0, 0:n], y[:, 0:n]), reads=tr, writes=["sq0"])
                    pi = self.next_ps()
                    ps = self.psb[pi]
                    self.op("pe", lambda e, ps=ps, n=n, sq=sq: e.matmul(ps[:, 0:n], self.ones_bf[:], sq[:, 0, 0:n],
                                                                        start=True, stop=True),
                            reads=["sq0", "ones_bf"], writes=["psb%d" % pi])
                    self.op("act", _act(AF.Sqrt, self.rstd[:, 0:n], ps[:, 0:n], bias=self.eps_t[:]),
                            reads=["psb%d" % pi, "eps_t"], writes=["rstd"])
                    self.op("dve", lambda e, n=n: e.reciprocal(out=self.rstd[:, 0:n], in_=self.rstd[:, 0:n]),
                            reads=["rstd"], writes=["rstd"])
                    sc = HD ** -0.5 if blk < 8 else 1.0
                    self.op("dve", lambda e, y=y, n=n, st=st, sc=sc: e.scalar_tensor_tensor(
                        out=st[:, 0:n], in0=y[:, 0:n], scalar=sc, in1=self.rstd[:, 0:n], op0=ALU.mult, op1=ALU.mult),
                        reads=tr + ["rstd"], writes=["stb%d" % si])
                else:
                    self.op("act", lambda e, y=y, n=n, st=st: e.copy(out=st[:, 0:n], in_=y[:, 0:n]), reads=tr,
                            writes=["stb%d" % si])
                self.dma("sp", self.P["B"][blk, :, t0:t0 + n], st[:, 0:n], reads=["stb%d" % si],
                         writes=[(("PB", blk), g) for g in sorted(set([t0 // TT, (t0 + n - 1) // TT]))])
                yield

    def m3_merge(self, l, tg):
        hv = self.hraw[:].bitcast(BF16)
        gv = self.graw[:].bitcast(BF16)
        ys = [hv[:, 0:8 * TG].rearrange("p (h t) -> p h t", h=8), hv[:, 8 * TG:16 * TG].rearrange("p (h t) -> p h t", h=8),
              gv[:, 0:8 * TG].rearrange("p (h t) -> p h t", h=8)]
        ysn = ["hT", "hT", "gT"]
        t_0 = tg * TG
        for i in range(3):
            self.dma("sp", ys[i], self.YS[i, :, :, t_0:t_0 + TG].rearrange("h p t -> p h t"),
                     reads=[((self.kys, i), (t_0 // TT) + j) for j in range(3)], writes=[ysn[i]])
        acc = {}
        for j in range(KC):
            blocks = [(self.w_br[l, i][:, j * 128:(j + 1) * 128], 8, 128) for i in range(3)]

            def evac(blk, tt, ps, pi, j=j):
                i = blk
                t0 = t_0 + tt * TT
                gtt = t0 // TT
                si = self.next_stb()
                gmt = self.stb[si]
                self.dma("sp", gmt[:], self.GM[i * KC + j, :, t0:t0 + TT], reads=[((self.kgm, i * KC + j), gtt)],
                         writes=["stb%d" % si])
                if i == 0:
                    ti = self.next_tmp()
                    acc[tt] = ti
                    self.op("dve", lambda e: e.tensor_tensor(out=self.tmpf[ti][:], in0=ps[:, :], in1=gmt[:], op=ALU.mult),
                            reads=["psb%d" % pi, "stb%d" % si], writes=["tmpf%d" % ti])
                else:
                    ti = acc[tt]
                    a = self.tmpf[ti]
                    xi = self.xr_i
                    self.xr_i = (xi + 1) % len(self.xr)
                    t2 = self.xr[xi]
                    self.op("dve", lambda e: e.tensor_tensor(out=t2[:], in0=ps[:, :], in1=gmt[:], op=ALU.mult),
                            reads=["psb%d" % pi, "stb%d" % si], writes=["xr%d" % xi])
                    if i == 1:
                        self.op("dve", lambda e: e.tensor_tensor(out=a[:], in0=a[:], in1=t2[:], op=ALU.add),
                                reads=["tmpf%d" % ti, "xr%d" % xi], writes=["tmpf%d" % ti])
                    else:
                        s2 = self.next_stb()
                        mo = self.stb[s2]
                        self.op("dve", lambda e: e.tensor_tensor(out=mo[:], in0=a[:], in1=t2[:], op=ALU.add),
                                reads=["tmpf%d" % ti, "xr%d" % xi], writes=["stb%d" % s2])
                        self.dma("sp", self.MG[j, :, t0:t0 + TT], mo[:], reads=["stb%d" % s2], writes=[(("MG", j), gtt)])
            self.linear3(blocks, ys, ysn, evac)
        self.dma("sp", self.hT, self.MG[:, :, t_0:t_0 + TG].rearrange("k p t -> p k t"),
                 reads=[(("MG", j), (t_0 // TT) + q) for j in range(KC) for q in range(3)], writes=["hT"])
        blocks = [(self.w_out[l][:, j * 128:(j + 1) * 128], KC, 128) for j in range(KC)]
        self.linear(blocks, lambda k, tt: self.hT[:, k, tt * TT:(tt + 1) * TT], ["hT"], [TT] * 3,
                    self.resid_evac(tg, self.gsc[(l, 1)], "gsc%d_1" % l))

    def linear3(self, blocks, ys, ysn, evac):
        slots = []
        for (ap, kc, ncols) in blocks:
            s = self.wr_i
            self.wr_i = (s + 1) % self.NWR
            slots.append(s)
            self.dma("pool", self.wr[s][:, 0:kc, 0:ncols], ap.rearrange("(k p) n -> p k n", p=128), writes=["wr%d" % s])
        for tt in range(3):
            for i in range(3):
                s = slots[i]
                pi = self.next_ps()
                ps = self.psb[pi]

                def mm(e, s=s, i=i, tt=tt, ps=ps):
                    for k in range(8):
                        r = e.matmul(ps[:, :], self.wr[s][:, k, :], ys[i][:, k, tt * TT:(tt + 1) * TT],
                                     start=(k == 0), stop=(k == 7))
                    return r
                self.op("pe", mm, reads=["wr%d" % s, ysn[i]], writes=["psb%d" % pi])
                evac(i, tt, ps, pi)
    def ar(self, name, parts, free, dt=F32):
        n = int(np.prod(free))
        nf = n if dt == F32 else (n + 1) // 2
        for raw, pname, key in ((self.hraw, "hT", "h"), (self.graw, "gT", "g")):
            off = self._aro[key]
            if off + nf <= 12288:
                self._aro[key] = off + nf
                v = raw[0:parts, off:off + nf]
                if dt != F32:
                    v = v.bitcast(dt)
                if len(free) == 2:
                    v = v.rearrange("p (a b) -> p a b", a=free[0])
                self.alias(name, pname)
                return v
        raise RuntimeError("arena full " + name)

    def scan_setup(self):
        self._aro = {"h": 0, "g": 0}
        a = self.ar
        T = {}
        for n in ("qT", "kT", "vT", "gt"):
            T[n] = a(n, 128, [8, TT], BF16)
        T["S"] = a("S", 128, [8, 128]); T["Sb"] = a("Sb", 128, [8, 128], BF16)
        T["nS"] = a("nS", 128, [8, 2]); T["nb"] = a("nb", 128, [8, 2], BF16)
        T["o"] = a("o_sb", 64, [8, 128]); T["of"] = a("of_sb", 64, [8, 128])
        T["U0"] = T["of"]
        for n in ("gi", "gf", "gb", "gg"):
            T[n] = a(n, 16, [TT])
        for n in ("ktm", "vtm", "khat", "bv", "bk", "kend", "Ub"):
            T[n] = a(n, 64, [8, 128], BF16)
        for n in ("ATb", "qkTb", "TTb"):
            T[n] = a(n, 64, [8, 64], BF16)
        T["WkT"] = a("WkT", 128, [8, 64], BF16)
        T["ysc"] = a("ysc", 128, [8, 64], BF16)
        for n in ("MT", "t64", "dec", "A", "AT", "T", "TT_", "W", "OkT", "qk"):
            T[n] = a(n, 64, [8, 64])
        T["gtm"] = a("gtm", 64, [32])
        for n in ("sa", "sb_", "sc_", "sd_", "se_"):
            T[n] = a(n, 64, [8])
        T["eL"] = a("eL", 128, [8]); T["g64"] = a("g64", 128, [8]); T["lgb"] = a("lgb", 128, [8])
        T["rsc"] = a("rsc", 64, [8]); T["ksc"] = a("ksc", 64, [8]); T["ss"] = a("ss", 64, [8])
        T["m0b"] = a("m0b", 128, [8])
        self.T = T
        self.hg = [[None] * 3 for _ in range(DEPTH)]

    def bank(self, i, rows, shape, dt=F32):
        v = self.psb[i][0:rows, :]
        if dt != F32:
            v = v.bitcast(dt)
        n = int(np.prod(shape))
        v = v[:, 0:n]
        if len(shape) == 2:
            v = v.rearrange("p (a b) -> p a b", a=shape[0])
        return v

    def bc(self, ap2, n):
        return ap2.unsqueeze(2).broadcast_to([ap2.shape[0], ap2.shape[1], n])

    def bm(self, ap2):
        return ap2.unsqueeze(1).broadcast_to([ap2.shape[0], 8, ap2.shape[1]])

    def load_tt(self, m, gtt, bwd):
        T = self.T
        t0 = gtt * TT
        P = self.P["ABC"[m]]
        for i, n in enumerate(("qT", "kT", "vT")):
            self.dma("sp", T[n], P[8 * i:8 * i + 8, :, t0:t0 + TT].rearrange("h p t -> p h t"),
                     reads=[(("P" + "ABC"[m], 8 * i + h), gtt) for h in range(8)], writes=[n])
        if bwd:
            self.dma("sp", T["gt"], self.GO[m, :, :, t0:t0 + TT].rearrange("h p t -> p h t"),
                     reads=[(("GO", m, h), gtt) for h in range(8)], writes=["gt"])
        if m == 0:
            self.dma("sp", T["gi"], self.GA[0, :, t0:t0 + TT], reads=[(("GA", 0), gtt)], writes=["gi"])
            self.dma("sp", T["gf"], self.GA[1, :, t0:t0 + TT], reads=[(("GA", 1), gtt)], writes=["gf"])
        if m == 1:
            self.dma("sp", T["gb"], self.GA[2, :, t0:t0 + TT], reads=[(("GA", 2), gtt)], writes=["gb"])
            self.dma("sp", T["gg"], self.GA[3, :, t0:t0 + TT], reads=[(("GA", 3), gtt)], writes=["gg"])

    def mm8(self, banks, rows, width, fn, reads):
        T = self.T
        per = 8 // len(banks)
        for bi, b in enumerate(banks):
            v = self.bank(b, rows, [per, width])

            def f(e, bi=bi, v=v):
                r = None
                for hh in range(per):
                    r = fn(e, bi * per + hh, v[:, hh, :])
                return r
            self.op("pe", f, reads=reads, writes=["psb%d" % b])

    def evac8(self, eng, banks, rows, width, fn, reads, writes):
        per = 8 // len(banks)
        for bi, b in enumerate(banks):
            v = self.bank(b, rows, [per, width])
            hs = slice(bi * per, (bi + 1) * per)
            self.op(eng, lambda e, v=v, hs=hs: fn(e, v, hs), reads=reads + ["psb%d" % b], writes=writes)

    def kv_tm(self, c0):
        T = self.T
        for src, bk_, dst in (("kT", 2, "ktm"), ("vT", 3, "vtm")):
            v = self.bank(bk_, 64, [8, 128], BF16)

            def f(e, src=src, v=v):
                for h in range(8):
                    r = e.transpose(out=v[:, h, :], in_=T[src][:, h, c0:c0 + L], identity=self.ident_bf[:])
                return r
            self.op("pe", f, reads=[src, "ident_bf"], writes=["psb%d" % bk_])
        v3 = self.bank(3, 64, [8, 128], BF16)
        self.op("act", lambda e: e.copy(out=T["vtm"], in_=v3), reads=["psb3"], writes=["vtm"])

    def scale_k(self, dst, sc_name, sc_ap):
        T = self.T
        v2 = self.bank(2, 64, [8, 128], BF16)
        self.op("dve", lambda e: e.tensor_tensor(out=T[dst], in0=v2, in1=self.bc(sc_ap, 128), op=ALU.mult),
                reads=["psb2", sc_name], writes=[dst])

    def gate_tm(self, rows_a, rows_b, c0):
        T = self.T
        v = self.psb[1]

        def f(e):
            e.transpose(out=v[0:64, 0:16], in_=T[rows_a][:, c0:c0 + L], identity=self.ident[0:16, 0:16])
            return e.transpose(out=v[0:64, 16:32], in_=T[rows_b][:, c0:c0 + L], identity=self.ident[0:16, 0:16])
        self.op("pe", f, reads=[rows_a, rows_b, "ident"], writes=["psb1"])
        self.op("dve", lambda e: e.tensor_copy(out=T["gtm"], in_=v[0:64, 0:32]), reads=["psb1"], writes=["gtm"])

    def state_update(self, banks_src_fn, dec_name, dec_ap, with_n=False):
        T = self.T
        self.op("dve", lambda e: e.tensor_tensor(out=T["S"], in0=T["S"], in1=self.bc(dec_ap, 128), op=ALU.mult),
                reads=["S", dec_name], writes=["S"])
        self.evac8("dve", [6, 7], 128, 128, lambda e, v, hs: e.tensor_tensor(out=T["S"][:, hs, :], in0=T["S"][:, hs, :],
                                                                               in1=v, op=ALU.add), ["S"], ["S"])
        self.op("act", lambda e: e.copy(out=T["Sb"], in_=T["S"]), reads=["S"], writes=["Sb"])

    def finalize(self, l, m, c0, t0):
        T = self.T
        gtt = t0 // TT
        self.dma("sp", T["of"], self.OFs[t0:t0 + L, :].rearrange("p (h e) -> p h e", h=8), reads=[("OF", t0)],
                 writes=["of_sb"])
        self.op("dve", lambda e: e.tensor_tensor(out=T["o"], in0=T["o"], in1=T["of"], op=ALU.add),
                reads=["o_sb", "of_sb"], writes=["o_sb"])
        self.op("dve", lambda e: e.tensor_tensor(out=T["of"], in0=T["o"], in1=T["o"], op=ALU.mult), reads=["o_sb"],
                writes=["of_sb"])
        self.op("dve", lambda e: e.tensor_reduce(out=T["ss"], in_=T["of"], op=ALU.add, axis=AX.X), reads=["of_sb"],
                writes=["ss"])
        self.op("act", _act(AF.Sqrt, T["ss"], T["ss"], scale=1.0 / HD, bias=self.eps_t[0:64, :]), reads=["ss", "eps_t"],
                writes=["ss"])
        self.op("dve", lambda e: e.reciprocal(out=T["ss"], in_=T["ss"]), reads=["ss"], writes=["ss"])
        self.op("dve", lambda e: e.tensor_tensor(out=T["Ub"], in0=T["o"], in1=self.bc(T["ss"], 128), op=ALU.mult),
                reads=["o_sb", "ss"], writes=["Ub"])
        v = self.bank(2, 128, [8, 64], BF16)

        def f(e):
            for h in range(8):
                r = e.transpose(out=v[:, h, :], in_=T["Ub"][:, h, :], identity=self.ident_bf[0:64, 0:64])
            return r
        self.op("pe", f, reads=["Ub", "ident_bf"], writes=["psb2"])
        hg = self.hg[l][m]
        self.op("dve", lambda e: e.scalar_tensor_tensor(out=T["ysc"], in0=v, scalar=hg[:, 0:1],
                                                        in1=T["gt"][:, :, c0:c0 + L], op0=ALU.mult, op1=ALU.mult),
                reads=["psb2", "gt", "hg%d_%d" % (l, m)], writes=["ysc"])
        self.dma("sp", self.YS[m, :, :, t0:t0 + L].rearrange("h p t -> p h t"), T["ysc"], reads=["ysc"],
                 writes=[(("YS", m), gtt)], allow_slow_non_contiguous=True)

    def out_chunk(self, l, m, d, c0, t0):
        T = self.T
        if d == 0:
            self.dma("sp", self.OFs[t0:t0 + L, :].rearrange("p (h e) -> p h e", h=8), T["o"], reads=["o_sb"],
                     writes=[("OF", t0)])
        else:
            self.finalize(l, m, c0, t0)

    def scan(self, l, m):
        T = self.T
        if self.hg[l][m] is None:
            self.hg[l][m] = self.load_vecT("hg%d_%d" % (l, m), self.hng[l, m], 1)
        for d in (0, 1):
            self.dir_setup(l, m, d)
            cur_tt = None
            for si, (s0, sl) in enumerate(SEGS):
                self.state_init(l, m, d, si)
                chunks = list(range(s0, s0 + sl, L))
                if d == 1:
                    chunks = chunks[::-1]
                for t0 in chunks:
                    gtt = t0 // TT
                    if gtt != cur_tt:
                        self.load_tt(m, gtt, d == 1)
                        cur_tt = gtt
                    c0 = t0 - gtt * TT
                    (self.step_mlstm, self.step_delta, self.step_ret)[m](l, d, c0, t0)
                if si < 2:
                    self.state_out(l, m, d, si)

    def dir_setup(self, l, m, d):
        T = self.T
        M = self.C("LE" if d == 0 else "GE")
        self._M = M
        self._Tri = M
        if m == 2:
            self.dma("sp", T["lgb"], self.lgam[l:l + 1, 8 * d:8 * d + 8].partition_broadcast(128), writes=["lgb"])
            p1 = self.C("p1f" if d == 0 else "p1b")
            p2 = self.C("p2f" if d == 0 else "p2b")
            self.op("act", _act(AF.Exp, T["rsc"], T["lgb"][0:64, :], scale=p1), reads=["lgb", "cst"], writes=["rsc"])
            self.op("act", _act(AF.Exp, T["ksc"], T["lgb"][0:64, :], scale=p2), reads=["lgb", "cst"], writes=["ksc"])
            self.op("act", _act(AF.Exp, T["g64"], T["lgb"], scale=64.0), reads=["lgb"], writes=["g64"])
            self.op("dve", lambda e: e.reciprocal(out=T["sa"], in_=T["rsc"]), reads=["rsc"], writes=["sa"])
            self.op("dve", lambda e: e.tensor_tensor(out=T["MT"], in0=self.bc(T["sa"], 64), in1=self.bm(M), op=ALU.mult),
                    reads=["sa", "cst"], writes=["MT"])

    def state_init(self, l, m, d, si):
        T = self.T
        if si < 2:
            self.op("dve", lambda e: e.memset(T["S"], 0.0), writes=["S"])
            self.op("dve", lambda e: e.memset(T["nS"], 0.0), writes=["nS"])
        else:
            src = (self.sC, self.sD, self.sR)[m]
            self.dma("sp", T["S"], src[l, d].rearrange("h k e -> k h e"), writes=["S"])
            if m == 0:
                self.dma("sp", T["nS"][:, :, 0], self.sn[l, d].rearrange("h k -> k h"), writes=["nS"],
                         allow_slow_non_contiguous=True)
                self.dma("sp", T["nS"][:, :, 1], self.sn[l, d].rearrange("h k -> k h"), writes=["nS"],
                         allow_slow_non_contiguous=True)
                self.dma("sp", T["m0b"], self.sm[l:l + 1, 8 * d:8 * d + 8].partition_broadcast(128), writes=["m0b"])
                self.op("act", _act(AF.Exp, T["m0b"], T["m0b"]), reads=["m0b"], writes=["m0b"])
                self.op("dve", lambda e: e.tensor_tensor(out=T["S"], in0=T["S"], in1=self.bc(T["m0b"], 128), op=ALU.mult),
                        reads=["S", "m0b"], writes=["S"])
                self.op("dve", lambda e: e.tensor_tensor(out=T["nS"], in0=T["nS"], in1=self.bc(T["m0b"], 2), op=ALU.mult),
                        reads=["nS", "m0b"], writes=["nS"])
        self.op("act", lambda e: e.copy(out=T["Sb"], in_=T["S"]), reads=["S"], writes=["Sb"])
        self.op("act", lambda e: e.copy(out=T["nb"], in_=T["nS"]), reads=["nS"], writes=["nb"])

    def state_out(self, l, m, d, si):
        T = self.T
        if m == 0:
            self.mlstm_state_out(l, d, si)
            return
        dst = (None, self.oD, self.oR)[m]
        self.dma("sp", dst[si, l, d].rearrange("h k e -> k h e"), T["S"], reads=["S"], writes=[("ost", m, si, l, d)])

    def step_ret(self, l, d, c0, t0):
        T = self.T
        self.mm8([0], 64, 64, lambda e, h, o: e.matmul(o, T["kT"][:, h, c0:c0 + L], T["qT"][:, h, c0:c0 + L],
                                                       start=True, stop=True), ["kT", "qT"])
        self.evac8("dve", [0], 64, 64, lambda e, v, hs: e.tensor_tensor(out=T["ATb"], in0=v, in1=T["MT"], op=ALU.mult),
                   ["MT"], ["ATb"])
        self.kv_tm(c0)
        self.scale_k("khat", "ksc", T["ksc"])

        def o_mm(e, h, o):
            e.matmul(o, T["qT"][:, h, c0:c0 + L], T["Sb"][:, h, :], start=True, stop=False)
            return e.matmul(o, T["ATb"][:, h, :], T["vtm"][:, h, :], start=False, stop=True)
        self.mm8([4, 5], 64, 128, o_mm, ["qT", "Sb", "ATb", "vtm"])
        self.evac8("dve", [4, 5], 64, 128, lambda e, v, hs: e.tensor_tensor(
            out=T["o"][:, hs, :], in0=v, in1=self.bc(T["rsc"][:, hs], 128), op=ALU.mult), ["rsc"], ["o_sb"])
        self.out_chunk(l, 2, d, c0, t0)
        self.mm8([6, 7], 128, 128, lambda e, h, o: e.matmul(o, T["khat"][:, h, :], T["vtm"][:, h, :], start=True,
                                                           stop=True), ["khat", "vtm"])
        self.state_update(None, "g64", T["g64"])

    def step_mlstm(self, l, d, c0, t0):
        T = self.T
        g = T["gtm"]
        self.gate_tm("gi", "gf", c0)
        ig, lf = g[:, 8 * d:8 * d + 8], g[:, 16 + 8 * d:16 + 8 * d + 8]
        v1 = self.psb[1]
        Tri = self._Tri
        Msk = self._M

        def f(e):
            e.matmul(v1[0:64, 32:40], Tri, lf, start=True, stop=True)
            return e.matmul(v1[0:128, 40:48], self.ones_f[0:64, 0:128], lf, start=True, stop=True)
        self.op("pe", f, reads=["gtm", "cst", "ones_f"], writes=["psb1"])
        self.op("dve", lambda e: e.tensor_tensor(out=T["sa"], in0=ig, in1=v1[0:64, 32:40], op=ALU.subtract),
                reads=["gtm", "psb1"], writes=["sa"])
        self.op("act", _act(AF.Exp, T["sa"], T["sa"]), reads=["sa"], writes=["sa"])
        self.op("act", _act(AF.Exp, T["sb_"], v1[0:64, 32:40], scale=-1.0), reads=["psb1"], writes=["sb_"])
        self.op("act", _act(AF.Exp, T["eL"], v1[0:128, 40:48]), reads=["psb1"], writes=["eL"])
        self.op("dve", lambda e: e.tensor_tensor(out=T["sc_"], in0=T["sa"], in1=T["eL"][0:64, :], op=ALU.mult),
                reads=["sa", "eL"], writes=["sc_"])
        self.mm8([0], 64, 64, lambda e, h, o: e.matmul(o, T["kT"][:, h, c0:c0 + L], T["qT"][:, h, c0:c0 + L],
                                                       start=True, stop=True), ["kT", "qT"])
        self.evac8("dve", [0], 64, 64, lambda e, v, hs: e.tensor_tensor(out=T["t64"], in0=v, in1=self.bc(T["sa"], 64),
                                                                        op=ALU.mult), ["sa"], ["t64"])
        self.op("dve", lambda e: e.tensor_tensor(out=T["ATb"], in0=T["t64"], in1=self.bm(Msk), op=ALU.mult),
                reads=["t64", "cst"], writes=["ATb"])
        self.kv_tm(c0)
        self.scale_k("khat", "sc_", T["sc_"])

        def o_mm(e, h, o):
            e.matmul(o, T["qT"][:, h, c0:c0 + L], T["Sb"][:, h, :], start=True, stop=False)
            return e.matmul(o, T["ATb"][:, h, :], T["vtm"][:, h, :], start=False, stop=True)
        self.mm8([4, 5], 64, 128, o_mm, ["qT", "Sb", "ATb", "vtm"])
        den = v1[0:64, 64:80].rearrange("p (h t) -> p h t", h=8)

        def d_mm(e):
            for h in range(8):
                e.matmul(den[:, h, :], T["qT"][:, h, c0:c0 + L], T["nb"][:, h, :], start=True, stop=False)
                r = e.matmul(den[:, h, :], T["ATb"][:, h, :], self.ones_bf[0:64, 0:2], start=False, stop=True)
            return r
        self.op("pe", d_mm, reads=["qT", "nb", "ATb", "ones_bf"], writes=["psb1"])
        self.op("dve", lambda e: e.tensor_scalar(out=T["sd_"], in0=den[:, :, 0], scalar1=-1.0, scalar2=None, op0=ALU.mult),
                reads=["psb1"], writes=["sd_"])
        self.op("dve", lambda e: e.tensor_tensor(out=T["sd_"], in0=T["sd_"], in1=den[:, :, 0], op=ALU.max),
                reads=["psb1", "sd_"], writes=["sd_"])
        self.op("dve", lambda e: e.tensor_tensor(out=T["sd_"], in0=T["sd_"], in1=T["sb_"], op=ALU.max),
                reads=["sd_", "sb_"], writes=["sd_"])
        self.op("dve", lambda e: e.reciprocal(out=T["sd_"], in_=T["sd_"]), reads=["sd_"], writes=["sd_"])
        self.evac8("dve", [4, 5], 64, 128, lambda e, v, hs: e.tensor_tensor(
            out=T["o"][:, hs, :], in0=v, in1=self.bc(T["sd_"][:, hs], 128), op=ALU.mult), ["sd_"], ["o_sb"])
        self.out_chunk(l, 0, d, c0, t0)
        self.mm8([6, 7], 128, 128, lambda e, h, o: e.matmul(o, T["khat"][:, h, :], T["vtm"][:, h, :], start=True,
                                                           stop=True), ["khat", "vtm"])
        dn = v1[0:128, 96:112].rearrange("p (h t) -> p h t", h=8)

        def n_mm(e):
            for h in range(8):
                r = e.matmul(dn[:, h, :], T["khat"][:, h, :], self.ones_bf[0:64, 0:2], start=True, stop=True)
            return r
        self.op("pe", n_mm, reads=["khat", "ones_bf"], writes=["psb1"])
        self.op("dve", lambda e: e.tensor_tensor(out=T["nS"], in0=T["nS"], in1=self.bc(T["eL"], 2), op=ALU.mult),
                reads=["nS", "eL"], writes=["nS"])
        self.op("dve", lambda e: e.tensor_tensor(out=T["nS"], in0=T["nS"], in1=dn, op=ALU.add), reads=["nS", "psb1"],
                writes=["nS"])
        self.op("act", lambda e: e.copy(out=T["nb"], in_=T["nS"]), reads=["nS"], writes=["nb"])
        self.state_update(None, "eL", T["eL"])

    def mlstm_state_out(self, l, d, si):
        T = self.T
        s0 = SEGS[si][0]
        gi, gf = T["gi"], T["gf"]
        n = 256
        P_ = self.tmpf[0][0:16, 0:n]
        E_ = self.tmpf[1][0:16, 0:n]
        tot = T["se_"][0:16, 0:1]
        mx = T["se_"][0:16, 1:2]
        rd = ["gi", "gf"]
        lfv, igv = gf[:, s0:s0 + n], gi[:, s0:s0 + n]
        self.op("dve", lambda e: e.tensor_tensor_scan(out=P_, data0=self.ones_f[0:16, 0:n], data1=lfv, initial=0.0,
                                                      op0=ALU.mult, op1=ALU.add), reads=rd + ["ones_f"], writes=["tmpf0"])
        self.op("dve", lambda e: e.tensor_copy(out=tot, in_=P_[:, n - 1:n]), reads=["tmpf0"], writes=["se_"])
        if d == 0:
            self.op("dve", lambda e: e.tensor_tensor(out=E_, in0=igv, in1=P_, op=ALU.subtract), reads=rd + ["tmpf0"],
                    writes=["tmpf1"])
            self.op("dve", lambda e: e.tensor_scalar(out=E_, in0=E_, scalar1=tot, scalar2=None, op0=ALU.add),
                    reads=["tmpf1", "se_"], writes=["tmpf1"])
        else:
            self.op("dve", lambda e: e.tensor_tensor(out=E_, in0=igv, in1=P_, op=ALU.add), reads=rd + ["tmpf0"],
                    writes=["tmpf1"])
            self.op("dve", lambda e: e.tensor_tensor(out=E_, in0=E_, in1=lfv, op=ALU.subtract), reads=rd + ["tmpf1"],
                    writes=["tmpf1"])
        self.op("dve", lambda e: e.tensor_reduce(out=mx, in_=E_, op=ALU.max, axis=AX.X), reads=["tmpf1"], writes=["se_"])
        self.op("dve", lambda e: e.tensor_tensor(out=mx, in0=mx, in1=tot, op=ALU.max), reads=["se_"], writes=["se_"])
        self.dma("sp", self.om[si, l, :].rearrange("(p o) -> p o", o=1)[8 * d:8 * d + 8, :], T["se_"][8 * d:8 * d + 8, 1:2],
                 reads=["se_"], writes=[("om", si, l, d)], allow_slow_non_contiguous=True)
        mrep = self.tmpf[2][0:16, 0:128]
        self.op("dve", lambda e: e.tensor_scalar(out=mrep, in0=self.ones_f[0:16, 0:128], scalar1=mx, scalar2=None, op0=ALU.mult),
                reads=["se_", "ones_f"], writes=["tmpf2"])
        v1 = self.psb[1]
        self.op("pe", lambda e: e.matmul(v1[0:128, 0:16], mrep, self.ident[0:16, 0:16], start=True, stop=True),
                reads=["tmpf2", "ident"], writes=["psb1"])
        self.op("act", _act(AF.Exp, T["m0b"], v1[0:128, 8 * d:8 * d + 8], scale=-1.0), reads=["psb1"], writes=["m0b"])
        self.op("dve", lambda e: e.tensor_tensor(out=T["S"], in0=T["S"], in1=self.bc(T["m0b"], 128), op=ALU.mult),
                reads=["S", "m0b"], writes=["S"])
        self.op("dve", lambda e: e.tensor_tensor(out=T["nS"], in0=T["nS"], in1=self.bc(T["m0b"], 2), op=ALU.mult),
                reads=["nS", "m0b"], writes=["nS"])
        self.dma("sp", self.oC[si, l, d].rearrange("h k e -> k h e"), T["S"], reads=["S"], writes=[("oC", si, l, d)])
        self.dma("sp", self.on[si, l, d].rearrange("h k -> k h"), T["nS"][:, :, 0], reads=["nS"], writes=[("on", si, l, d)],
                 allow_slow_non_contiguous=True)

    def step_delta(self, l, d, c0, t0):
        T = self.T
        g = T["gtm"]
        self.gate_tm("gb", "gg", c0)
        be, gg = g[:, 8 * d:8 * d + 8], g[:, 16 + 8 * d:16 + 8 * d + 8]
        v1 = self.psb[1]
        Tri = self._Tri
        INC = self.C("GE" if d == 0 else "LE")
        STR = self.C("GT" if d == 0 else "LT")
        self.op("dve", lambda e: e.tensor_tensor(out=T["t64"], in0=self.bc(gg, 64), in1=self.bm(STR), op=ALU.mult),
                reads=["gtm", "cst"], writes=["t64"])

        def f(e):
            e.matmul(v1[0:64, 32:40], Tri, gg, start=True, stop=True)
            return e.matmul(v1[0:128, 40:48], self.ones_f[0:64, 0:128], gg, start=True, stop=True)
        self.op("pe", f, reads=["gtm", "cst", "ones_f"], writes=["psb1"])
        self.op("dve", lambda e: e.tensor_copy(out=T["sa"], in_=v1[0:64, 32:40]), reads=["psb1"], writes=["sa"])
        self.op("act", _act(AF.Exp, T["sb_"], T["sa"]), reads=["sa"], writes=["sb_"])
        self.op("dve", lambda e: e.tensor_tensor(out=T["sc_"], in0=v1[0:64, 40:48], in1=T["sa"], op=ALU.subtract),
                reads=["psb1", "sa"], writes=["sc_"])
        self.op("act", _act(AF.Exp, T["sc_"], T["sc_"]), reads=["sc_"], writes=["sc_"])
        self.op("act", _act(AF.Exp, T["eL"], v1[0:128, 40:48]), reads=["psb1"], writes=["eL"])
        self.op("dve", lambda e: e.tensor_tensor(out=T["sd_"], in0=be, in1=T["sb_"], op=ALU.mult), reads=["gtm", "sb_"],
                writes=["sd_"])
        b0 = self.bank(0, 64, [8, 64])
        self.op("pe", lambda e: e.matmul(self.psb[0][0:64, :], Tri, T["t64"].rearrange("p h m -> p (h m)"), start=True,
                                         stop=True), reads=["t64", "cst"], writes=["psb0"])
        self.op("act", _act(AF.Exp, T["dec"], b0), reads=["psb0"], writes=["dec"])
        self.op("dve", lambda e: e.tensor_tensor(out=T["t64"], in0=T["dec"], in1=self.bm(STR), op=ALU.mult),
                reads=["dec", "cst"], writes=["t64"])
        self.op("dve", lambda e: e.tensor_tensor(out=T["dec"], in0=T["dec"], in1=self.bm(INC), op=ALU.mult),
                reads=["dec", "cst"], writes=["dec"])
        self.op("dve", lambda e: e.tensor_tensor(out=T["t64"], in0=T["t64"], in1=self.bc(be, 64), op=ALU.mult),
                reads=["t64", "gtm"], writes=["t64"])
        self.mm8([0], 64, 64, lambda e, h, o: e.matmul(o, T["kT"][:, h, c0:c0 + L], T["kT"][:, h, c0:c0 + L],
                                                       start=True, stop=True), ["kT"])
        self.evac8("dve", [0], 64, 64, lambda e, v, hs: e.tensor_tensor(out=T["A"], in0=v, in1=T["t64"], op=ALU.mult),
                   ["t64"], ["A"])
        self.mm8([1], 64, 64, lambda e, h, o: e.matmul(o, T["qT"][:, h, c0:c0 + L], T["kT"][:, h, c0:c0 + L],
                                                       start=True, stop=True), ["qT", "kT"])
        self.evac8("dve", [1], 64, 64, lambda e, v, hs: e.tensor_tensor(out=T["qk"], in0=v, in1=T["dec"], op=ALU.mult),
                   ["dec"], ["qk"])
        i64 = self.ident[0:64, 0:64]
        for src, bk_, dst, eng in (("A", 0, "AT", "dve"), ("qk", 1, "qkTb", "act")):
            vb = self.bank(bk_, 64, [8, 64])

            def tr(e, src=src, vb=vb):
                for h in range(8):
                    r = e.transpose(out=vb[:, h, :], in_=T[src][:, h, :], identity=i64)
                return r
            self.op("pe", tr, reads=[src, "ident"], writes=["psb%d" % bk_])
            if eng == "dve":
                self.op("dve", lambda e, vb=vb, dst=dst: e.tensor_copy(out=T[dst], in_=vb), reads=["psb%d" % bk_],
                        writes=[dst])
            else:
                self.op("act", lambda e, vb=vb, dst=dst: e.copy(out=T[dst], in_=vb), reads=["psb%d" % bk_], writes=[dst])
        I8 = self.bm(i64)
        self.op("dve", lambda e: e.tensor_tensor(out=T["W"], in0=T["A"], in1=self.bm(self.C("BM0")), op=ALU.mult),
                reads=["A", "cst"], writes=["W"])
        self.op("dve", lambda e: e.tensor_tensor(out=T["T"], in0=I8, in1=T["W"], op=ALU.subtract), reads=["W", "ident"],
                writes=["T"])
        self.op("dve", lambda e: e.tensor_tensor(out=T["W"], in0=T["AT"], in1=self.bm(self.C("BM0")), op=ALU.mult),
                reads=["AT", "cst"], writes=["W"])
        self.op("dve", lambda e: e.tensor_tensor(out=T["TT_"], in0=I8, in1=T["W"], op=ALU.subtract), reads=["W", "ident"],
                writes=["TT_"])
        okn = ["OkT", "dec"]

        def mask_level(k):
            nm = okn[k % 2]
            self.op("dve", lambda e, k=k, nm=nm: e.tensor_tensor(out=T[nm], in0=T["AT"], in1=self.bm(self.C("BM%d" % k)),
                                                                 op=ALU.mult), reads=["AT", "cst"], writes=[nm])
        mask_level(1)
        for k in range(1, 6):
            ok = okn[k % 2]
            self.mm8([0], 64, 64, lambda e, h, o, ok=ok: e.matmul(o, T[ok][:, h, :], T["T"][:, h, :], start=True,
                                                                  stop=True), [ok, "T"])
            if k < 5:
                mask_level(k + 1)
            self.evac8("act", [0], 64, 64, lambda e, v, hs: e.copy(out=T["W"], in_=v), [], ["W"])
            if k < 5:
                self.mm8([1], 64, 64, lambda e, h, o: e.matmul(o, T["TT_"][:, h, :], T["W"][:, h, :], start=True,
                                                               stop=True), ["TT_", "W"])
            self.mm8([3], 64, 64, lambda e, h, o: e.matmul(o, T["W"][:, h, :], T["TT_"][:, h, :], start=True, stop=True),
                     ["TT_", "W"])
            if k < 5:
                self.evac8("dve", [1], 64, 64, lambda e, v, hs: e.tensor_tensor(out=T["T"], in0=T["T"], in1=v,
                                                                                op=ALU.subtract), ["T"], ["T"])
            self.evac8("dve", [3], 64, 64, lambda e, v, hs: e.tensor_tensor(out=T["TT_"], in0=T["TT_"], in1=v,
                                                                            op=ALU.subtract), ["TT_"], ["TT_"])
        self.op("act", lambda e: e.copy(out=T["TTb"], in_=T["TT_"]), reads=["TT_"], writes=["TTb"])
        self.kv_tm(c0)
        self.scale_k("bk", "sd_", T["sd_"])
        self.scale_k("kend", "sc_", T["sc_"])
        self.op("dve", lambda e: e.tensor_tensor(out=T["bv"], in0=T["vtm"], in1=self.bc(be, 128), op=ALU.mult),
                reads=["vtm", "gtm"], writes=["bv"])
        self.mm8([4, 5], 64, 128, lambda e, h, o: e.matmul(o, T["TTb"][:, h, :], T["bv"][:, h, :], start=True, stop=True),
                 ["TTb", "bv"])
        self.evac8("act", [4, 5], 64, 128, lambda e, v, hs: e.copy(out=T["U0"][:, hs, :], in_=v), [], ["of_sb"])
        self.mm8([2], 128, 64, lambda e, h, o: e.matmul(o, T["bk"][:, h, :], T["TTb"][:, h, :], start=True, stop=True),
                 ["TTb", "bk"])
        self.evac8("act", [2], 128, 64, lambda e, v, hs: e.copy(out=T["WkT"], in_=v), [], ["WkT"])
        self.mm8([6, 7], 64, 128, lambda e, h, o: e.matmul(o, T["WkT"][:, h, :], T["Sb"][:, h, :], start=True, stop=True),
                 ["WkT", "Sb"])
        self.evac8("dve", [6, 7], 64, 128, lambda e, v, hs: e.tensor_tensor(out=T["Ub"][:, hs, :], in0=T["U0"][:, hs, :],
                                                                             in1=v, op=ALU.subtract), ["of_sb"], ["Ub"])
        self.mm8([4, 5], 64, 128, lambda e, h, o: e.matmul(o, T["qT"][:, h, c0:c0 + L], T["Sb"][:, h, :], start=True,
                                                          stop=True), ["qT", "Sb"])
        self.evac8("dve", [4, 5], 64, 128, lambda e, v, hs: e.tensor_tensor(
            out=T["o"][:, hs, :], in0=v, in1=self.bc(T["sb_"][:, hs], 128), op=ALU.mult), ["sb_"], ["o_sb"])
        self.mm8([6, 7], 64, 128, lambda e, h, o: e.matmul(o, T["qkTb"][:, h, :], T["Ub"][:, h, :], start=True, stop=True),
                 ["qkTb", "Ub"])
        self.evac8("dve", [6, 7], 64, 128, lambda e, v, hs: e.tensor_tensor(out=T["o"][:, hs, :], in0=T["o"][:, hs, :],
                                                                             in1=v, op=ALU.add), ["o_sb"], ["o_sb"])
        self.mm8([6, 7], 128, 128, lambda e, h, o: e.matmul(o, T["kend"][:, h, :], T["Ub"][:, h, :], start=True, stop=True),
                 ["kend", "Ub"])
        self.state_update(None, "eL", T["eL"])
        self.out_chunk(l, 1, d, c0, t0)

    def select_own(self):
        rm = self.sb("rmask_sb", [128, 4], F32)
        self.dma("sp", rm[:], self.rmask_in, writes=["rmask"])
        xO = self.dram("xO", [KC, 128, TG])
        YSO = self.dram("YSO", [3, 8, 128, TG], BF16)
        GMO = self.dram("GMO", [48, 128, TG], BF16)
        hf = self.hraw[:]
        self.op("dve", lambda e: e.memset(hf[:, 0:2], 0.0), writes=["hT"])
        ld = [hf[:, i * 1024:(i + 1) * 1024] for i in range(4)]
        ac = [hf[:, (4 + i) * 1024:(5 + i) * 1024] for i in range(2)]
        for i in range(4):
            self.alias("selL%d" % i, "hT")
        for i in range(2):
            self.alias("selA%d" % i, "hT")
        cnt = [0, 0]

        def sel(src, dst, dt, rkeys, wkeys):
            n = 1024
            def view(t, w):
                return t[:, 0:w] if dt == F32 else t[:, 0:w // 2].bitcast(BF16)
            li = cnt[0] % 4
            cnt[0] += 1
            self.dma("sp", view(ld[li], 512), src[:, 0:512], reads=[rkeys(0)], writes=["selL%d" % li])
            self.dma("sp", dst[:, 0:512], view(ld[li], 512), reads=["selL%d" % li], writes=[wkeys(0)])
            ai = cnt[1] % 2
            cnt[1] += 1
            a = view(ac[ai], n)
            for q in range(4):
                li = cnt[0] % 4
                cnt[0] += 1
                t = view(ld[li], n)
                c0 = 512 + q * n
                self.dma("sp", t, src[:, c0:c0 + n], reads=[rkeys(c0 // TT), rkeys(c0 // TT + 1)],
                         writes=["selL%d" % li])
                if q == 0:
                    self.op("dve", lambda e, t=t, a=a: e.tensor_scalar(out=a, in0=t, scalar1=rm[:, 0:1], scalar2=None,
                                                                       op0=ALU.mult),
                            reads=["selL%d" % li, "rmask"], writes=["selA%d" % ai])
                else:
                    self.op("dve", lambda e, t=t, a=a, q=q: e.scalar_tensor_tensor(
                        out=a, in0=t, scalar=rm[:, q:q + 1], in1=a, op0=ALU.mult, op1=ALU.add),
                        reads=["selL%d" % li, "selA%d" % ai, "rmask"], writes=["selA%d" % ai])
            self.dma("sp", dst[:, 512:512 + n], a, reads=["selA%d" % ai], writes=[wkeys(1), wkeys(2)])
        for k in range(KC):
            sel(self.xT[k], xO[k], F32, lambda g: ("xT", g), lambda g: ("xO", g))
        for m in range(3):
            for h in range(8):
                sel(self.YS[m, h], YSO[m, h], BF16, lambda g, m=m: (("YS", m), g), lambda g, m=m: (("YSO", m), g))
        self.xT, self.YS, self.GM = xO, YSO, GMO
        self.kx, self.kys, self.kgm = "xO", "YSO", "GMO"

    def mixer(self, l, do_m3=True):
        if l == 0:
            self.mixer_setup()
        self.mixer_params(l)
        self.conv_prep(l)
        idxs = None
        if not do_m3:
            idxs = [i for i, b in enumerate(self._B) if b[3][0] != "GM"]
        for tg in range(NTG):
            self.norm_to_hT(tg, l, 1)
            self._conv_args = (l, tg)
            self.m1_project(l, tg, idxs)
        self.bg_drain()
        self.scan_setup()
        for m in range(3):
            self.scan(l, m)
        if do_m3:
            for tg in range(NTG):
                self.m3_merge(l, tg)


_CACHE = {}


def _get_nc(cfg_key):
    if cfg_key not in _CACHE:
        k = Kern(dict(cfg_key))
        _CACHE[cfg_key] = k.build()
    return _CACHE[cfg_key]


def kernel(x_prompt, x_sample, state_mlstm_C, state_mlstm_n, state_mlstm_m, state_delta_S, state_ret_S,
           c, c_ctx, norm_g, final_norm_g, w_ada, b_ada, w_in, b_in, mlstm_f_bias, conv_w, delta_A_log,
           delta_dt_bias, ret_log_gamma, head_norm_g, w_br, w_out, ffn_w13, ffn_w2, _cfg=None):
    cfg = dict(_cfg or {})
    nc = _get_nc(tuple(sorted(cfg.items())))
    f = lambda a: np.ascontiguousarray(np.asarray(a, dtype=np.float32))
    in_maps = []
    ident = np.eye(128, dtype=np.float32)
    full = cfg.get("mixer", True)
    rc, rs = _rope_np()
    for cidx in range(8):
        b = cidx // 4
        x_tok = np.concatenate([x_prompt[2 * cidx], x_prompt[2 * cidx + 1], x_sample[b]], axis=0)
        cvec = np.stack([c_ctx, c[b]], axis=0)
        m = {
            "x_tok": f(x_tok), "cvec": f(cvec), "norm_g": f(norm_g), "final_norm_g": f(final_norm_g),
            "w_ada": f(w_ada), "b_ada": f(b_ada), "w_in": f(w_in), "b_in": f(b_in), "w_br": f(w_br),
            "w_out": f(w_out), "ffn_w13": f(ffn_w13), "ffn_w2": f(ffn_w2), "ident": ident,
        }
        rmk = np.zeros((128, 4), np.float32)
        rmk[:, cidx % 4] = 1.0
        m["rmask"] = rmk
        if full:
            m.update({
                "cst": _CST_NP, "ropeC": rc, "ropeS": rs,
                "mlstm_f_bias": f(mlstm_f_bias).reshape(DEPTH, 16), "conv_w": f(conv_w),
                "delta_A_log": f(delta_A_log).reshape(DEPTH, 16), "delta_dt_bias": f(delta_dt_bias).reshape(DEPTH, 16),
                "ret_log_gamma": f(ret_log_gamma).reshape(DEPTH, 16), "head_norm_g": f(head_norm_g),
                "st_C": f(state_mlstm_C[b]), "st_n": f(state_mlstm_n[b]), "st_m": f(state_mlstm_m[b]).reshape(DEPTH, 16),
                "st_D": f(state_delta_S[b]), "st_R": f(state_ret_S[b]),
            })
        in_maps.append(m)
    res = run_bass_kernel_spmd(nc, in_maps, core_ids=list(range(8)))
    outs = res.results
    y_prompt = np.zeros((16, 256, D), np.float32)
    y_sample = np.zeros((2, 4096, D), np.float32)
    z = lambda *s: np.zeros(s, np.float32)
    nC, nn, nm, nD, nR = z(16, 2, 2, 8, 128, 128), z(16, 2, 2, 8, 128), z(16, 2, 2, 8), z(16, 2, 2, 8, 128, 128), z(16, 2, 2, 8, 128, 128)
    for cidx in range(8):
        y = outs[cidx]["y_tok"]
        y_prompt[2 * cidx] = y[0:256]
        y_prompt[2 * cidx + 1] = y[256:512]
        r = cidx % 4
        y_sample[cidx // 4, 1024 * r:1024 * (r + 1)] = y[512:1536]
        if full:
            for si in range(2):
                nC[2 * cidx + si] = outs[cidx]["o_C"][si]
                nn[2 * cidx + si] = outs[cidx]["o_n"][si]
                nm[2 * cidx + si] = outs[cidx]["o_m"][si].reshape(DEPTH, 2, 8)
                nD[2 * cidx + si] = outs[cidx]["o_D"][si]
                nR[2 * cidx + si] = outs[cidx]["o_R"][si]
    return (y_prompt, y_sample, nC, nn, nm, nD, nR)
```

```python
import numpy as np
from contextlib import ExitStack
import concourse.bass as bass
import concourse.mybir as mybir
from concourse.bass_utils import run_bass_kernel_spmd

F32 = mybir.dt.float32
BF16 = mybir.dt.bfloat16
AF = mybir.ActivationFunctionType
ALU = mybir.AluOpType
AX = mybir.AxisListType

D = 2048
KC = 16
DEPTH = 2
HD = 128
NH = 8
WM = 1024
DFF = 4096
NMOD = 9
EPS = 1e-6
L = 64
N_IN = 18496
OFF = dict(qA=0, kA=1024, vA=2048, oA=3072, iA=4096, fA=4112, qkvB=4128, zB=7200, betaB=8224, aB=8240,
           qC=8256, kC=9280, vC=10304, gC=11328, gm=12352)
SEGS = [(0, 256), (256, 256), (512, 4096)]
TC = 4608
TT = 512
NTT = TC // TT
TG = 1536
NTG = TC // TG


def _build_cst():
    r = np.arange(64)
    LE = (r[:, None] <= r[None, :]).astype(np.float32)
    M = {"LE": LE, "GE": LE.T.copy(), "LT": (r[:, None] < r[None, :]).astype(np.float32),
         "GT": (r[:, None] > r[None, :]).astype(np.float32)}
    for k in range(6):
        M["BM%d" % k] = ((r[:, None] >> (k + 1) == r[None, :] >> (k + 1)) & (r[:, None] >> k != r[None, :] >> k)).astype(np.float32)
    cols = {}
    arr = []
    off = 0
    for n, m in M.items():
        a = np.zeros((128, 64), np.float32); a[:64] = m
        arr.append(a); cols[n] = (off, 64); off += 64
    for n, v in (("p1f", r + 1.0), ("p1b", 64.0 - r), ("p2f", 63.0 - r), ("p2b", r * 1.0)):
        a = np.zeros((128, 1), np.float32); a[:64, 0] = v
        arr.append(a); cols[n] = (off, 1); off += 1
    Rm = np.zeros((128, 128), np.float32)
    for dp in range(64):
        Rm[dp + 64, dp] = -1.0
        Rm[dp, dp + 64] = 1.0
    arr.append(Rm); cols["Rm"] = (off, 128); off += 128
    return np.concatenate(arr, axis=1), cols


_CST_NP, CST = _build_cst()
NCST = _CST_NP.shape[1]


def _rope_np():
    T = 4096
    pos_r = np.repeat(np.arange(T // 64), 64).astype(np.float32)
    pos_c = np.tile(np.arange(64), T // 64).astype(np.float32)
    nf = HD // 4
    freqs = (10000.0 ** (-np.arange(nf, dtype=np.float32) / nf)).astype(np.float32)
    ang = np.concatenate([pos_r[:, None] * freqs, pos_c[:, None] * freqs], axis=-1)
    ang = np.concatenate([ang, ang], axis=-1)
    return np.ascontiguousarray(np.cos(ang).T.astype(np.float32)), np.ascontiguousarray(np.sin(ang).T.astype(np.float32))


class Res:
    __slots__ = ("name", "w", "rs", "parent", "kids")

    def __init__(self, name):
        self.name = name
        self.w = None
        self.rs = []
        self.parent = None
        self.kids = []


class Sched:
    ENGS = ("pe", "act", "dve", "pool", "sp")

    def __init__(self, nc, es):
        self.nc = nc
        self.es = es
        self.ops = []
        self.per_eng = {e: [] for e in self.ENGS}

    def _deps(self, reads, writes, idx):
        deps = set()
        for r in reads:
            if r.w is not None:
                deps.add(r.w)
            if r.parent is not None and r.parent.w is not None:
                deps.add(r.parent.w)
            for k in r.kids:
                if k.w is not None:
                    deps.add(k.w)
        for w in writes:
            rel = [w] + w.kids + ([w.parent] if w.parent is not None else [])
            for x in rel:
                if x.w is not None:
                    deps.add(x.w)
                deps.update(x.rs)
        for r in reads:
            r.rs.append(idx)
        for w in writes:
            w.w = idx
            w.rs = []
        deps.discard(idx)
        return deps

    def op(self, eng, fn, reads=(), writes=()):
        idx = len(self.ops)
        deps = self._deps(reads, writes, idx)
        self.ops.append((eng, fn, deps, False))
        return idx

    def dma(self, eng, fn, reads=(), writes=()):
        idx = len(self.ops)
        deps = self._deps(reads, writes, idx)
        self.ops.append((eng, fn, deps, True))
        return idx

    def emit(self, final_waits=True):
        nc, es = self.nc, self.es
        NDS = {"sp": 24, "pool": 12, "act": 4}
        sem = {e: es.enter_context(nc.semaphore("s_" + e)) for e in ("pe", "act", "dve", "pool")}
        dsem = {q: [es.enter_context(nc.semaphore("d_%s%d" % (q, i))) for i in range(n)] for q, n in NDS.items()}
        cnt = {e: 0 for e in sem}
        dcnt = {q: [0] * n for q, n in NDS.items()}
        dnext = {q: 0 for q in NDS}
        handle = {}
        known = {e: {} for e in self.ENGS}
        prog = {e: [] for e in self.ENGS}
        last_dma = {}

        def need(eng, s, v):
            k = known[eng]
            if k.get(id(s), 0) < v:
                k[id(s)] = v
                prog[eng].append(("wait", s, v))

        for idx, (eng, fn, deps, is_dma) in enumerate(self.ops):
            for d in sorted(deps):
                deng = self.ops[d][0]
                if (not self.ops[d][3]) and deng == eng and eng == "pe":
                    continue
                s, v = handle[d]
                need(eng, s, v)
            if is_dma:
                q = eng
                slot = dnext[q]
                dnext[q] = (slot + 1) % len(dsem[q])
                s = dsem[q][slot]
                if dcnt[q][slot] > 0:
                    need(eng, s, 16 * dcnt[q][slot])
                dcnt[q][slot] += 1
                v = 16 * dcnt[q][slot]
                handle[idx] = (s, v)
                prog[eng].append(("dma", fn, s))
                last_dma[(q, slot)] = (s, v)
            else:
                cnt[eng] += 1
                handle[idx] = (sem[eng], cnt[eng])
                prog[eng].append(("op", fn, sem[eng]))
        for (q, slot), (s, v) in last_dma.items():
            need("sp", s, v)
        for e in ("pe", "act", "dve", "pool"):
            if cnt[e]:
                need("sp", sem[e], cnt[e])

        block = es.enter_context(nc.Block())

        def run(engobj, items):
            for it in items:
                if it[0] == "wait":
                    engobj.wait_ge(it[1], it[2])
                elif it[0] == "dma":
                    it[1](engobj).then_inc(it[2], 16)
                else:
                    it[1](engobj).then_inc(it[2], 1)

        @block.sync
        def _(e):
            run(e, prog["sp"])

        @block.gpsimd
        def _(e):
            run(e, prog["pool"])

        @block.scalar
        def _(e):
            run(e, prog["act"])

        @block.vector
        def _(e):
            run(e, prog["dve"])

        @block.tensor
        def _(e):
            run(e, prog["pe"])


class Builder:
    def __init__(self, cfg):
        self.cfg = cfg
        self.nc = bass.Bass("TRN2", target_bir_lowering=False)
        self.es = ExitStack()
        self.S = Sched(self.nc, self.es)
        self.res = {}
        self.tiles = {}

    def R(self, key):
        r = self.res.get(key)
        if r is None:
            r = self.res[key] = Res(str(key))
        return r

    def alias(self, child, parent):
        c, p = self.R(child), self.R(parent)
        c.parent = p
        p.kids.append(c)

    def sb(self, name, shape, dt=F32):
        t = self.es.enter_context(self.nc.sbuf_tensor(name, list(shape), dt))
        self.tiles[name] = t
        return t

    def ps(self, name, shape, dt=F32):
        t = self.es.enter_context(self.nc.psum_tensor(name, list(shape), dt))
        self.tiles[name] = t
        return t

    def dram(self, name, shape, dt=F32, kind="Internal"):
        return self.nc.dram_tensor(name, list(shape), dt, kind=kind).ap()

    def op(self, eng, fn, reads=(), writes=()):
        return self.S.op(eng, fn, [self.R(r) for r in reads], [self.R(w) for w in writes])

    def dma(self, q, out, in_, reads=(), writes=(), **kw):
        return self.S.dma(q, lambda e: e.dma_start(out=out, in_=in_, **kw),
                          [self.R(r) for r in reads], [self.R(w) for w in writes])


def _act(func, out, in_, **kw):
    return lambda e: e.activation(out=out, in_=in_, func=func, **kw)


class Kern(Builder):
    def build(self):
        nc = self.nc
        cfg = self.cfg
        di = lambda n, s: nc.dram_tensor(n, list(s), F32, kind="ExternalInput").ap()
        do = lambda n, s: nc.dram_tensor(n, list(s), F32, kind="ExternalOutput").ap()
        self.x_tok = di("x_tok", [TC, D])
        self.cvec = di("cvec", [2, D])
        self.norm_g = di("norm_g", [DEPTH, 3, D])
        self.final_g = di("final_norm_g", [D])
        self.w_ada = di("w_ada", [DEPTH, D, NMOD * D])
        self.b_ada = di("b_ada", [DEPTH, NMOD * D])
        self.w_in = di("w_in", [DEPTH, D, N_IN])
        self.b_in = di("b_in", [DEPTH, N_IN])
        self.w_br = di("w_br", [DEPTH, 3, WM, D])
        self.w_out = di("w_out", [DEPTH, D, D])
        self.w13 = di("ffn_w13", [DEPTH, 2, D, 2 * DFF])
        self.w2 = di("ffn_w2", [DEPTH, 2, DFF, D])
        self.ident_in = di("ident", [128, 128])
        self.y_tok = do("y_tok", [TG, D])
        self.rmask_in = di("rmask", [128, 4])
        self.xT = self.dram("xT", [KC, 128, TC])

        self.ident = self.sb("ident_sb", [128, 128], F32)
        self.dma("sp", self.ident[:], self.ident_in, writes=["ident"])
        self.ones_bf = self.sb("ones_bf", [128, 128], BF16)
        self.op("dve", lambda e: e.memset(self.ones_bf[:], 1.0), writes=["ones_bf"])
        self.eps_t = self.sb("eps_t", [128, 1], F32)
        self.op("dve", lambda e: e.memset(self.eps_t[:], EPS), writes=["eps_t"])

        self.NWR = 4
        self.wr = [self.sb("wr%d" % i, [128, KC, 128], BF16) for i in range(self.NWR)]
        self.wr_i = 0
        self.kx, self.kys, self.kgm = "xT", "YS", "GM"
        self.bg = None
        self.bg_n = 0
        self._start_conv = False
        self._conv_args = None
        self.NPS = 8
        self.psb = [self.ps("psb%d" % i, [128, 512], F32) for i in range(8)]
        self.ps_i = 0
        hraw = self.sb("hT", [128, KC * TG // 2], F32)
        self.hraw = hraw
        self.hT = hraw[:].bitcast(BF16).rearrange("p (k t) -> p k t", k=KC)

        graw = self.sb("gT", [128, KC * TG // 2], F32)
        self.graw = graw
        gf = graw[:]
        self.gT = graw[:].bitcast(BF16).rearrange("p (k t) -> p k t", k=KC)
        self._xl = [gf[:, i * 2048:(i + 1) * 2048] for i in range(2)]
        self._xo = [gf[:, 4096 + i * 2048:4096 + (i + 1) * 2048].rearrange("p (k t) -> p k t", k=KC) for i in range(2)]
        self._yn = gf[:, 0:8192].rearrange("p (k t) -> p k t", k=KC)
        self._yo = [gf[:, 8192 + i * 2048:8192 + (i + 1) * 2048] for i in range(2)]
        for nm in ("xl0", "xl1", "xo0", "xo1", "yn", "yo0", "yo1"):
            self.alias(nm, "gT")
        self.xq = [self.sb("xq%d" % i, [128, 4, TT], F32) for i in range(3)]
        self.xq_i = 0
        self.sq = [self.sb("sq%d" % i, [128, 4, TT], BF16) for i in range(2)]
        self.rstd = self.sb("rstd", [128, TT], F32)
        self.tmpf = [self.sb("tmpf%d" % i, [128, TT], F32) for i in range(3)]
        self.tmp_i = 0
        self.xr = [self.sb("xr%d" % i, [128, TT], F32) for i in range(4)]
        self.xr_i = 0

        self.phase_load_x()
        self.phase_adaln()
        for l in range(DEPTH):
            for tg in range(NTG):
                self.norm_to_hT(tg, l, 0)
                self.ffn(tg, l, 0)
            last = (l == DEPTH - 1) and cfg.get("mixer", True)
            if cfg.get("mixer", True):
                self.mixer(l, do_m3=not last)
            if last:
                self.select_own()
                self.norm_to_hT(0, l, 1)
                self.m1_project(l, 0, [i for i, b in enumerate(self._B) if b[3][0] == "GM"], conv_bg=False)
                self.m3_merge(l, 0)
                self.norm_to_hT(0, l, 2)
                self.ffn(0, l, 1)
            else:
                for tg in range(NTG):
                    self.norm_to_hT(tg, l, 2)
                    self.ffn(tg, l, 1)
        self.phase_final(TG // TT if cfg.get("mixer", True) else TG // TT)
        self.S.emit()
        return nc

    def next_ps(self):
        i = self.ps_i
        self.ps_i = (i + 1) % self.NPS
        return i

    def next_tmp(self):
        i = self.tmp_i
        self.tmp_i = (i + 1) % len(self.tmpf)
        return i

    def phase_load_x(self):
        xl, xo = self._xl, self._xo
        for t in range(TC // 128):
            b = t % 2
            self.dma("sp", xl[b], self.x_tok[t * 128:(t + 1) * 128, :], writes=["xl%d" % b])
            for q in range(4):
                pi = self.next_ps()
                ps = self.psb[pi]

                def tr(e, b=b, q=q, ps=ps):
                    for j in range(4):
                        k = q * 4 + j
                        r = e.transpose(out=ps[:, j * 128:(j + 1) * 128], in_=xl[b][:, k * 128:(k + 1) * 128],
                                        identity=self.ident[:])
                    return r
                self.op("pe", tr, reads=["xl%d" % b, "ident"], writes=["psb%d" % pi])
                self.op("dve", lambda e, b=b, q=q, ps=ps: e.tensor_copy(
                    out=xo[b][:, q * 4:(q + 1) * 4, :], in_=ps[:].rearrange("p (a b) -> p a b", a=4)),
                    reads=["psb%d" % pi], writes=["xo%d" % b])
            self.dma("sp", self.xT[:, :, t * 128:(t + 1) * 128].rearrange("k p t -> p k t"), xo[b],
                     reads=["xo%d" % b], writes=[("xT", t // 4)])

    def load_vecT(self, name, src_1d, nblk):
        t = self.sb(name, [128, nblk], F32)
        self.dma("sp", t[:], src_1d.rearrange("(j p) -> p j", p=128), writes=[name],
                 allow_slow_non_contiguous=True)
        return t

    def phase_adaln(self):
        nc = self.nc
        cT = self.sb("cT", [128, KC, 2], F32)
        for b in range(2):
            self.dma("sp", cT[:, :, b], self.cvec[b].rearrange("(k p) -> p k", p=128), writes=["cT"],
                     allow_slow_non_contiguous=True)
        cS = self.sb("cS", [128, KC, 2], BF16)
        self.op("act", _act(AF.Silu, cS[:], cT[:]), reads=["cT"], writes=["cS"])
        self.mod = []
        self.nsc, self.nbi, self.gsc = {}, {}, {}
        for l in range(DEPTH):
            bT = self.load_vecT("badaT%d" % l, self.b_ada[l], NMOD * KC)
            gT_ = [self.load_vecT("ng%d_%d" % (l, i), self.norm_g[l, i], KC) for i in range(3)]
            mod = self.sb("mod%d" % l, [128, NMOD * KC, 2], F32)
            self.mod.append(mod)

            def evac(blk, tt, ps, pi, mod=mod, bT=bT, l=l):
                self.op("dve", lambda e: e.tensor_scalar(out=mod[:, blk, :], in0=ps[:, 0:2], scalar1=bT[:, blk:blk + 1],
                                                         scalar2=None, op0=ALU.add),
                        reads=["psb%d" % pi, "badaT%d" % l], writes=["mod%d" % l])
            blocks = [(self.w_ada[l][:, j * 128:(j + 1) * 128], KC, 128) for j in range(NMOD * KC)]
            self.linear(blocks, lambda k, tt: cS[:, k, :], ["cS"], [2], evac)
            for i in range(3):
                sc = self.sb("nsc%d_%d" % (l, i), [128, KC, 2], F32)
                bi = self.sb("nbi%d_%d" % (l, i), [128, KC, 2], F32)
                gs = self.sb("gsc%d_%d" % (l, i), [128, KC, 2], F32)
                m0 = mod[:, (3 * i) * KC:(3 * i + 1) * KC, :]
                m1 = mod[:, (3 * i + 1) * KC:(3 * i + 2) * KC, :]
                m2 = mod[:, (3 * i + 2) * KC:(3 * i + 3) * KC, :]
                for b in range(2):
                    self.op("dve", lambda e, sc=sc, m1=m1, b=b, g=gT_[i]: e.scalar_tensor_tensor(
                        out=sc[:, :, b], in0=m1[:, :, b], scalar=1.0, in1=g[:], op0=ALU.add, op1=ALU.mult),
                        reads=["mod%d" % l, "ng%d_%d" % (l, i)], writes=["nsc%d_%d" % (l, i)])
                self.op("dve", lambda e, bi=bi, m0=m0: e.tensor_copy(out=bi[:], in_=m0), reads=["mod%d" % l],
                        writes=["nbi%d_%d" % (l, i)])
                fac = 1.0 if i == 1 else 0.5
                self.op("dve", lambda e, gs=gs, m2=m2, fac=fac: e.tensor_scalar(
                    out=gs[:], in0=m2, scalar1=fac, scalar2=None, op0=ALU.mult), reads=["mod%d" % l],
                    writes=["gsc%d_%d" % (l, i)])
                self.nsc[(l, i)], self.nbi[(l, i)], self.gsc[(l, i)] = sc, bi, gs

    def linear(self, blocks, in_fn, in_res, ntoks, evac, pf=3):
        nb = len(blocks)
        slots = {}

        def issue(j):
            ap, kc, ncols = blocks[j]
            s = self.wr_i
            self.wr_i = (s + 1) % self.NWR
            slots[j] = s
            self.dma("pool", self.wr[s][:, 0:kc, 0:ncols], ap.rearrange("(k p) n -> p k n", p=128),
                     writes=["wr%d" % s])
        for j in range(min(pf, nb)):
            issue(j)
        for j in range(nb):
            if j + pf < nb:
                issue(j + pf)
            ap, kc, ncols = blocks[j]
            s = slots[j]
            for tt, nt in enumerate(ntoks):
                pi = self.next_ps()
                ps = self.psb[pi]

                def mm(e, s=s, kc=kc, ncols=ncols, tt=tt, nt=nt, ps=ps):
                    for k in range(kc):
                        r = e.matmul(ps[0:ncols, 0:nt], self.wr[s][:, k, 0:ncols], in_fn(k, tt),
                                     start=(k == 0), stop=(k == kc - 1))
                    return r
                self.op("pe", mm, reads=["wr%d" % s] + list(in_res), writes=["psb%d" % pi])
                evac(j, tt, ps, pi)
                if self._start_conv:
                    self._start_conv = False
                    self.bg_drain()
                    self.bg = self.conv_gen(*self._conv_args)
                if self.bg is not None:
                    self.bg_n += 1
                    if self.bg_n % 3 == 0:
                        self.bg_step()

    def bg_step(self):
        try:
            next(self.bg)
        except StopIteration:
            self.bg = None

    def bg_drain(self):
        while self.bg is not None:
            self.bg_step()

    def cvi(self, tg, tt):
        return 0 if (tg == 0 and tt == 0) else 1

    def load_xq(self, t0, q):
        xi = self.xq_i
        self.xq_i = (xi + 1) % len(self.xq)
        self.dma("sp", self.xq[xi][:], self.xT[q * 4:(q + 1) * 4, :, t0:t0 + TT].rearrange("k p t -> p k t"),
                 reads=[(self.kx, t0 // TT)], writes=["xq%d" % xi])
        return xi

    def rstd_tile(self, t0):
        pi = self.next_ps()
        ps = self.psb[pi]
        for q in range(4):
            xi = self.load_xq(t0, q)
            sqt = self.sq[q % 2]
            self.op("act", _act(AF.Square, sqt[:], self.xq[xi][:]), reads=["xq%d" % xi], writes=["sq%d" % (q % 2)])

            def mm(e, ps=ps, q=q, sqt=sqt):
                for k in range(4):
                    r = e.matmul(ps[:, :], self.ones_bf[:], sqt[:, k, :], start=(q == 0 and k == 0),
                                 stop=(q == 3 and k == 3))
                return r
            self.op("pe", mm, reads=["sq%d" % (q % 2), "ones_bf"], writes=["psb%d" % pi])
        self.op("act", _act(AF.Sqrt, self.rstd[:], ps[:, :], scale=1.0 / D, bias=self.eps_t[:]),
                reads=["psb%d" % pi, "eps_t"], writes=["rstd"])
        self.op("dve", lambda e: e.reciprocal(out=self.rstd[:], in_=self.rstd[:]), reads=["rstd"], writes=["rstd"])

    def norm_to_hT(self, tg, l, i):
        sc, bi = self.nsc[(l, i)], self.nbi[(l, i)]
        for tt in range(TG // TT):
            t0 = tg * TG + tt * TT
            self.rstd_tile(t0)
            cv = self.cvi(tg, tt)
            for q in range(4):
                xi = self.load_xq(t0, q)
                for kk in range(4):
                    k = q * 4 + kk
                    ti = self.next_tmp()
                    tm = self.tmpf[ti]
                    self.op("dve", lambda e, kk=kk, tm=tm, xi=xi: e.tensor_tensor(
                        out=tm[:], in0=self.xq[xi][:, kk, :], in1=self.rstd[:], op=ALU.mult),
                        reads=["xq%d" % xi, "rstd"], writes=["tmpf%d" % ti])
                    self.op("act", _act(AF.Identity, self.hT[:, k, tt * TT:(tt + 1) * TT], tm[:],
                                        scale=sc[:, k, cv:cv + 1], bias=bi[:, k, cv:cv + 1]),
                            reads=["tmpf%d" % ti, "nsc%d_%d" % (l, i), "nbi%d_%d" % (l, i)], writes=["hT"])

    def resid_evac(self, tg, gs, gs_name):
        def evac(blk, tt, ps, pi):
            t0 = tg * TG + tt * TT
            gtt = t0 // TT
            xi = self.xr_i
            self.xr_i = (xi + 1) % len(self.xr)
            xr = self.xr[xi]
            cv = self.cvi(tg, tt)
            self.dma("sp", xr[:], self.xT[blk, :, t0:t0 + TT], reads=[(self.kx, gtt)], writes=["xr%d" % xi])
            self.op("dve", lambda e: e.scalar_tensor_tensor(out=xr[:], in0=ps[:, :], scalar=gs[:, blk, cv:cv + 1],
                                                            in1=xr[:], op0=ALU.mult, op1=ALU.add),
                    reads=["psb%d" % pi, "xr%d" % xi, gs_name], writes=["xr%d" % xi])
            self.dma("sp", self.xT[blk, :, t0:t0 + TT], xr[:], reads=["xr%d" % xi], writes=[(self.kx, gtt)])
        return evac

    def ffn(self, tg, l, which):
        i = 0 if which == 0 else 2
        w13 = self.w13[l, which]
        w2 = self.w2[l, which]
        gs = self.gsc[(l, i)]
        NJ = DFF // 128 // 2
        for half in range(2):
            blocks = []
            for jj in range(NJ):
                j = half * NJ + jj
                blocks.append((w13[:, j * 128:(j + 1) * 128], KC, 128))
                blocks.append((w13[:, DFF + j * 128:DFF + (j + 1) * 128], KC, 128))
            sa = {}

            def evac13(blk, tt, ps, pi, sa=sa):
                jj, isb = blk // 2, blk % 2
                if not isb:
                    ti = self.next_tmp()
                    sa[tt] = ti
                    self.op("act", _act(AF.Silu, self.tmpf[ti][:], ps[:, :]), reads=["psb%d" % pi],
                            writes=["tmpf%d" % ti])
                else:
                    ti = sa[tt]
                    self.op("dve", lambda e: e.tensor_tensor(out=self.gT[:, jj, tt * TT:(tt + 1) * TT],
                                                             in0=self.tmpf[ti][:], in1=ps[:, :], op=ALU.mult),
                            reads=["psb%d" % pi, "tmpf%d" % ti], writes=["gT"])
            self.linear(blocks, lambda k, tt: self.hT[:, k, tt * TT:(tt + 1) * TT], ["hT"], [TT] * 3, evac13)
            r0 = half * (DFF // 2)
            blocks2 = [(w2[r0:r0 + DFF // 2, j * 128:(j + 1) * 128], KC, 128) for j in range(KC)]
            self.linear(blocks2, lambda k, tt: self.gT[:, k, tt * TT:(tt + 1) * TT], ["gT"], [TT] * 3,
                        self.resid_evac(tg, gs, "gsc%d_%d" % (l, i)))

    def phase_final(self, ntt):
        fg = self.load_vecT("fgT", self.final_g, KC)
        yo, yn = self._yo, self._yn
        for gtt in range(ntt):
            t0 = gtt * TT
            self.rstd_tile(t0)
            for q in range(4):
                xi = self.load_xq(t0, q)
                for kk in range(4):
                    k = q * 4 + kk
                    self.op("dve", lambda e, k=k, kk=kk, xi=xi: e.scalar_tensor_tensor(
                        out=yn[:, k, :], in0=self.xq[xi][:, kk, :], scalar=fg[:, k:k + 1], in1=self.rstd[:],
                        op0=ALU.mult, op1=ALU.mult), reads=["xq%d" % xi, "rstd", "fgT"], writes=["yn"])
            for s_ in range(TT // 128):
                b = (gtt * 4 + s_) % 2
                for q in range(4):
                    pi = self.next_ps()
                    ps = self.psb[pi]

                    def tr(e, q=q, ps=ps, s_=s_):
                        for j in range(4):
                            k = q * 4 + j
                            r = e.transpose(out=ps[:, j * 128:(j + 1) * 128], in_=yn[:, k, s_ * 128:(s_ + 1) * 128],
                                            identity=self.ident[:])
                        return r
                    self.op("pe", tr, reads=["yn", "ident"], writes=["psb%d" % pi])
                    self.op("act", lambda e, b=b, q=q, ps=ps: e.copy(out=yo[b][:, q * 512:(q + 1) * 512], in_=ps[:, :]),
                            reads=["psb%d" % pi], writes=["yo%d" % b])
                r0 = t0 + s_ * 128
                self.dma("sp", self.y_tok[r0:r0 + 128, :], yo[b], reads=["yo%d" % b], writes=[("y", r0)])

    def mixer_setup(self):
        nc = self.nc
        di = lambda n, s: nc.dram_tensor(n, list(s), F32, kind="ExternalInput").ap()
        do = lambda n, s: nc.dram_tensor(n, list(s), F32, kind="ExternalOutput").ap()
        self.cst_in = di("cst", [128, NCST])
        self.ropeC = di("ropeC", [128, 4096])
        self.ropeS = di("ropeS", [128, 4096])
        self.f_bias = di("mlstm_f_bias", [DEPTH, 16])
        self.conv_w = di("conv_w", [DEPTH, 3, 3 * WM])
        self.A_log = di("delta_A_log", [DEPTH, 16])
        self.dt_bias = di("delta_dt_bias", [DEPTH, 16])
        self.lgam = di("ret_log_gamma", [DEPTH, 16])
        self.hng = di("head_norm_g", [DEPTH, 3, HD])
        self.sC = di("st_C", [DEPTH, 2, NH, HD, HD])
        self.sn = di("st_n", [DEPTH, 2, NH, HD])
        self.sm = di("st_m", [DEPTH, 16])
        self.sD = di("st_D", [DEPTH, 2, NH, HD, HD])
        self.sR = di("st_R", [DEPTH, 2, NH, HD, HD])
        self.oC = do("o_C", [2, DEPTH, 2, NH, HD, HD])
        self.on = do("o_n", [2, DEPTH, 2, NH, HD])
        self.om = do("o_m", [2, DEPTH, 16])
        self.oD = do("o_D", [2, DEPTH, 2, NH, HD, HD])
        self.oR = do("o_R", [2, DEPTH, 2, NH, HD, HD])
        self.P = {m: self.dram("P" + m, [24, 128, TC], BF16) for m in "ABC"}
        self.PRE = self.dram("PRE", [24, 128, TC], F32)
        self.GO = self.dram("GO", [3, 8, 128, TC], BF16)
        self.GM = self.dram("GM", [48, 128, TC], BF16)
        self.GA = self.dram("GA", [4, 16, TC], F32)
        self.YS = self.dram("YS", [3, 8, 128, TC], BF16)
        self.OFs = self.dram("OFs", [TC, WM], F32)
        self.MG = self.dram("MG", [KC, 128, TC], BF16)
        self.cst = self.sb("cst_sb", [128, NCST], F32)
        self.dma("sp", self.cst[:], self.cst_in, writes=["cst"])
        self.ident_bf = self.sb("ident_bf", [128, 128], BF16)
        self.op("dve", lambda e: e.tensor_copy(out=self.ident_bf[:], in_=self.ident[:]), reads=["ident"], writes=["ident_bf"])
        self.ones_f = self.sb("ones_f", [128, 256], F32)
        self.op("dve", lambda e: e.memset(self.ones_f[:], 1.0), writes=["ones_f"])
        self.stb = [self.sb("stb%d" % i, [128, TT], BF16) for i in range(3)]
        self.stb_i = 0
        self.binT = self.sb("binT", [128, 160], F32)
        self.gpar = self.sb("gpar", [16, 8], F32)
        self.rC = self.sb("rC", [128, 3, TT], F32)
        self.rS = self.sb("rS", [128, 3, TT], F32)

    def C(self, name, rows=64):
        o, n = CST[name]
        return self.cst[0:rows, o:o + n]

    def next_stb(self):
        i = self.stb_i
        self.stb_i = (i + 1) % 3
        return i

    def w_in_blocks(self, l):
        B = []
        s = HD ** -0.5
        for j in range(24):
            B.append((OFF["qkvB"] + j * 128, 128, "pre", ("PRE", j), 1.0))
        for j in range(8):
            B.append((OFF["qA"] + j * 128, 128, "lin", ("PA", j), 1.0))
        for j in range(8):
            B.append((OFF["kA"] + j * 128, 128, "lin", ("PA", 8 + j), s))
        for j in range(8):
            B.append((OFF["vA"] + j * 128, 128, "lin", ("PA", 16 + j), 1.0))
        for j in range(8):
            B.append((OFF["oA"] + j * 128, 128, "sig", ("GO", 0, j), 1.0))
        B.append((OFF["iA"], 16, "gi", ("GA", 0), 1.0))
        B.append((OFF["fA"], 16, "gf", ("GA", 1), 1.0))
        for j in range(8):
            B.append((OFF["zB"] + j * 128, 128, "silu", ("GO", 1, j), 1.0))
        B.append((OFF["betaB"], 16, "gb", ("GA", 2), 1.0))
        B.append((OFF["aB"], 16, "gg", ("GA", 3), 1.0))
        for j in range(8):
            B.append((OFF["qC"] + j * 128, 128, "rope", ("PC", j), s))
        for j in range(8):
            B.append((OFF["kC"] + j * 128, 128, "rope", ("PC", 8 + j), 1.0))
        for j in range(8):
            B.append((OFF["vC"] + j * 128, 128, "lin", ("PC", 16 + j), 1.0))
        for j in range(8):
            B.append((OFF["gC"] + j * 128, 128, "silu", ("GO", 2, j), 1.0))
        for j in range(48):
            B.append((OFF["gm"] + j * 128, 128, "sig", ("GM", j), 1.0))
        return B

    def dst_ap(self, dst, t0, n, rows=128):
        if dst[0] in ("PA", "PB", "PC"):
            return self.P[dst[0][1]][dst[1], 0:rows, t0:t0 + n]
        if dst[0] == "PRE":
            return self.PRE[dst[1], 0:rows, t0:t0 + n]
        if dst[0] == "GO":
            return self.GO[dst[1], dst[2], 0:rows, t0:t0 + n]
        if dst[0] == "GM":
            return self.GM[dst[1], 0:rows, t0:t0 + n]
        if dst[0] == "GA":
            return self.GA[dst[1], 0:rows, t0:t0 + n]
        raise KeyError(dst)

    def mixer_params(self, l):
        B = self.w_in_blocks(l)
        self._B = B
        for bi, (c0, nco, kind, dst, sc) in enumerate(B):
            self.dma("sp", self.binT[0:nco, bi:bi + 1], self.b_in[l, c0:c0 + nco].rearrange("(p o) -> p o", o=1),
                     writes=["binT"], allow_slow_non_contiguous=True)
        gp = self.gpar
        for j, src in enumerate((self.f_bias, self.A_log, self.dt_bias)):
            self.dma("sp", gp[:, j:j + 1], src[l].rearrange("(p o) -> p o", o=1), writes=["gpar"],
                     allow_slow_non_contiguous=True)
        bi_f = [i for i, b in enumerate(B) if b[2] == "gf"][0]
        bi_g = [i for i, b in enumerate(B) if b[2] == "gg"][0]
        self.op("dve", lambda e: e.scalar_tensor_tensor(out=gp[:, 3:4], in0=gp[:, 0:1], scalar=-1.0,
                                                        in1=self.binT[0:16, bi_f:bi_f + 1], op0=ALU.mult, op1=ALU.subtract),
                reads=["gpar", "binT"], writes=["gpar"])
        self.op("dve", lambda e: e.tensor_tensor(out=gp[:, 4:5], in0=gp[:, 2:3], in1=self.binT[0:16, bi_g:bi_g + 1],
                                                 op=ALU.add), reads=["gpar", "binT"], writes=["gpar"])
        self.op("act", _act(AF.Exp, gp[:, 5:6], gp[:, 1:2]), reads=["gpar"], writes=["gpar"])
        self.op("dve", lambda e: e.tensor_scalar(out=gp[:, 5:6], in0=gp[:, 5:6], scalar1=-1.0, scalar2=None, op0=ALU.mult),
                reads=["gpar"], writes=["gpar"])

    def m1_project(self, l, tg, idxs=None, conv_bg=True):
        B0 = self._B
        if idxs is None:
            idxs = list(range(len(B0)))
        B = [B0[i] for i in idxs]
        blocks = [(self.w_in[l][:, c0:c0 + nco], KC, nco) for (c0, nco, kind, dst, sc) in B]
        for tt in range(3):
            if conv_bg and self.cvi(tg, tt):
                p0 = tg * TG + tt * TT - 512
                self.dma("sp", self.rC[:, tt, :], self.ropeC[:, p0:p0 + TT], writes=["rC"])
                self.dma("sp", self.rS[:, tt, :], self.ropeS[:, p0:p0 + TT], writes=["rS"])
        gp = self.gpar

        def evac(blk, tt, ps, pi):
            c0, nco, kind, dst, sc = B[blk]
            t0 = tg * TG + tt * TT
            gtt = t0 // TT
            bias = self.binT[0:nco, idxs[blk]:idxs[blk] + 1]
            pr = ["psb%d" % pi, "binT"]
            wr = [(dst, gtt)]
            if dst[0] == "GM":
                wr = [((self.kgm, dst[1]), gtt)]
            if conv_bg and kind == "pre" and dst[1] == 23 and tt == 2:
                self._start_conv = True
            if kind in ("lin", "sig", "silu") or (kind == "rope" and not self.cvi(tg, tt)):
                si = self.next_stb()
                st = self.stb[si]
                if kind in ("lin", "rope"):
                    self.op("dve", lambda e: e.tensor_scalar(out=st[:], in0=ps[:, :], scalar1=bias, scalar2=sc,
                                                             op0=ALU.add, op1=ALU.mult), reads=pr, writes=["stb%d" % si])
                else:
                    f = AF.Sigmoid if kind == "sig" else AF.Silu
                    self.op("act", _act(f, st[:], ps[:, :], bias=bias), reads=pr, writes=["stb%d" % si])
                self.dma("sp", self.dst_ap(dst, t0, TT), st[:], reads=["stb%d" % si], writes=wr)
            elif kind == "rope":
                ti = self.next_tmp()
                xf = self.tmpf[ti]
                self.op("dve", lambda e: e.tensor_scalar(out=xf[:], in0=ps[:, :], scalar1=bias, scalar2=sc,
                                                         op0=ALU.add, op1=ALU.mult), reads=pr, writes=["tmpf%d" % ti])
                p2 = self.next_ps()
                ps2 = self.psb[p2]
                self.op("pe", lambda e: e.matmul(ps2[:, :], self.C("Rm", 128), xf[:], start=True, stop=True),
                        reads=["tmpf%d" % ti, "cst"], writes=["psb%d" % p2])
                t2 = self.next_tmp()
                x2 = self.tmpf[t2]
                self.op("dve", lambda e: e.tensor_tensor(out=x2[:], in0=ps2[:, :], in1=self.rS[:, tt, :], op=ALU.mult),
                        reads=["psb%d" % p2, "rS"], writes=["tmpf%d" % t2])
                self.op("dve", lambda e: e.tensor_tensor(out=xf[:], in0=xf[:], in1=self.rC[:, tt, :], op=ALU.mult),
                        reads=["tmpf%d" % ti, "rC"], writes=["tmpf%d" % ti])
                si = self.next_stb()
                st = self.stb[si]
                self.op("dve", lambda e: e.tensor_tensor(out=st[:], in0=xf[:], in1=x2[:], op=ALU.add),
                        reads=["tmpf%d" % ti, "tmpf%d" % t2], writes=["stb%d" % si])
                self.dma("sp", self.dst_ap(dst, t0, TT), st[:], reads=["stb%d" % si], writes=wr)
            else:
                ti = self.next_tmp()
                tm = self.tmpf[ti]
                tr = ["tmpf%d" % ti]
                n = nco
                if kind == "pre" or kind == "gi":
                    self.op("dve", lambda e: e.tensor_scalar(out=tm[0:n, :], in0=ps[0:n, :], scalar1=bias, scalar2=None,
                                                             op0=ALU.add), reads=pr, writes=tr)
                elif kind == "gb":
                    self.op("act", _act(AF.Sigmoid, tm[0:n, :], ps[0:n, :], bias=bias), reads=pr, writes=tr)
                elif kind == "gf":
                    self.op("act", _act(AF.Exp, tm[0:n, :], ps[0:n, :], scale=-1.0, bias=gp[:, 3:4]),
                            reads=pr + ["gpar"], writes=tr)
                    self.op("act", _act(AF.Ln, tm[0:n, :], tm[0:n, :], bias=1.0), reads=tr, writes=tr)
                    self.op("dve", lambda e: e.tensor_scalar(out=tm[0:n, :], in0=tm[0:n, :], scalar1=-1.0, scalar2=None,
                                                             op0=ALU.mult), reads=tr, writes=tr)
                elif kind == "gg":
                    self.op("act", _act(AF.Exp, tm[0:n, :], ps[0:n, :], bias=gp[:, 4:5]), reads=pr + ["gpar"], writes=tr)
                    self.op("act", _act(AF.Ln, tm[0:n, :], tm[0:n, :], bias=1.0), reads=tr, writes=tr)
                    self.op("dve", lambda e: e.tensor_scalar(out=tm[0:n, :], in0=tm[0:n, :], scalar1=gp[:, 5:6],
                                                             scalar2=None, op0=ALU.mult), reads=tr + ["gpar"], writes=tr)
                self.dma("sp", self.dst_ap(dst, t0, TT, rows=n), tm[0:n, :], reads=tr, writes=wr)
        self.linear(blocks, lambda k, tt: self.hT[:, k, tt * TT:(tt + 1) * TT], ["hT"], [TT] * 3, evac)

    def conv_prep(self, l):
        self._cw = [self.load_vecT("cw%d_%d" % (l, j), self.conv_w[l, j], 24) for j in range(3)]
        if l == 0:
            self._cv = self.sb("cvt0", [128, TT + 2], F32)

    def conv_gen(self, l, grp):
        cw = self._cw
        cv = self._cv
        pieces = [(0, 256, True, True), (256, 256, True, True)]
        for i in range(8):
            pieces.append((512 + i * 512, 512, i == 0, i == 7))
        pieces = [p for p in pieces if (p[0] + p[1] + (0 if p[3] else 1) - 1) // TG == grp]
        for blk in range(24):
            for (t0, n, first, last) in pieces:
                lo = t0 - (0 if first else 1)
                hi = t0 + n + (0 if last else 1)
                if first or last:
                    self.op("dve", lambda e: e.memset(cv[:, 0:n + 2], 0.0), writes=["cvt"])
                gts = sorted(set([lo // TT, (hi - 1) // TT]))
                self.dma("sp", cv[:, (lo - t0 + 1):(hi - t0 + 1)], self.PRE[blk, :, lo:hi],
                         reads=[(("PRE", blk), g) for g in gts], writes=["cvt"])
                ti = self.next_tmp()
                y = self.tmpf[ti]
                tr = ["tmpf%d" % ti]
                names = ["cw%d_%d" % (l, j) for j in range(3)]
                self.op("dve", lambda e, y=y, n=n, blk=blk: e.tensor_scalar(
                    out=y[:, 0:n], in0=cv[:, 1:n + 1], scalar1=cw[1][:, blk:blk + 1], scalar2=None, op0=ALU.mult),
                    reads=["cvt"] + names, writes=tr)
                self.op("dve", lambda e, y=y, n=n, blk=blk: e.scalar_tensor_tensor(
                    out=y[:, 0:n], in0=cv[:, 0:n], scalar=cw[0][:, blk:blk + 1], in1=y[:, 0:n], op0=ALU.mult,
                    op1=ALU.add), reads=["cvt"] + names + tr, writes=tr)
                self.op("dve", lambda e, y=y, n=n, blk=blk: e.scalar_tensor_tensor(
                    out=y[:, 0:n], in0=cv[:, 2:n + 2], scalar=cw[2][:, blk:blk + 1], in1=y[:, 0:n], op0=ALU.mult,
                    op1=ALU.add), reads=["cvt"] + names + tr, writes=tr)
                self.op("act", _act(AF.Silu, y[:, 0:n], y[:, 0:n]), reads=tr, writes=tr)
                si = self.next_stb()
                st = self.stb[si]
                if blk < 16:
                    sq = self.sq[0]
                    self.op("act", _act(AF.Square, sq[:, 0, 0:n], y[:, 0:n]), reads=tr, writes=["sq0"])
                    pi = self.next_ps()
                    ps = self.psb[pi]
                    self.op("pe", lambda e, ps=ps, n=n, sq=sq: e.matmul(ps[:, 0:n], self.ones_bf[:], sq[:, 0, 0:n],
                                                                        start=True, stop=True),
                            reads=["sq0", "ones_bf"], writes=["psb%d" % pi])
                    self.op("act", _act(AF.Sqrt, self.rstd[:, 0:n], ps[:, 0:n], bias=self.eps_t[:]),
                            reads=["psb%d" % pi, "eps_t"], writes=["rstd"])
                    self.op("dve", lambda e, n=n: e.reciprocal(out=self.rstd[:, 0:n], in_=self.rstd[:, 0:n]),
                            reads=["rstd"], writes=["rstd"])
                    sc = HD ** -0.5 if blk < 8 else 1.0
                    self.op("dve", lambda e, y=y, n=n, st=st, sc=sc: e.scalar_tensor_tensor(
                        out=st[:, 0:n], in0=y[:, 0:n], scalar=sc, in1=self.rstd[:, 0:n], op0=ALU.mult, op1=ALU.mult),
                        reads=tr + ["rstd"], writes=["stb%d" % si])
                else:
                    self.op("act", lambda e, y=y, n=n, st=st: e.copy(out=st[:, 0:n], in_=y[:, 0:n]), reads=tr,
                            writes=["stb%d" % si])
                self.dma("sp", self.P["B"][blk, :, t0:t0 + n], st[:, 0:n], reads=["stb%d" % si],
                         writes=[(("PB", blk), g) for g in sorted(set([t0 // TT, (t0 + n - 1) // TT]))])
                yield

    def m3_merge(self, l, tg):
        hv = self.hraw[:].bitcast(BF16)
        gv = self.graw[:].bitcast(BF16)
        ys = [hv[:, 0:8 * TG].rearrange("p (h t) -> p h t", h=8), hv[:, 8 * TG:16 * TG].rearrange("p (h t) -> p h t", h=8),
              gv[:, 0:8 * TG].rearrange("p (h t) -> p h t", h=8)]
        ysn = ["hT", "hT", "gT"]
        t_0 = tg * TG
        for i in range(3):
            self.dma("sp", ys[i], self.YS[i, :, :, t_0:t_0 + TG].rearrange("h p t -> p h t"),
                     reads=[((self.kys, i), (t_0 // TT) + j) for j in range(3)], writes=[ysn[i]])
        acc = {}
        for j in range(KC):
            blocks = [(self.w_br[l, i][:, j * 128:(j + 1) * 128], 8, 128) for i in range(3)]

            def evac(blk, tt, ps, pi, j=j):
                i = blk
                t0 = t_0 + tt * TT
                gtt = t0 // TT
                si = self.next_stb()
                gmt = self.stb[si]
                self.dma("sp", gmt[:], self.GM[i * KC + j, :, t0:t0 + TT], reads=[((self.kgm, i * KC + j), gtt)],
                         writes=["stb%d" % si])
                if i == 0:
                    ti = self.next_tmp()
                    acc[tt] = ti
                    self.op("dve", lambda e: e.tensor_tensor(out=self.tmpf[ti][:], in0=ps[:, :], in1=gmt[:], op=ALU.mult),
                            reads=["psb%d" % pi, "stb%d" % si], writes=["tmpf%d" % ti])
                else:
                    ti = acc[tt]
                    a = self.tmpf[ti]
                    xi = self.xr_i
                    self.xr_i = (xi + 1) % len(self.xr)
                    t2 = self.xr[xi]
                    self.op("dve", lambda e: e.tensor_tensor(out=t2[:], in0=ps[:, :], in1=gmt[:], op=ALU.mult),
                            reads=["psb%d" % pi, "stb%d" % si], writes=["xr%d" % xi])
                    if i == 1:
                        self.op("dve", lambda e: e.tensor_tensor(out=a[:], in0=a[:], in1=t2[:], op=ALU.add),
                                reads=["tmpf%d" % ti, "xr%d" % xi], writes=["tmpf%d" % ti])
                    else:
                        s2 = self.next_stb()
                        mo = self.stb[s2]
                        self.op("dve", lambda e: e.tensor_tensor(out=mo[:], in0=a[:], in1=t2[:], op=ALU.add),
                                reads=["tmpf%d" % ti, "xr%d" % xi], writes=["stb%d" % s2])
                        self.dma("sp", self.MG[j, :, t0:t0 + TT], mo[:], reads=["stb%d" % s2], writes=[(("MG", j), gtt)])
            self.linear3(blocks, ys, ysn, evac)
        self.dma("sp", self.hT, self.MG[:, :, t_0:t_0 + TG].rearrange("k p t -> p k t"),
                 reads=[(("MG", j), (t_0 // TT) + q) for j in range(KC) for q in range(3)], writes=["hT"])
        blocks = [(self.w_out[l][:, j * 128:(j + 1) * 128], KC, 128) for j in range(KC)]
        self.linear(blocks, lambda k, tt: self.hT[:, k, tt * TT:(tt + 1) * TT], ["hT"], [TT] * 3,
                    self.resid_evac(tg, self.gsc[(l, 1)], "gsc%d_1" % l))

    def linear3(self, blocks, ys, ysn, evac):
        slots = []
        for (ap, kc, ncols) in blocks:
            s = self.wr_i
            self.wr_i = (s + 1) % self.NWR
            slots.append(s)
            self.dma("pool", self.wr[s][:, 0:kc, 0:ncols], ap.rearrange("(k p) n -> p k n", p=128), writes=["wr%d" % s])
        for tt in range(3):
            for i in range(3):
                s = slots[i]
                pi = self.next_ps()
                ps = self.psb[pi]

                def mm(e, s=s, i=i, tt=tt, ps=ps):
                    for k in range(8):
                        r = e.matmul(ps[:, :], self.wr[s][:, k, :], ys[i][:, k, tt * TT:(tt + 1) * TT],
                                     start=(k == 0), stop=(k == 7))
                    return r
                self.op("pe", mm, reads=["wr%d" % s, ysn[i]], writes=["psb%d" % pi])
                evac(i, tt, ps, pi)
    def ar(self, name, parts, free, dt=F32):
        n = int(np.prod(free))
        nf = n if dt == F32 else (n + 1) // 2
        for raw, pname, key in ((self.hraw, "hT", "h"), (self.graw, "gT", "g")):
            off = self._aro[key]
            if off + nf <= 12288:
                self._aro[key] = off + nf
                v = raw[0:parts, off:off + nf]
                if dt != F32:
                    v = v.bitcast(dt)
                if len(free) == 2:
                    v = v.rearrange("p (a b) -> p a b", a=free[0])
                self.alias(name, pname)
                return v
        raise RuntimeError("arena full " + name)

    def scan_setup(self):
        self._aro = {"h": 0, "g": 0}
        a = self.ar
        T = {}
        for n in ("qT", "kT", "vT", "gt"):
            T[n] = a(n, 128, [8, TT], BF16)
        T["S"] = a("S", 128, [8, 128]); T["Sb"] = a("Sb", 128, [8, 128], BF16)
        T["nS"] = a("nS", 128, [8, 2]); T["nb"] = a("nb", 128, [8, 2], BF16)
        T["o"] = a("o_sb", 64, [8, 128]); T["of"] = a("of_sb", 64, [8, 128])
        T["U0"] = T["of"]
        for n in ("gi", "gf", "gb", "gg"):
            T[n] = a(n, 16, [TT])
        for n in ("ktm", "vtm", "khat", "bv", "bk", "kend", "Ub"):
            T[n] = a(n, 64, [8, 128], BF16)
        for n in ("ATb", "qkTb", "TTb"):
            T[n] = a(n, 64, [8, 64], BF16)
        T["WkT"] = a("WkT", 128, [8, 64], BF16)
        T["ysc"] = a("ysc", 128, [8, 64], BF16)
        for n in ("MT", "t64", "dec", "A", "AT", "T", "TT_", "W", "OkT", "qk"):
            T[n] = a(n, 64, [8, 64])
        T["gtm"] = a("gtm", 64, [32])
        for n in ("sa", "sb_", "sc_", "sd_", "se_"):
            T[n] = a(n, 64, [8])
        T["eL"] = a("eL", 128, [8]); T["g64"] = a("g64", 128, [8]); T["lgb"] = a("lgb", 128, [8])
        T["rsc"] = a("rsc", 64, [8]); T["ksc"] = a("ksc", 64, [8]); T["ss"] = a("ss", 64, [8])
        T["m0b"] = a("m0b", 128, [8])
        self.T = T
        self.hg = [[None] * 3 for _ in range(DEPTH)]

    def bank(self, i, rows, shape, dt=F32):
        v = self.psb[i][0:rows, :]
        if dt != F32:
            v = v.bitcast(dt)
        n = int(np.prod(shape))
        v = v[:, 0:n]
        if len(shape) == 2:
            v = v.rearrange("p (a b) -> p a b", a=shape[0])
        return v

    def bc(self, ap2, n):
        return ap2.unsqueeze(2).broadcast_to([ap2.shape[0], ap2.shape[1], n])

    def bm(self, ap2):
        return ap2.unsqueeze(1).broadcast_to([ap2.shape[0], 8, ap2.shape[1]])

    def load_tt(self, m, gtt, bwd):
        T = self.T
        t0 = gtt * TT
        P = self.P["ABC"[m]]
        for i, n in enumerate(("qT", "kT", "vT")):
            self.dma("sp", T[n], P[8 * i:8 * i + 8, :, t0:t0 + TT].rearrange("h p t -> p h t"),
                     reads=[(("P" + "ABC"[m], 8 * i + h), gtt) for h in range(8)], writes=[n])
        if bwd:
            self.dma("sp", T["gt"], self.GO[m, :, :, t0:t0 + TT].rearrange("h p t -> p h t"),
                     reads=[(("GO", m, h), gtt) for h in range(8)], writes=["gt"])
        if m == 0:
            self.dma("sp", T["gi"], self.GA[0, :, t0:t0 + TT], reads=[(("GA", 0), gtt)], writes=["gi"])
            self.dma("sp", T["gf"], self.GA[1, :, t0:t0 + TT], reads=[(("GA", 1), gtt)], writes=["gf"])
        if m == 1:
            self.dma("sp", T["gb"], self.GA[2, :, t0:t0 + TT], reads=[(("GA", 2), gtt)], writes=["gb"])
            self.dma("sp", T["gg"], self.GA[3, :, t0:t0 + TT], reads=[(("GA", 3), gtt)], writes=["gg"])

    def mm8(self, banks, rows, width, fn, reads):
        T = self.T
        per = 8 // len(banks)
        for bi, b in enumerate(banks):
            v = self.bank(b, rows, [per, width])

            def f(e, bi=bi, v=v):
                r = None
                for hh in range(per):
                    r = fn(e, bi * per + hh, v[:, hh, :])
                return r
            self.op("pe", f, reads=reads, writes=["psb%d" % b])

    def evac8(self, eng, banks, rows, width, fn, reads, writes):
        per = 8 // len(banks)
        for bi, b in enumerate(banks):
            v = self.bank(b, rows, [per, width])
            hs = slice(bi * per, (bi + 1) * per)
            self.op(eng, lambda e, v=v, hs=hs: fn(e, v, hs), reads=reads + ["psb%d" % b], writes=writes)

    def kv_tm(self, c0):
        T = self.T
        for src, bk_, dst in (("kT", 2, "ktm"), ("vT", 3, "vtm")):
            v = self.bank(bk_, 64, [8, 128], BF16)

            def f(e, src=src, v=v):
                for h in range(8):
                    r = e.transpose(out=v[:, h, :], in_=T[src][:, h, c0:c0 + L], identity=self.ident_bf[:])
                return r
            self.op("pe", f, reads=[src, "ident_bf"], writes=["psb%d" % bk_])
        v3 = self.bank(3, 64, [8, 128], BF16)
        self.op("act", lambda e: e.copy(out=T["vtm"], in_=v3), reads=["psb3"], writes=["vtm"])

    def scale_k(self, dst, sc_name, sc_ap):
        T = self.T
        v2 = self.bank(2, 64, [8, 128], BF16)
        self.op("dve", lambda e: e.tensor_tensor(out=T[dst], in0=v2, in1=self.bc(sc_ap, 128), op=ALU.mult),
                reads=["psb2", sc_name], writes=[dst])

    def gate_tm(self, rows_a, rows_b, c0):
        T = self.T
        v = self.psb[1]

        def f(e):
            e.transpose(out=v[0:64, 0:16], in_=T[rows_a][:, c0:c0 + L], identity=self.ident[0:16, 0:16])
            return e.transpose(out=v[0:64, 16:32], in_=T[rows_b][:, c0:c0 + L], identity=self.ident[0:16, 0:16])
        self.op("pe", f, reads=[rows_a, rows_b, "ident"], writes=["psb1"])
        self.op("dve", lambda e: e.tensor_copy(out=T["gtm"], in_=v[0:64, 0:32]), reads=["psb1"], writes=["gtm"])

    def state_update(self, banks_src_fn, dec_name, dec_ap, with_n=False):
        T = self.T
        self.op("dve", lambda e: e.tensor_tensor(out=T["S"], in0=T["S"], in1=self.bc(dec_ap, 128), op=ALU.mult),
                reads=["S", dec_name], writes=["S"])
        self.evac8("dve", [6, 7], 128, 128, lambda e, v, hs: e.tensor_tensor(out=T["S"][:, hs, :], in0=T["S"][:, hs, :],
                                                                               in1=v, op=ALU.add), ["S"], ["S"])
        self.op("act", lambda e: e.copy(out=T["Sb"], in_=T["S"]), reads=["S"], writes=["Sb"])

    def finalize(self, l, m, c0, t0):
        T = self.T
        gtt = t0 // TT
        self.dma("sp", T["of"], self.OFs[t0:t0 + L, :].rearrange("p (h e) -> p h e", h=8), reads=[("OF", t0)],
                 writes=["of_sb"])
        self.op("dve", lambda e: e.tensor_tensor(out=T["o"], in0=T["o"], in1=T["of"], op=ALU.add),
                reads=["o_sb", "of_sb"], writes=["o_sb"])
        self.op("dve", lambda e: e.tensor_tensor(out=T["of"], in0=T["o"], in1=T["o"], op=ALU.mult), reads=["o_sb"],
                writes=["of_sb"])
        self.op("dve", lambda e: e.tensor_reduce(out=T["ss"], in_=T["of"], op=ALU.add, axis=AX.X), reads=["of_sb"],
                writes=["ss"])
        self.op("act", _act(AF.Sqrt, T["ss"], T["ss"], scale=1.0 / HD, bias=self.eps_t[0:64, :]), reads=["ss", "eps_t"],
                writes=["ss"])
        self.op("dve", lambda e: e.reciprocal(out=T["ss"], in_=T["ss"]), reads=["ss"], writes=["ss"])
        self.op("dve", lambda e: e.tensor_tensor(out=T["Ub"], in0=T["o"], in1=self.bc(T["ss"], 128), op=ALU.mult),
                reads=["o_sb", "ss"], writes=["Ub"])
        v = self.bank(2, 128, [8, 64], BF16)

        def f(e):
            for h in range(8):
                r = e.transpose(out=v[:, h, :], in_=T["Ub"][:, h, :], identity=self.ident_bf[0:64, 0:64])
            return r
        self.op("pe", f, reads=["Ub", "ident_bf"], writes=["psb2"])
        hg = self.hg[l][m]
        self.op("dve", lambda e: e.scalar_tensor_tensor(out=T["ysc"], in0=v, scalar=hg[:, 0:1],
                                                        in1=T["gt"][:, :, c0:c0 + L], op0=ALU.mult, op1=ALU.mult),
                reads=["psb2", "gt", "hg%d_%d" % (l, m)], writes=["ysc"])
        self.dma("sp", self.YS[m, :, :, t0:t0 + L].rearrange("h p t -> p h t"), T["ysc"], reads=["ysc"],
                 writes=[(("YS", m), gtt)], allow_slow_non_contiguous=True)

    def out_chunk(self, l, m, d, c0, t0):
        T = self.T
        if d == 0:
            self.dma("sp", self.OFs[t0:t0 + L, :].rearrange("p (h e) -> p h e", h=8), T["o"], reads=["o_sb"],
                     writes=[("OF", t0)])
        else:
            self.finalize(l, m, c0, t0)

    def scan(self, l, m):
        T = self.T
        if self.hg[l][m] is None:
            self.hg[l][m] = self.load_vecT("hg%d_%d" % (l, m), self.hng[l, m], 1)
        for d in (0, 1):
            self.dir_setup(l, m, d)
            cur_tt = None
            for si, (s0, sl) in enumerate(SEGS):
                self.state_init(l, m, d, si)
                chunks = list(range(s0, s0 + sl, L))
                if d == 1:
                    chunks = chunks[::-1]
                for t0 in chunks:
                    gtt = t0 // TT
                    if gtt != cur_tt:
                        self.load_tt(m, gtt, d == 1)
                        cur_tt = gtt
                    c0 = t0 - gtt * TT
                    (self.step_mlstm, self.step_delta, self.step_ret)[m](l, d, c0, t0)
                if si < 2:
                    self.state_out(l, m, d, si)

    def dir_setup(self, l, m, d):
        T = self.T
        M = self.C("LE" if d == 0 else "GE")
        self._M = M
        self._Tri = M
        if m == 2:
            self.dma("sp", T["lgb"], self.lgam[l:l + 1, 8 * d:8 * d + 8].partition_broadcast(128), writes=["lgb"])
            p1 = self.C("p1f" if d == 0 else "p1b")
            p2 = self.C("p2f" if d == 0 else "p2b")
            self.op("act", _act(AF.Exp, T["rsc"], T["lgb"][0:64, :], scale=p1), reads=["lgb", "cst"], writes=["rsc"])
            self.op("act", _act(AF.Exp, T["ksc"], T["lgb"][0:64, :], scale=p2), reads=["lgb", "cst"], writes=["ksc"])
            self.op("act", _act(AF.Exp, T["g64"], T["lgb"], scale=64.0), reads=["lgb"], writes=["g64"])
            self.op("dve", lambda e: e.reciprocal(out=T["sa"], in_=T["rsc"]), reads=["rsc"], writes=["sa"])
            self.op("dve", lambda e: e.tensor_tensor(out=T["MT"], in0=self.bc(T["sa"], 64), in1=self.bm(M), op=ALU.mult),
                    reads=["sa", "cst"], writes=["MT"])

    def state_init(self, l, m, d, si):
        T = self.T
        if si < 2:
            self.op("dve", lambda e: e.memset(T["S"], 0.0), writes=["S"])
            self.op("dve", lambda e: e.memset(T["nS"], 0.0), writes=["nS"])
        else:
            src = (self.sC, self.sD, self.sR)[m]
            self.dma("sp", T["S"], src[l, d].rearrange("h k e -> k h e"), writes=["S"])
            if m == 0:
                self.dma("sp", T["nS"][:, :, 0], self.sn[l, d].rearrange("h k -> k h"), writes=["nS"],
                         allow_slow_non_contiguous=True)
                self.dma("sp", T["nS"][:, :, 1], self.sn[l, d].rearrange("h k -> k h"), writes=["nS"],
                         allow_slow_non_contiguous=True)
                self.dma("sp", T["m0b"], self.sm[l:l + 1, 8 * d:8 * d + 8].partition_broadcast(128), writes=["m0b"])
                self.op("act", _act(AF.Exp, T["m0b"], T["m0b"]), reads=["m0b"], writes=["m0b"])
                self.op("dve", lambda e: e.tensor_tensor(out=T["S"], in0=T["S"], in1=self.bc(T["m0b"], 128), op=ALU.mult),
                        reads=["S", "m0b"], writes=["S"])
                self.op("dve", lambda e: e.tensor_tensor(out=T["nS"], in0=T["nS"], in1=self.bc(T["m0b"], 2), op=ALU.mult),
                        reads=["nS", "m0b"], writes=["nS"])
        self.op("act", lambda e: e.copy(out=T["Sb"], in_=T["S"]), reads=["S"], writes=["Sb"])
        self.op("act", lambda e: e.copy(out=T["nb"], in_=T["nS"]), reads=["nS"], writes=["nb"])

    def state_out(self, l, m, d, si):
        T = self.T
        if m == 0:
            self.mlstm_state_out(l, d, si)
            return
        dst = (None, self.oD, self.oR)[m]
        self.dma("sp", dst[si, l, d].rearrange("h k e -> k h e"), T["S"], reads=["S"], writes=[("ost", m, si, l, d)])

    def step_ret(self, l, d, c0, t0):
        T = self.T
        self.mm8([0], 64, 64, lambda e, h, o: e.matmul(o, T["kT"][:, h, c0:c0 + L], T["qT"][:, h, c0:c0 + L],
                                                       start=True, stop=True), ["kT", "qT"])
        self.evac8("dve", [0], 64, 64, lambda e, v, hs: e.tensor_tensor(out=T["ATb"], in0=v, in1=T["MT"], op=ALU.mult),
                   ["MT"], ["ATb"])
        self.kv_tm(c0)
        self.scale_k("khat", "ksc", T["ksc"])

        def o_mm(e, h, o):
            e.matmul(o, T["qT"][:, h, c0:c0 + L], T["Sb"][:, h, :], start=True, stop=False)
            return e.matmul(o, T["ATb"][:, h, :], T["vtm"][:, h, :], start=False, stop=True)
        self.mm8([4, 5], 64, 128, o_mm, ["qT", "Sb", "ATb", "vtm"])
        self.evac8("dve", [4, 5], 64, 128, lambda e, v, hs: e.tensor_tensor(
            out=T["o"][:, hs, :], in0=v, in1=self.bc(T["rsc"][:, hs], 128), op=ALU.mult), ["rsc"], ["o_sb"])
        self.out_chunk(l, 2, d, c0, t0)
        self.mm8([6, 7], 128, 128, lambda e, h, o: e.matmul(o, T["khat"][:, h, :], T["vtm"][:, h, :], start=True,
                                                           stop=True), ["khat", "vtm"])
        self.state_update(None, "g64", T["g64"])

    def step_mlstm(self, l, d, c0, t0):
        T = self.T
        g = T["gtm"]
        self.gate_tm("gi", "gf", c0)
        ig, lf = g[:, 8 * d:8 * d + 8], g[:, 16 + 8 * d:16 + 8 * d + 8]
        v1 = self.psb[1]
        Tri = self._Tri
        Msk = self._M

        def f(e):
            e.matmul(v1[0:64, 32:40], Tri, lf, start=True, stop=True)
            return e.matmul(v1[0:128, 40:48], self.ones_f[0:64, 0:128], lf, start=True, stop=True)
        self.op("pe", f, reads=["gtm", "cst", "ones_f"], writes=["psb1"])
        self.op("dve", lambda e: e.tensor_tensor(out=T["sa"], in0=ig, in1=v1[0:64, 32:40], op=ALU.subtract),
                reads=["gtm", "psb1"], writes=["sa"])
        self.op("act", _act(AF.Exp, T["sa"], T["sa"]), reads=["sa"], writes=["sa"])
        self.op("act", _act(AF.Exp, T["sb_"], v1[0:64, 32:40], scale=-1.0), reads=["psb1"], writes=["sb_"])
        self.op("act", _act(AF.Exp, T["eL"], v1[0:128, 40:48]), reads=["psb1"], writes=["eL"])
        self.op("dve", lambda e: e.tensor_tensor(out=T["sc_"], in0=T["sa"], in1=T["eL"][0:64, :], op=ALU.mult),
                reads=["sa", "eL"], writes=["sc_"])
        self.mm8([0], 64, 64, lambda e, h, o: e.matmul(o, T["kT"][:, h, c0:c0 + L], T["qT"][:, h, c0:c0 + L],
                                                       start=True, stop=True), ["kT", "qT"])
        self.evac8("dve", [0], 64, 64, lambda e, v, hs: e.tensor_tensor(out=T["t64"], in0=v, in1=self.bc(T["sa"], 64),
                                                                        op=ALU.mult), ["sa"], ["t64"])
        self.op("dve", lambda e: e.tensor_tensor(out=T["ATb"], in0=T["t64"], in1=self.bm(Msk), op=ALU.mult),
                reads=["t64", "cst"], writes=["ATb"])
        self.kv_tm(c0)
        self.scale_k("khat", "sc_", T["sc_"])

        def o_mm(e, h, o):
            e.matmul(o, T["qT"][:, h, c0:c0 + L], T["Sb"][:, h, :], start=True, stop=False)
            return e.matmul(o, T["ATb"][:, h, :], T["vtm"][:, h, :], start=False, stop=True)
        self.mm8([4, 5], 64, 128, o_mm, ["qT", "Sb", "ATb", "vtm"])
        den = v1[0:64, 64:80].rearrange("p (h t) -> p h t", h=8)

        def d_mm(e):
            for h in range(8):
                e.matmul(den[:, h, :], T["qT"][:, h, c0:c0 + L], T["nb"][:, h, :], start=True, stop=False)
                r = e.matmul(den[:, h, :], T["ATb"][:, h, :], self.ones_bf[0:64, 0:2], start=False, stop=True)
            return r
        self.op("pe", d_mm, reads=["qT", "nb", "ATb", "ones_bf"], writes=["psb1"])
        self.op("dve", lambda e: e.tensor_scalar(out=T["sd_"], in0=den[:, :, 0], scalar1=-1.0, scalar2=None, op0=ALU.mult),
                reads=["psb1"], writes=["sd_"])
        self.op("dve", lambda e: e.tensor_tensor(out=T["sd_"], in0=T["sd_"], in1=den[:, :, 0], op=ALU.max),
                reads=["psb1", "sd_"], writes=["sd_"])
        self.op("dve", lambda e: e.tensor_tensor(out=T["sd_"], in0=T["sd_"], in1=T["sb_"], op=ALU.max),
                reads=["sd_", "sb_"], writes=["sd_"])
        self.op("dve", lambda e: e.reciprocal(out=T["sd_"], in_=T["sd_"]), reads=["sd_"], writes=["sd_"])
        self.evac8("dve", [4, 5], 64, 128, lambda e, v, hs: e.tensor_tensor(
            out=T["o"][:, hs, :], in0=v, in1=self.bc(T["sd_"][:, hs], 128), op=ALU.mult), ["sd_"], ["o_sb"])
        self.out_chunk(l, 0, d, c0, t0)
        self.mm8([6, 7], 128, 128, lambda e, h, o: e.matmul(o, T["khat"][:, h, :], T["vtm"][:, h, :], start=True,
                                                           stop=True), ["khat", "vtm"])
        dn = v1[0:128, 96:112].rearrange("p (h t) -> p h t", h=8)

        def n_mm(e):
            for h in range(8):
                r = e.matmul(dn[:, h, :], T["khat"][:, h, :], self.ones_bf[0:64, 0:2], start=True, stop=True)
            return r
        self.op("pe", n_mm, reads=["khat", "ones_bf"], writes=["psb1"])
        self.op("dve", lambda e: e.tensor_tensor(out=T["nS"], in0=T["nS"], in1=self.bc(T["eL"], 2), op=ALU.mult),
                reads=["nS", "eL"], writes=["nS"])
        self.op("dve", lambda e: e.tensor_tensor(out=T["nS"], in0=T["nS"], in1=dn, op=ALU.add), reads=["nS", "psb1"],
                writes=["nS"])
        self.op("act", lambda e: e.copy(out=T["nb"], in_=T["nS"]), reads=["nS"], writes=["nb"])
        self.state_update(None, "eL", T["eL"])

    def mlstm_state_out(self, l, d, si):
        T = self.T
        s0 = SEGS[si][0]
        gi, gf = T["gi"], T["gf"]
        n = 256
        P_ = self.tmpf[0][0:16, 0:n]
        E_ = self.tmpf[1][0:16, 0:n]
        tot = T["se_"][0:16, 0:1]
        mx = T["se_"][0:16, 1:2]
        rd = ["gi", "gf"]
        lfv, igv = gf[:, s0:s0 + n], gi[:, s0:s0 + n]
        self.op("dve", lambda e: e.tensor_tensor_scan(out=P_, data0=self.ones_f[0:16, 0:n], data1=lfv, initial=0.0,
                                                      op0=ALU.mult, op1=ALU.add), reads=rd + ["ones_f"], writes=["tmpf0"])
        self.op("dve", lambda e: e.tensor_copy(out=tot, in_=P_[:, n - 1:n]), reads=["tmpf0"], writes=["se_"])
        if d == 0:
            self.op("dve", lambda e: e.tensor_tensor(out=E_, in0=igv, in1=P_, op=ALU.subtract), reads=rd + ["tmpf0"],
                    writes=["tmpf1"])
            self.op("dve", lambda e: e.tensor_scalar(out=E_, in0=E_, scalar1=tot, scalar2=None, op0=ALU.add),
                    reads=["tmpf1", "se_"], writes=["tmpf1"])
        else:
            self.op("dve", lambda e: e.tensor_tensor(out=E_, in0=igv, in1=P_, op=ALU.add), reads=rd + ["tmpf0"],
                    writes=["tmpf1"])
            self.op("dve", lambda e: e.tensor_tensor(out=E_, in0=E_, in1=lfv, op=ALU.subtract), reads=rd + ["tmpf1"],
                    writes=["tmpf1"])
        self.op("dve", lambda e: e.tensor_reduce(out=mx, in_=E_, op=ALU.max, axis=AX.X), reads=["tmpf1"], writes=["se_"])
        self.op("dve", lambda e: e.tensor_tensor(out=mx, in0=mx, in1=tot, op=ALU.max), reads=["se_"], writes=["se_"])
        self.dma("sp", self.om[si, l, :].rearrange("(p o) -> p o", o=1)[8 * d:8 * d + 8, :], T["se_"][8 * d:8 * d + 8, 1:2],
                 reads=["se_"], writes=[("om", si, l, d)], allow_slow_non_contiguous=True)
        mrep = self.tmpf[2][0:16, 0:128]
        self.op("dve", lambda e: e.tensor_scalar(out=mrep, in0=self.ones_f[0:16, 0:128], scalar1=mx, scalar2=None, op0=ALU.mult),
                reads=["se_", "ones_f"], writes=["tmpf2"])
        v1 = self.psb[1]
        self.op("pe", lambda e: e.matmul(v1[0:128, 0:16], mrep, self.ident[0:16, 0:16], start=True, stop=True),
                reads=["tmpf2", "ident"], writes=["psb1"])
        self.op("act", _act(AF.Exp, T["m0b"], v1[0:128, 8 * d:8 * d + 8], scale=-1.0), reads=["psb1"], writes=["m0b"])
        self.op("dve", lambda e: e.tensor_tensor(out=T["S"], in0=T["S"], in1=self.bc(T["m0b"], 128), op=ALU.mult),
                reads=["S", "m0b"], writes=["S"])
        self.op("dve", lambda e: e.tensor_tensor(out=T["nS"], in0=T["nS"], in1=self.bc(T["m0b"], 2), op=ALU.mult),
                reads=["nS", "m0b"], writes=["nS"])
        self.dma("sp", self.oC[si, l, d].rearrange("h k e -> k h e"), T["S"], reads=["S"], writes=[("oC", si, l, d)])
        self.dma("sp", self.on[si, l, d].rearrange("h k -> k h"), T["nS"][:, :, 0], reads=["nS"], writes=[("on", si, l, d)],
                 allow_slow_non_contiguous=True)

    def step_delta(self, l, d, c0, t0):
        T = self.T
        g = T["gtm"]
        self.gate_tm("gb", "gg", c0)
        be, gg = g[:, 8 * d:8 * d + 8], g[:, 16 + 8 * d:16 + 8 * d + 8]
        v1 = self.psb[1]
        Tri = self._Tri
        INC = self.C("GE" if d == 0 else "LE")
        STR = self.C("GT" if d == 0 else "LT")
        self.op("dve", lambda e: e.tensor_tensor(out=T["t64"], in0=self.bc(gg, 64), in1=self.bm(STR), op=ALU.mult),
                reads=["gtm", "cst"], writes=["t64"])

        def f(e):
            e.matmul(v1[0:64, 32:40], Tri, gg, start=True, stop=True)
            return e.matmul(v1[0:128, 40:48], self.ones_f[0:64, 0:128], gg, start=True, stop=True)
        self.op("pe", f, reads=["gtm", "cst", "ones_f"], writes=["psb1"])
        self.op("dve", lambda e: e.tensor_copy(out=T["sa"], in_=v1[0:64, 32:40]), reads=["psb1"], writes=["sa"])
        self.op("act", _act(AF.Exp, T["sb_"], T["sa"]), reads=["sa"], writes=["sb_"])
        self.op("dve", lambda e: e.tensor_tensor(out=T["sc_"], in0=v1[0:64, 40:48], in1=T["sa"], op=ALU.subtract),
                reads=["psb1", "sa"], writes=["sc_"])
        self.op("act", _act(AF.Exp, T["sc_"], T["sc_"]), reads=["sc_"], writes=["sc_"])
        self.op("act", _act(AF.Exp, T["eL"], v1[0:128, 40:48]), reads=["psb1"], writes=["eL"])
        self.op("dve", lambda e: e.tensor_tensor(out=T["sd_"], in0=be, in1=T["sb_"], op=ALU.mult), reads=["gtm", "sb_"],
                writes=["sd_"])
        b0 = self.bank(0, 64, [8, 64])
        self.op("pe", lambda e: e.matmul(self.psb[0][0:64, :], Tri, T["t64"].rearrange("p h m -> p (h m)"), start=True,
                                         stop=True), reads=["t64", "cst"], writes=["psb0"])
        self.op("act", _act(AF.Exp, T["dec"], b0), reads=["psb0"], writes=["dec"])
        self.op("dve", lambda e: e.tensor_tensor(out=T["t64"], in0=T["dec"], in1=self.bm(STR), op=ALU.mult),
                reads=["dec", "cst"], writes=["t64"])
        self.op("dve", lambda e: e.tensor_tensor(out=T["dec"], in0=T["dec"], in1=self.bm(INC), op=ALU.mult),
                reads=["dec", "cst"], writes=["dec"])
        self.op("dve", lambda e: e.tensor_tensor(out=T["t64"], in0=T["t64"], in1=self.bc(be, 64), op=ALU.mult),
                reads=["t64", "gtm"], writes=["t64"])
        self.mm8([0], 64, 64, lambda e, h, o: e.matmul(o, T["kT"][:, h, c0:c0 + L], T["kT"][:, h, c0:c0 + L],
                                                       start=True, stop=True), ["kT"])
        self.evac8("dve", [0], 64, 64, lambda e, v, hs: e.tensor_tensor(out=T["A"], in0=v, in1=T["t64"], op=ALU.mult),
                   ["t64"], ["A"])
        self.mm8([1], 64, 64, lambda e, h, o: e.matmul(o, T["qT"][:, h, c0:c0 + L], T["kT"][:, h, c0:c0 + L],
                                                       start=True, stop=True), ["qT", "kT"])
        self.evac8("dve", [1], 64, 64, lambda e, v, hs: e.tensor_tensor(out=T["qk"], in0=v, in1=T["dec"], op=ALU.mult),
                   ["dec"], ["qk"])
        i64 = self.ident[0:64, 0:64]
        for src, bk_, dst, eng in (("A", 0, "AT", "dve"), ("qk", 1, "qkTb", "act")):
            vb = self.bank(bk_, 64, [8, 64])

            def tr(e, src=src, vb=vb):
                for h in range(8):
                    r = e.transpose(out=vb[:, h, :], in_=T[src][:, h, :], identity=i64)
                return r
            self.op("pe", tr, reads=[src, "ident"], writes=["psb%d" % bk_])
            if eng == "dve":
                self.op("dve", lambda e, vb=vb, dst=dst: e.tensor_copy(out=T[dst], in_=vb), reads=["psb%d" % bk_],
                        writes=[dst])
            else:
                self.op("act", lambda e, vb=vb, dst=dst: e.copy(out=T[dst], in_=vb), reads=["psb%d" % bk_], writes=[dst])
        I8 = self.bm(i64)
        self.op("dve", lambda e: e.tensor_tensor(out=T["W"], in0=T["A"], in1=self.bm(self.C("BM0")), op=ALU.mult),
                reads=["A", "cst"], writes=["W"])
        self.op("dve", lambda e: e.tensor_tensor(out=T["T"], in0=I8, in1=T["W"], op=ALU.subtract), reads=["W", "ident"],
                writes=["T"])
        self.op("dve", lambda e: e.tensor_tensor(out=T["W"], in0=T["AT"], in1=self.bm(self.C("BM0")), op=ALU.mult),
                reads=["AT", "cst"], writes=["W"])
        self.op("dve", lambda e: e.tensor_tensor(out=T["TT_"], in0=I8, in1=T["W"], op=ALU.subtract), reads=["W", "ident"],
                writes=["TT_"])
        okn = ["OkT", "dec"]

        def mask_level(k):
            nm = okn[k % 2]
            self.op("dve", lambda e, k=k, nm=nm: e.tensor_tensor(out=T[nm], in0=T["AT"], in1=self.bm(self.C("BM%d" % k)),
                                                                 op=ALU.mult), reads=["AT", "cst"], writes=[nm])
        mask_level(1)
        for k in range(1, 6):
            ok = okn[k % 2]
            self.mm8([0], 64, 64, lambda e, h, o, ok=ok: e.matmul(o, T[ok][:, h, :], T["T"][:, h, :], start=True,
                                                                  stop=True), [ok, "T"])
            if k < 5:
                mask_level(k + 1)
            self.evac8("act", [0], 64, 64, lambda e, v, hs: e.copy(out=T["W"], in_=v), [], ["W"])
            if k < 5:
                self.mm8([1], 64, 64, lambda e, h, o: e.matmul(o, T["TT_"][:, h, :], T["W"][:, h, :], start=True,
                                                               stop=True), ["TT_", "W"])
            self.mm8([3], 64, 64, lambda e, h, o: e.matmul(o, T["W"][:, h, :], T["TT_"][:, h, :], start=True, stop=True),
                     ["TT_", "W"])
            if k < 5:
                self.evac8("dve", [1], 64, 64, lambda e, v, hs: e.tensor_tensor(out=T["T"], in0=T["T"], in1=v,
                                                                                op=ALU.subtract), ["T"], ["T"])
            self.evac8("dve", [3], 64, 64, lambda e, v, hs: e.tensor_tensor(out=T["TT_"], in0=T["TT_"], in1=v,
                                                                            op=ALU.subtract), ["TT_"], ["TT_"])
        self.op("act", lambda e: e.copy(out=T["TTb"], in_=T["TT_"]), reads=["TT_"], writes=["TTb"])
        self.kv_tm(c0)
        self.scale_k("bk", "sd_", T["sd_"])
        self.scale_k("kend", "sc_", T["sc_"])
        self.op("dve", lambda e: e.tensor_tensor(out=T["bv"], in0=T["vtm"], in1=self.bc(be, 128), op=ALU.mult),
                reads=["vtm", "gtm"], writes=["bv"])
        self.mm8([4, 5], 64, 128, lambda e, h, o: e.matmul(o, T["TTb"][:, h, :], T["bv"][:, h, :], start=True, stop=True),
                 ["TTb", "bv"])
        self.evac8("act", [4, 5], 64, 128, lambda e, v, hs: e.copy(out=T["U0"][:, hs, :], in_=v), [], ["of_sb"])
        self.mm8([2], 128, 64, lambda e, h, o: e.matmul(o, T["bk"][:, h, :], T["TTb"][:, h, :], start=True, stop=True),
                 ["TTb", "bk"])
        self.evac8("act", [2], 128, 64, lambda e, v, hs: e.copy(out=T["WkT"], in_=v), [], ["WkT"])
        self.mm8([6, 7], 64, 128, lambda e, h, o: e.matmul(o, T["WkT"][:, h, :], T["Sb"][:, h, :], start=True, stop=True),
                 ["WkT", "Sb"])
        self.evac8("dve", [6, 7], 64, 128, lambda e, v, hs: e.tensor_tensor(out=T["Ub"][:, hs, :], in0=T["U0"][:, hs, :],
                                                                             in1=v, op=ALU.subtract), ["of_sb"], ["Ub"])
        self.mm8([4, 5], 64, 128, lambda e, h, o: e.matmul(o, T["qT"][:, h, c0:c0 + L], T["Sb"][:, h, :], start=True,
                                                          stop=True), ["qT", "Sb"])
        self.evac8("dve", [4, 5], 64, 128, lambda e, v, hs: e.tensor_tensor(
            out=T["o"][:, hs, :], in0=v, in1=self.bc(T["sb_"][:, hs], 128), op=ALU.mult), ["sb_"], ["o_sb"])
        self.mm8([6, 7], 64, 128, lambda e, h, o: e.matmul(o, T["qkTb"][:, h, :], T["Ub"][:, h, :], start=True, stop=True),
                 ["qkTb", "Ub"])
        self.evac8("dve", [6, 7], 64, 128, lambda e, v, hs: e.tensor_tensor(out=T["o"][:, hs, :], in0=T["o"][:, hs, :],
                                                                             in1=v, op=ALU.add), ["o_sb"], ["o_sb"])
        self.mm8([6, 7], 128, 128, lambda e, h, o: e.matmul(o, T["kend"][:, h, :], T["Ub"][:, h, :], start=True, stop=True),
                 ["kend", "Ub"])
        self.state_update(None, "eL", T["eL"])
        self.out_chunk(l, 1, d, c0, t0)

    def select_own(self):
        rm = self.sb("rmask_sb", [128, 4], F32)
        self.dma("sp", rm[:], self.rmask_in, writes=["rmask"])
        xO = self.dram("xO", [KC, 128, TG])
        YSO = self.dram("YSO", [3, 8, 128, TG], BF16)
        GMO = self.dram("GMO", [48, 128, TG], BF16)
        hf = self.hraw[:]
        self.op("dve", lambda e: e.memset(hf[:, 0:2], 0.0), writes=["hT"])
        ld = [hf[:, i * 1024:(i + 1) * 1024] for i in range(4)]
        ac = [hf[:, (4 + i) * 1024:(5 + i) * 1024] for i in range(2)]
        for i in range(4):
            self.alias("selL%d" % i, "hT")
        for i in range(2):
            self.alias("selA%d" % i, "hT")
        cnt = [0, 0]

        def sel(src, dst, dt, rkeys, wkeys):
            n = 1024
            def view(t, w):
                return t[:, 0:w] if dt == F32 else t[:, 0:w // 2].bitcast(BF16)
            li = cnt[0] % 4
            cnt[0] += 1
            self.dma("sp", view(ld[li], 512), src[:, 0:512], reads=[rkeys(0)], writes=["selL%d" % li])
            self.dma("sp", dst[:, 0:512], view(ld[li], 512), reads=["selL%d" % li], writes=[wkeys(0)])
            ai = cnt[1] % 2
            cnt[1] += 1
            a = view(ac[ai], n)
            for q in range(4):
                li = cnt[0] % 4
                cnt[0] += 1
                t = view(ld[li], n)
                c0 = 512 + q * n
                self.dma("sp", t, src[:, c0:c0 + n], reads=[rkeys(c0 // TT), rkeys(c0 // TT + 1)],
                         writes=["selL%d" % li])
                if q == 0:
                    self.op("dve", lambda e, t=t, a=a: e.tensor_scalar(out=a, in0=t, scalar1=rm[:, 0:1], scalar2=None,
                                                                       op0=ALU.mult),
                            reads=["selL%d" % li, "rmask"], writes=["selA%d" % ai])
                else:
                    self.op("dve", lambda e, t=t, a=a, q=q: e.scalar_tensor_tensor(
                        out=a, in0=t, scalar=rm[:, q:q + 1], in1=a, op0=ALU.mult, op1=ALU.add),
                        reads=["selL%d" % li, "selA%d" % ai, "rmask"], writes=["selA%d" % ai])
            self.dma("sp", dst[:, 512:512 + n], a, reads=["selA%d" % ai], writes=[wkeys(1), wkeys(2)])
        for k in range(KC):
            sel(self.xT[k], xO[k], F32, lambda g: ("xT", g), lambda g: ("xO", g))
        for m in range(3):
            for h in range(8):
                sel(self.YS[m, h], YSO[m, h], BF16, lambda g, m=m: (("YS", m), g), lambda g, m=m: (("YSO", m), g))
        self.xT, self.YS, self.GM = xO, YSO, GMO
        self.kx, self.kys, self.kgm = "xO", "YSO", "GMO"

    def mixer(self, l, do_m3=True):
        if l == 0:
            self.mixer_setup()
        self.mixer_params(l)
        self.conv_prep(l)
        idxs = None
        if not do_m3:
            idxs = [i for i, b in enumerate(self._B) if b[3][0] != "GM"]
        for tg in range(NTG):
            self.norm_to_hT(tg, l, 1)
            self._conv_args = (l, tg)
            self.m1_project(l, tg, idxs)
        self.bg_drain()
        self.scan_setup()
        for m in range(3):
            self.scan(l, m)
        if do_m3:
            for tg in range(NTG):
                self.m3_merge(l, tg)


_CACHE = {}


def _get_nc(cfg_key):
    if cfg_key not in _CACHE:
        k = Kern(dict(cfg_key))
        _CACHE[cfg_key] = k.build()
    return _CACHE[cfg_key]


def kernel(x_prompt, x_sample, state_mlstm_C, state_mlstm_n, state_mlstm_m, state_delta_S, state_ret_S,
           c, c_ctx, norm_g, final_norm_g, w_ada, b_ada, w_in, b_in, mlstm_f_bias, conv_w, delta_A_log,
           delta_dt_bias, ret_log_gamma, head_norm_g, w_br, w_out, ffn_w13, ffn_w2, _cfg=None):
    cfg = dict(_cfg or {})
    nc = _get_nc(tuple(sorted(cfg.items())))
    f = lambda a: np.ascontiguousarray(np.asarray(a, dtype=np.float32))
    in_maps = []
    ident = np.eye(128, dtype=np.float32)
    full = cfg.get("mixer", True)
    rc, rs = _rope_np()
    for cidx in range(8):
        b = cidx // 4
        x_tok = np.concatenate([x_prompt[2 * cidx], x_prompt[2 * cidx + 1], x_sample[b]], axis=0)
        cvec = np.stack([c_ctx, c[b]], axis=0)
        m = {
            "x_tok": f(x_tok), "cvec": f(cvec), "norm_g": f(norm_g), "final_norm_g": f(final_norm_g),
            "w_ada": f(w_ada), "b_ada": f(b_ada), "w_in": f(w_in), "b_in": f(b_in), "w_br": f(w_br),
            "w_out": f(w_out), "ffn_w13": f(ffn_w13), "ffn_w2": f(ffn_w2), "ident": ident,
        }
        rmk = np.zeros((128, 4), np.float32)
        rmk[:, cidx % 4] = 1.0
        m["rmask"] = rmk
        if full:
            m.update({
                "cst": _CST_NP, "ropeC": rc, "ropeS": rs,
                "mlstm_f_bias": f(mlstm_f_bias).reshape(DEPTH, 16), "conv_w": f(conv_w),
                "delta_A_log": f(delta_A_log).reshape(DEPTH, 16), "delta_dt_bias": f(delta_dt_bias).reshape(DEPTH, 16),
                "ret_log_gamma": f(ret_log_gamma).reshape(DEPTH, 16), "head_norm_g": f(head_norm_g),
                "st_C": f(state_mlstm_C[b]), "st_n": f(state_mlstm_n[b]), "st_m": f(state_mlstm_m[b]).reshape(DEPTH, 16),
                "st_D": f(state_delta_S[b]), "st_R": f(state_ret_S[b]),
            })
        in_maps.append(m)
    res = run_bass_kernel_spmd(nc, in_maps, core_ids=list(range(8)))
    outs = res.results
    y_prompt = np.zeros((16, 256, D), np.float32)
    y_sample = np.zeros((2, 4096, D), np.float32)
    z = lambda *s: np.zeros(s, np.float32)
    nC, nn, nm, nD, nR = z(16, 2, 2, 8, 128, 128), z(16, 2, 2, 8, 128), z(16, 2, 2, 8), z(16, 2, 2, 8, 128, 128), z(16, 2, 2, 8, 128, 128)
    for cidx in range(8):
        y = outs[cidx]["y_tok"]
        y_prompt[2 * cidx] = y[0:256]
        y_prompt[2 * cidx + 1] = y[256:512]
        r = cidx % 4
        y_sample[cidx // 4, 1024 * r:1024 * (r + 1)] = y[512:1536]
        if full:
            for si in range(2):
                nC[2 * cidx + si] = outs[cidx]["o_C"][si]
                nn[2 * cidx + si] = outs[cidx]["o_n"][si]
                nm[2 * cidx + si] = outs[cidx]["o_m"][si].reshape(DEPTH, 2, 8)
                nD[2 * cidx + si] = outs[cidx]["o_D"][si]
                nR[2 * cidx + si] = outs[cidx]["o_R"][si]
    return (y_prompt, y_sample, nC, nn, nm, nD, nR)
```
